# Optimizing a Trainium2 kernel written in Bass

```python
import math
import jax
import jax.numpy as jnp
from jax import lax
import numpy as np

D_MODEL = 1024
BATCH = 8
SEQ = 4096
DEPTH = 4

GRID_W = 64
CTX_LEN = 256
N_MIXERS = 2
HEAD_DIM = 64
RWKV_HEADS = D_MODEL // HEAD_DIM
DECAY_LORA = 64
AAA_LORA = 64
MV_LORA = 32
GATE_LORA = 160
N_MIX_COEF = 6
DIFF_HEADS = D_MODEL // (2 * HEAD_DIM)
D_FF = 4 * D_MODEL
Q_BLOCK = 128
ROPE_THETA = 10000.0
ROPE_PAIRS = HEAD_DIM // 4
NORM_EPS = 1e-6
SUBLN_EPS = 1e-5
GN_EPS = 64e-5
N_RWKV = (DEPTH + 1) // 2
N_DIFF = DEPTH // 2
N_VRES = N_RWKV - 1

kernel_name = 'hybrid_rwkv7_diffattn_dit'


def _rmsnorm(x, g, eps=NORM_EPS):
    xf = x.astype(jnp.float32)
    y = xf * lax.rsqrt(jnp.mean(jnp.square(xf), axis=-1, keepdims=True) + eps)
    return (y * g).astype(x.dtype)


def _modulate(h, shift, scale):
    return h * (1 + scale) + shift


def _sqrelu_mlp(h, w1, w2):
    return jnp.square(jax.nn.relu(h @ w1)) @ w2


def _rope_tables(L):
    rows = L // GRID_W
    row = jnp.repeat(jnp.arange(rows, dtype=jnp.int32), GRID_W)
    col = jnp.tile(jnp.arange(GRID_W, dtype=jnp.int32), rows)
    inv = ROPE_THETA ** (-jnp.arange(ROPE_PAIRS, dtype=jnp.float32) / ROPE_PAIRS)
    ang_r = row.astype(jnp.float32)[:, None] * inv[None, :]
    ang_c = col.astype(jnp.float32)[:, None] * inv[None, :]
    return (jnp.cos(ang_r), jnp.sin(ang_r), jnp.cos(ang_c), jnp.sin(ang_c))


def _rope_half(x, cos, sin):
    x1, x2 = jnp.split(x, 2, axis=-1)
    return jnp.concatenate([x1 * cos - x2 * sin, x1 * sin + x2 * cos], axis=-1)


def _rope2d(x, rope):
    cos_r, sin_r, cos_c, sin_c = rope
    xr, xc = jnp.split(x, 2, axis=-1)
    return jnp.concatenate([_rope_half(xr, cos_r, sin_r), _rope_half(xc, cos_c, sin_c)], axis=-1).astype(x.dtype)


def _token_shift(h):
    z = jnp.zeros_like(h[:, :1])
    prev = jnp.concatenate([z, h[:, :-1]], axis=1)
    nxt = jnp.concatenate([h[:, 1:], z], axis=1)
    return 0.5 * (prev + nxt) - h


def _rwkv_features(h, mix, w_rkv, w0, w1, w2, a0, a1, a2, g1, g2, k_k, k_a, vres):
    B, T, _ = h.shape
    xx = _token_shift(h)
    xr, xw, xk, xv, xa, xg = (h + xx * mix[m] for m in range(N_MIX_COEF))
    r, k, v = jnp.einsum('sbtd,sde->sbte', jnp.stack([xr, xk, xv]), w_rkv)
    decays = tuple(
        jnp.exp(-jnp.exp(-jax.nn.softplus(-(w0[d] + jnp.tanh(xw @ w1[d]) @ w2[d]).astype(jnp.float32)) - 0.5))
        for d in range(2))
    gates = tuple(jax.nn.sigmoid(xg @ g1[d]) @ g2[d] for d in range(2))
    a = jax.nn.sigmoid(a0 + (xa @ a1) @ a2)
    if vres is not None:
        v_first, v0, v1, v2 = vres
        v = v + (v_first - v) * jax.nn.sigmoid(v0 + (xv @ v1) @ v2)
    kk = (k * k_k).reshape(B, T, RWKV_HEADS, HEAD_DIM)
    inv_norm = lax.rsqrt(jnp.maximum(jnp.sum(jnp.square(kk.astype(jnp.float32)), -1, keepdims=True), 1e-24))
    kk = (kk * inv_norm.astype(kk.dtype)).reshape(B, T, D_MODEL)
    k = k * (1 + (a - 1) * k_a)
    return (r, k, v, kk, a, decays, gates)


def _wkv_scan(r, decay, k, v, a_vec, b_vec, s0, reverse):
    B, T, _ = r.shape

    def to_steps(t):
        return jnp.moveaxis(t.astype(jnp.float32).reshape(B, T, RWKV_HEADS, HEAD_DIM), 1, 0)

    def step(S, inp):
        r_t, w_t, k_t, v_t, a_t, b_t = inp
        sa = jnp.einsum('bhvk,bhk->bhv', S, a_t)
        S = S * w_t[:, :, None, :] + sa[..., None] * b_t[:, :, None, :] + v_t[..., None] * k_t[:, :, None, :]
        return S, jnp.einsum('bhvk,bhk->bhv', S, r_t)

    s_fin, o = lax.scan(step, s0, tuple(to_steps(t) for t in (r, decay, k, v, a_vec, b_vec)), reverse=reverse)
    return s_fin, jnp.moveaxis(o, 0, 1).reshape(B, T, D_MODEL)


def _group_norm(o, g, b):
    B, T, _ = o.shape
    oh = o.astype(jnp.float32).reshape(B, T, RWKV_HEADS, HEAD_DIM)
    mu = jnp.mean(oh, axis=-1, keepdims=True)
    var = jnp.mean(jnp.square(oh - mu), axis=-1, keepdims=True)
    on = ((oh - mu) * lax.rsqrt(var + GN_EPS)).reshape(B, T, D_MODEL)
    return on * g + b


def _rwkv_output(f, o_f, o_b, r_k, ln_g, ln_b, w_o):
    r, k, v, _, _, _, gates = f
    B, T, _ = r.shape
    hs = (B, T, RWKV_HEADS, HEAD_DIM)
    bonus = (jnp.sum((r * k).reshape(hs) * r_k, axis=-1, keepdims=True) * v.reshape(hs)).reshape(B, T, D_MODEL)
    y_f = (_group_norm(o_f, ln_g, ln_b).astype(r.dtype) + bonus) * gates[0]
    y_b = (_group_norm(o_b, ln_g, ln_b).astype(r.dtype) + bonus) * gates[1]
    return (y_f + y_b) @ w_o


def _scan_dir(f, d, s_init, reverse):
    r, k, v, kk, a, decays, _ = f
    return _wkv_scan(r, decays[d], k, v, -kk, kk * a, s_init, reverse)


def _rwkv_mixer(h_lat, h_ctx, feat_params, out_params, vres_lat, vres_ctx, need_ctx_out):
    f_ctx = _rwkv_features(h_ctx, *feat_params, vres_ctx)
    f_lat = _rwkv_features(h_lat, *feat_params, vres_lat)
    s0 = jnp.zeros((h_lat.shape[0], RWKV_HEADS, HEAD_DIM, HEAD_DIM), jnp.float32)
    s_cf, o_cf = _scan_dir(f_ctx, 0, s0, False)
    _, o_lf = _scan_dir(f_lat, 0, s_cf, False)
    s_cb, o_cb = _scan_dir(f_ctx, 1, s0, True)
    _, o_lb = _scan_dir(f_lat, 1, s_cb, True)
    y_lat = _rwkv_output(f_lat, o_lf.astype(h_lat.dtype), o_lb.astype(h_lat.dtype), *out_params)
    y_ctx = _rwkv_output(f_ctx, o_cf.astype(h_ctx.dtype), o_cb.astype(h_ctx.dtype), *out_params) if need_ctx_out else None
    return y_lat, y_ctx, f_lat[2], f_ctx[2]


def _diff_attention(h_lat, h_ctx, w_qkv, w_o, lq1, lk1, lq2, lk2, subln_g, lambda_init, rope, need_ctx_out):
    B, L, _ = h_lat.shape

    def project(h):
        T = h.shape[1]
        q, k, v = jnp.split(h @ w_qkv, 3, axis=-1)
        q = q.reshape(B, T, DIFF_HEADS, 2, HEAD_DIM).transpose(0, 2, 3, 1, 4)
        k = k.reshape(B, T, DIFF_HEADS, 2, HEAD_DIM).transpose(0, 2, 3, 1, 4)
        v = v.reshape(B, T, DIFF_HEADS, 2 * HEAD_DIM).transpose(0, 2, 1, 3)
        return q, k, v

    q_l, k_l, v_l = project(h_lat)
    _, k_c, v_c = project(h_ctx) if not need_ctx_out else (None, None, None)
    if need_ctx_out:
        q_c, k_c, v_c = project(h_ctx)
    q_l = _rope2d(q_l, rope)
    k_l = _rope2d(k_l, rope)
    lam = (jnp.exp(jnp.sum(lq1 * lk1).astype(jnp.float32)) - jnp.exp(jnp.sum(lq2 * lk2).astype(jnp.float32))
           + lambda_init)
    scale = HEAD_DIM ** -0.5

    def combine(s):
        p = jax.nn.softmax(s.astype(jnp.float32) * scale, axis=-1)
        return p[:, :, 0] - lam * p[:, :, 1]

    def head_out(o):
        T = o.shape[1]
        o = _rmsnorm(o, subln_g, SUBLN_EPS) * (1 - lambda_init)
        return o.reshape(B, T, D_MODEL) @ w_o

    k_all = jnp.concatenate([k_l, k_c], axis=3)
    v_all = jnp.concatenate([v_l, v_c], axis=2)
    nb = L // Q_BLOCK
    q_blocks = q_l.reshape(B, DIFF_HEADS, 2, nb, Q_BLOCK, HEAD_DIM).transpose(3, 0, 1, 2, 4, 5)

    def block(qb):
        a = combine(jnp.einsum('bhsqd,bhskd->bhsqk', qb, k_all))
        return jnp.einsum('bhqk,bhkv->bhqv', a.astype(v_all.dtype), v_all)

    o_l = lax.map(block, q_blocks)
    o_l = o_l.transpose(1, 0, 3, 2, 4).reshape(B, L, DIFF_HEADS, 2 * HEAD_DIM)
    y_lat = head_out(o_l)
    y_ctx = None
    if need_ctx_out:
        a_c = combine(jnp.einsum('bhsqd,bhskd->bhsqk', q_c, k_c))
        o_c = jnp.einsum('bhqk,bhkv->bhqv', a_c.astype(v_c.dtype), v_c).transpose(0, 2, 1, 3)
        y_ctx = head_out(o_c)
    return y_lat, y_ctx


def setup_inputs(seed: int = 0) -> dict:
    key = jax.random.key(seed)
    ks = iter(jax.random.split(key, 48))
    D = D_MODEL

    def nrm(shape, s):
        return jax.random.normal(next(ks), shape, jnp.float32) * s

    def uni(shape, lo, hi):
        return jax.random.uniform(next(ks), shape, jnp.float32, lo, hi)

    return {
        'x': nrm((BATCH, SEQ, D), 1.0),
        'c': nrm((BATCH, D), 1.0),
        'ctx': nrm((BATCH, CTX_LEN, D), 1.0),
        'c_ctx': nrm((D,), 1.0),
        'ada_w': nrm((DEPTH, D, 6 * D), 0.5 * D ** -0.5),
        'ada_b': nrm((DEPTH, 6 * D), 0.01),
        'norm_g': 1.0 + nrm((DEPTH, 2, D), 0.02),
        'final_g': 1.0 + nrm((D,), 0.02),
        'rw_mix': uni((N_RWKV, N_MIX_COEF, D), 0.0, 1.0),
        'rw_w_rkv': nrm((N_RWKV, 3, D, D), D ** -0.5),
        'rw_w0': uni((N_RWKV, 2, D), -6.0, -1.0),
        'rw_w1': nrm((N_RWKV, 2, D, DECAY_LORA), D ** -0.5),
        'rw_w2': nrm((N_RWKV, 2, DECAY_LORA, D), 0.1 * DECAY_LORA ** -0.5),
        'rw_a0': nrm((N_RWKV, D), 0.1),
        'rw_a1': nrm((N_RWKV, D, AAA_LORA), D ** -0.5),
        'rw_a2': nrm((N_RWKV, AAA_LORA, D), 0.1 * AAA_LORA ** -0.5),
        'rw_g1': nrm((N_RWKV, 2, D, GATE_LORA), D ** -0.5),
        'rw_g2': nrm((N_RWKV, 2, GATE_LORA, D), GATE_LORA ** -0.5),
        'rw_kk': 0.85 + nrm((N_RWKV, D), 0.02),
        'rw_ka': 1.0 + nrm((N_RWKV, D), 0.02),
        'rw_rk': nrm((N_RWKV, RWKV_HEADS, HEAD_DIM), 0.1),
        'rw_ln_g': 1.0 + nrm((N_RWKV, D), 0.02),
        'rw_ln_b': nrm((N_RWKV, D), 0.01),
        'rw_w_o': nrm((N_RWKV, D, D), D ** -0.5),
        'rw_v0': 1.0 + nrm((N_VRES, D), 0.02),
        'rw_v1': nrm((N_VRES, D, MV_LORA), D ** -0.5),
        'rw_v2': nrm((N_VRES, MV_LORA, D), 0.1 * MV_LORA ** -0.5),
        'da_w_qkv': nrm((N_DIFF, D, 3 * D), D ** -0.5),
        'da_w_o': nrm((N_DIFF, D, D), D ** -0.5),
        'da_lq1': nrm((N_DIFF, HEAD_DIM), 0.1),
        'da_lk1': nrm((N_DIFF, HEAD_DIM), 0.1),
        'da_lq2': nrm((N_DIFF, HEAD_DIM), 0.1),
        'da_lk2': nrm((N_DIFF, HEAD_DIM), 0.1),
        'da_subln_g': 1.0 + nrm((N_DIFF, 2 * HEAD_DIM), 0.02),
        'mlp_w1': nrm((DEPTH, D, D_FF), D ** -0.5),
        'mlp_w2': nrm((DEPTH, D_FF, D), D_FF ** -0.5),
    }


def reference(x, c, ctx, c_ctx, ada_w, ada_b, norm_g, final_g,
              rw_mix, rw_w_rkv, rw_w0, rw_w1, rw_w2, rw_a0, rw_a1, rw_a2,
              rw_g1, rw_g2, rw_kk, rw_ka, rw_rk, rw_ln_g, rw_ln_b, rw_w_o,
              rw_v0, rw_v1, rw_v2,
              da_w_qkv, da_w_o, da_lq1, da_lk1, da_lq2, da_lk2, da_subln_g,
              mlp_w1, mlp_w2):
    L = x.shape[1]
    rope = _rope_tables(L)
    xc = ctx
    v_first_lat = None
    v_first_ctx = None
    for i in range(DEPTH):
        last = i == DEPTH - 1
        mod_lat = (jax.nn.silu(c) @ ada_w[i] + ada_b[i])[:, None, :]
        mod_ctx = jax.nn.silu(c_ctx) @ ada_w[i] + ada_b[i]
        sh1, sc1, g1, sh2, sc2, g2 = jnp.split(mod_lat, 6, axis=-1)
        csh1, csc1, cg1, csh2, csc2, cg2 = jnp.split(mod_ctx, 6, axis=-1)
        h_lat = _modulate(_rmsnorm(x, norm_g[i, 0]), sh1, sc1)
        h_ctx = _modulate(_rmsnorm(xc, norm_g[i, 0]), csh1, csc1)
        j = i // N_MIXERS
        if i % N_MIXERS == 0:
            vres_lat = None
            vres_ctx = None
            if j > 0:
                vres_lat = (v_first_lat, rw_v0[j - 1], rw_v1[j - 1], rw_v2[j - 1])
                vres_ctx = (v_first_ctx, rw_v0[j - 1], rw_v1[j - 1], rw_v2[j - 1])
            feat_params = (rw_mix[j], rw_w_rkv[j], rw_w0[j], rw_w1[j], rw_w2[j], rw_a0[j], rw_a1[j], rw_a2[j],
                           rw_g1[j], rw_g2[j], rw_kk[j], rw_ka[j])
            out_params = (rw_rk[j], rw_ln_g[j], rw_ln_b[j], rw_w_o[j])
            y_lat, y_ctx, v_lat, v_ctx = _rwkv_mixer(h_lat, h_ctx, feat_params, out_params,
                                                     vres_lat, vres_ctx, not last)
            if j == 0:
                v_first_lat = v_lat
                v_first_ctx = v_ctx
        else:
            lambda_init = 0.8 - 0.6 * math.exp(-0.3 * i)
            y_lat, y_ctx = _diff_attention(h_lat, h_ctx, da_w_qkv[j], da_w_o[j], da_lq1[j], da_lk1[j],
                                           da_lq2[j], da_lk2[j], da_subln_g[j], lambda_init, rope, not last)
        x = x + g1 * y_lat
        h = _modulate(_rmsnorm(x, norm_g[i, 1]), sh2, sc2)
        x = x + g2 * _sqrelu_mlp(h, mlp_w1[i], mlp_w2[i])
        if not last:
            xc = xc + cg1 * y_ctx
            hc = _modulate(_rmsnorm(xc, norm_g[i, 1]), csh2, csc2)
            xc = xc + cg2 * _sqrelu_mlp(hc, mlp_w1[i], mlp_w2[i])
    return _rmsnorm(x, final_g)
```

```python
import math
from contextlib import ExitStack
import numpy as np
import concourse.bass as bass
import concourse.mybir as mybir
from concourse.bass_utils import run_bass_kernel_spmd

F32 = mybir.dt.float32
BF16 = mybir.dt.bfloat16
AF = mybir.ActivationFunctionType
ALU = mybir.AluOpType

D = 1024
FC = 8
TC = 256
TL = 4096
T = TC + TL
DFF = 4096
DEPTH = 4
NH = 16
CH = 64
NCHUNK = T // CH
DEC = math.exp(-0.5)


class Tok:
    __slots__ = ("sem", "val")

    def __init__(self, sem, val):
        self.sem = sem
        self.val = val


class Buf:
    def __init__(self, t, name):
        self.t = t
        self.name = name
        self.w = None
        self.r = {}
        self.dsem = None
        self.psum = False

    def __getitem__(self, k):
        return self.t[k]


class Eng:
    def __init__(self, h, sem):
        self.h = h
        self.sem = sem
        self.cnt = 0
        self.waited = {}
        self.pending = []


class K:
    def __init__(self, nc):
        self.nc = nc
        self.es = ExitStack()
        self.E = {}
        for nm, h in (("pe", nc.tensor), ("act", nc.scalar), ("dve", nc.vector), ("pool", nc.gpsimd), ("sp", nc.sync)):
            self.E[nm] = Eng(h, self.es.enter_context(nc.semaphore("e_" + nm)))
        self.dsems = [[self.es.enter_context(nc.semaphore("d%d" % i)), 0] for i in range(90)]
        self.dfree = list(range(len(self.dsems)))
        self.phase_es = None
        self.phase_bufs = []
        self.rr = 0
        self.nbuf = 0
        self.plog = []

    def begin(self):
        self.phase_es = ExitStack()
        self.phase_bufs = []

    def sb(self, shape, dt=F32, name=None):
        self.nbuf += 1
        name = "%s_%d" % (name or "sb", self.nbuf)
        b = Buf(self.phase_es.enter_context(self.nc.sbuf_tensor(name, list(shape), dt)), name)
        self.phase_bufs.append(b)
        return b

    def ps(self, shape, dt=F32, name=None):
        self.nbuf += 1
        name = "%s_%d" % (name or "ps", self.nbuf)
        b = Buf(self.phase_es.enter_context(self.nc.psum_tensor(name, list(shape), dt)), name)
        b.psum = True
        self.phase_bufs.append(b)
        return b

    def gsb(self, shape, dt=F32, name=None):
        self.nbuf += 1
        name = "%s_%d" % (name or "g", self.nbuf)
        return Buf(self.es.enter_context(self.nc.sbuf_tensor(name, list(shape), dt)), name)

    def barrier(self):
        toks = [Tok(e.sem, e.cnt) for e in self.E.values() if e.cnt > 0]
        toks += [Tok(s, c) for s, c in self.dsems if c > 0]
        for e in self.E.values():
            assert not e.pending
            for t in toks:
                self._wait(e, t)

    def end(self, name=""):
        self.plog.append((name, {n: e.cnt for n, e in self.E.items()}))
        self.barrier()
        for b in self.phase_bufs:
            if b.dsem is not None:
                self.dfree.append(b.dsem)
        self.phase_es.close()
        self.phase_es = None
        self.phase_bufs = []

    def _wait(self, e, tok, raw=False):
        if tok is None:
            return
        if tok.sem is e.sem and (not raw or e is self.E["pe"]):
            return
        k = id(tok.sem)
        if e.waited.get(k, 0) >= tok.val:
            return
        e.h.wait_ge(tok.sem, tok.val)
        e.waited[k] = tok.val

    def _deps(self, e, reads, writes):
        for r in reads:
            self._wait(e, r.w, raw=True)
            if r.psum:
                for t in r.r.values():
                    self._wait(e, t)
        for w in writes:
            self._wait(e, w.w)
            for t in w.r.values():
                self._wait(e, t)

    @staticmethod
    def _commit(tok, reads, writes):
        for r in reads:
            r.r[id(tok.sem)] = tok
        for w in writes:
            w.w = tok
            w.r = {}

    def op(self, eng, fn, reads, writes, inc=True):
        e = self.E[eng]
        self._deps(e, reads, writes)
        ins = fn(e.h)
        if not inc:
            e.pending.append((reads, writes))
            return
        e.cnt += 1
        ins.then_inc(e.sem, 1)
        tok = Tok(e.sem, e.cnt)
        for (r, w) in e.pending:
            self._commit(tok, r, w)
        e.pending = []
        self._commit(tok, reads, writes)

    def dma(self, q, out, in_, reads, writes, owner):
        e = self.E[q]
        self._deps(e, reads, writes)
        if owner.dsem is None:
            owner.dsem = self.dfree.pop()
        ent = self.dsems[owner.dsem]
        ins = e.h.dma_start(out=out, in_=in_)
        ent[1] += 16
        ins.then_inc(ent[0], 16)
        tok = Tok(ent[0], ent[1])
        self._commit(tok, reads, writes)

    def load(self, buf, dst_ap, src_ap, q="sp"):
        self.dma(q, dst_ap, src_ap, [], [buf], buf)

    def store(self, dst_ap, buf, src_ap, q=None):
        import os
        self.dma(q or os.environ.get("STQ", "sp"), dst_ap, src_ap, [buf], [], buf)

    def mm(self, out, lhsT, rhs, reads, writes, start=True, stop=True, inc=True):
        self.op("pe", lambda h: h.matmul(out, lhsT, rhs, start=start, stop=stop), reads, writes, inc=inc)

    def tr(self, out, in_, ident, reads, writes, inc=True):
        self.op("pe", lambda h: h.transpose(out, in_, ident), reads, writes, inc=inc)

    def act(self, out, in_, func, reads, writes, bias=0.0, scale=1.0, accum=None):
        if accum is None:
            self.op("act", lambda h: h.activation(out=out, in_=in_, func=func, bias=bias, scale=scale), reads, writes)
        else:
            self.op("act", lambda h: h.activation(out=out, in_=in_, func=func, bias=bias, scale=scale, accum_out=accum), reads, writes)

    def veng(self):
        self.rr += 1
        return "pool" if self.rr % 3 == 0 else "dve"

    def tt(self, out, in0, in1, op, reads, writes, eng="dve"):
        self.op(eng, lambda h: h.tensor_tensor(out=out, in0=in0, in1=in1, op=op), reads, writes)

    def ts(self, out, in0, s1, s2, op0, op1, reads, writes, eng="dve"):
        if s2 is None:
            self.op(eng, lambda h: h.tensor_scalar(out=out, in0=in0, scalar1=s1, scalar2=None, op0=op0), reads, writes)
        else:
            self.op(eng, lambda h: h.tensor_scalar(out=out, in0=in0, scalar1=s1, scalar2=s2, op0=op0, op1=op1), reads, writes)

    def stt(self, out, in0, scalar, in1, op0, op1, reads, writes, eng="dve"):
        self.op(eng, lambda h: h.scalar_tensor_tensor(out=out, in0=in0, scalar=scalar, in1=in1, op0=op0, op1=op1), reads, writes)

    def copy(self, out, in_, reads, writes, eng="dve"):
        self.op(eng, lambda h: h.tensor_copy(out, in_), reads, writes)

    def memset(self, out, val, writes, eng="dve"):
        self.op(eng, lambda h: h.memset(out, val), [], writes)


def fm(ap, c0, n):
    return ap[:, c0:c0 + n].rearrange("(c p) t -> p c t", p=128)


def bc(ap, shape):
    return ap.to_broadcast(list(shape))


class VecLayout:
    def __init__(self):
        self.cols = {}
        self.arrs = []
        self.n = 0

    def add(self, name, v):
        v = np.asarray(v, np.float32).reshape(-1)
        assert v.size % 128 == 0
        c = v.size // 128
        self.arrs.append(v.reshape(c, 128).T)
        self.cols[name] = (self.n, c)
        self.n += c

    def build(self):
        return np.ascontiguousarray(np.concatenate(self.arrs, axis=1))


def vec_names():
    names = []
    for i in range(DEPTH):
        names += [("ng%d_0" % i, 8), ("ng%d_1" % i, 8), ("adab%d" % i, 48)]
    names += [("fing", 8)]
    for j in range(2):
        names += [("mix%d_%d" % (j, m), 8) for m in range(6)]
        names += [("w0_%d_%d" % (j, d), 8) for d in range(2)]
        names += [("a0_%d" % j, 8), ("kk_%d" % j, 8), ("ka_%d" % j, 8), ("rk_%d" % j, 8), ("lng_%d" % j, 8), ("lnb_%d" % j, 8)]
    names += [("v0", 8)]
    off = {}
    n = 0
    for nm, c in names:
        off[nm] = (n, c)
        n += c
    return off, n


VOFF, NV = vec_names()


def build_vecs(inp):
    vl = VecLayout()
    for i in range(DEPTH):
        vl.add("ng%d_0" % i, inp["norm_g"][i, 0])
        vl.add("ng%d_1" % i, inp["norm_g"][i, 1])
        vl.add("adab%d" % i, inp["ada_b"][i])
    vl.add("fing", inp["final_g"])
    for j in range(2):
        for m in range(6):
            vl.add("mix%d_%d" % (j, m), inp["rw_mix"][j, m])
        for d in range(2):
            vl.add("w0_%d_%d" % (j, d), inp["rw_w0"][j, d])
        vl.add("a0_%d" % j, inp["rw_a0"][j])
        vl.add("kk_%d" % j, inp["rw_kk"][j])
        vl.add("ka_%d" % j, inp["rw_ka"][j])
        vl.add("rk_%d" % j, inp["rw_rk"][j])
        vl.add("lng_%d" % j, inp["rw_ln_g"][j])
        vl.add("lnb_%d" % j, inp["rw_ln_b"][j])
    vl.add("v0", inp["rw_v0"][0])
    assert vl.cols == VOFF, (vl.cols, VOFF)
    return vl.build()


def const_layout():
    off = {}
    n = 0
    for nm, c in (("ident", 128), ("ones", 128), ("blk", 128), ("rmask", 512), ("mf_s", 64), ("mf_i", 64), ("mb_s", 64), ("mb_i", 64),
                  ("perm", 128), ("cos", TL), ("sin", TL)):
        off[nm] = (n, c)
        n += c
    return off, n


COFF, NCON = const_layout()


def build_consts():
    c = np.zeros((128, NCON), np.float32)

    def put(nm, a):
        o, w = COFF[nm]
        c[:a.shape[0], o:o + w] = a

    put("ident", np.eye(128, dtype=np.float32))
    put("ones", np.ones((128, 128), np.float32))
    blk = np.zeros((128, 128), np.float32)
    blk[:64, :64] = 1
    blk[64:, 64:] = 1
    put("blk", blk)
    rm = np.ones((128, 512), np.float32)
    rm[:, ::CH] = 0
    put("rmask", rm)
    s = np.arange(64)[:, None]
    t = np.arange(64)[None, :]
    put("mf_s", (s < t).astype(np.float32))
    put("mf_i", (s <= t).astype(np.float32))
    put("mb_s", (s > t).astype(np.float32))
    put("mb_i", (s >= t).astype(np.float32))
    perm = np.zeros((128, 128), np.float32)
    for m in range(128):
        j = m % 64
        q = j % 32
        partner = m + 16 if q < 16 else m - 16
        perm[partner, m] = 1.0
    put("perm", perm)
    inv = (10000.0 ** (-np.arange(16, dtype=np.float32) / 16)).astype(np.float32)
    pos = np.arange(TL)
    row = (pos // 64).astype(np.float32)
    col = (pos % 64).astype(np.float32)
    ang_r = (row[None, :] * inv[:, None]).astype(np.float32)
    ang_c = (col[None, :] * inv[:, None]).astype(np.float32)
    cos = np.zeros((128, TL), np.float32)
    sin = np.zeros((128, TL), np.float32)
    for m in range(128):
        j = m % 64
        a = ang_r if j < 32 else ang_c
        q = j % 32
        f = q % 16
        cos[m] = np.cos(a[f])
        sin[m] = -np.sin(a[f]) if q < 16 else np.sin(a[f])
    put("cos", cos)
    put("sin", sin)
    return c


WNAMES = ["ada_w", "rw_w_rkv", "rw_w1", "rw_w2", "rw_a1", "rw_a2", "rw_g1", "rw_g2", "rw_w_o", "rw_v1", "rw_v2",
          "da_w_qkv", "da_w_o", "mlp_w1", "mlp_w2"]
WSHAPES = {"ada_w": [4, 1024, 6144], "rw_w_rkv": [2, 3, 1024, 1024], "rw_w1": [2, 2, 1024, 64], "rw_w2": [2, 2, 64, 1024],
           "rw_a1": [2, 1024, 64], "rw_a2": [2, 64, 1024], "rw_g1": [2, 2, 1024, 160], "rw_g2": [2, 2, 160, 1024],
           "rw_w_o": [2, 1024, 1024], "rw_v1": [1, 1024, 32], "rw_v2": [1, 32, 1024], "da_w_qkv": [2, 1024, 3072],
           "da_w_o": [2, 1024, 1024], "mlp_w1": [4, 1024, 4096], "mlp_w2": [4, 4096, 1024]}


class Prog:
    def __init__(self, upto="all", dumps=()):
        self.upto = upto
        self.dumps = set(dumps)
        nc = bass.Bass("TRN2", target_bir_lowering=False)
        self.nc = nc
        self.k = K(nc)

        def din(name, shape):
            return nc.dram_tensor(name, list(shape), F32, kind="ExternalInput").ap()

        self.x = din("x", [TL, D])
        self.ctx = din("ctx", [TC, D])
        self.cc = din("cc", [128, 16])
        self.vecs_d = din("vecs", [128, NV])
        self.consts_d = din("consts", [128, NCON])
        self.lam_d = din("lamv", [2, 4, 64])
        self.subg_d = din("subg", [2, 128])
        self.W = {n: din(n, WSHAPES[n]) for n in WNAMES}
        self.out = nc.dram_tensor("out", [TL, D], F32, kind="ExternalOutput").ap()
        self.scr = {}

    def scratch(self, name, shape, dt=F32):
        if name not in self.scr:
            kind = "ExternalOutput" if name in self.dumps else "Internal"
            self.scr[name] = self.nc.dram_tensor(name, list(shape), dt, kind=kind).ap()
        return self.scr[name]

    def wload(self, dst, dst_ap, src_ap, shape, stage):
        k = self.k
        st = stage[self._stg % len(stage)]
        self._stg += 1
        sl = tuple(slice(0, s) for s in shape)
        k.load(st, st.t[sl], src_ap)
        eng = ("dve", "pool", "act")[self._stg % 3]
        if eng == "act":
            k.act(dst_ap, st.t[sl], AF.Copy, [st], [dst])
        else:
            k.copy(dst_ap, st.t[sl], [st], [dst], eng=eng)

    def wload_mat(self, dst, src, kc, ncols, stage, col0=0, dcol0=0, piece=1024):
        for c in range(kc):
            for n0 in range(0, ncols, piece):
                n = min(piece, ncols - n0)
                self.wload(dst, dst.t[:, c, dcol0 + n0:dcol0 + n0 + n], src[c * 128:(c + 1) * 128, col0 + n0:col0 + n0 + n], (128, n), stage)

    def build(self):
        k = self.k
        nc = self.nc
        self._stg = 0
        self.vecs = k.gsb([128, NV], F32, "vecs")
        self.con = k.gsb([128, NCON - 2 * TL], F32, "con")
        self.mod = k.gsb([128, DEPTH, 2, 48], F32, "mod")
        self.identb = k.gsb([128, 128], BF16, "identb")
        self.onesb = k.gsb([128, 128], BF16, "onesb")
        self.blkb = k.gsb([128, 128], BF16, "blkb")
        k.load(self.vecs, self.vecs.t[:, :], self.vecs_d[:, :])
        k.load(self.con, self.con.t[:, :], self.consts_d[:, 0:NCON - 2 * TL])
        k.copy(self.identb.t[:, :], self.cs("ident"), [self.con], [self.identb])
        k.copy(self.onesb.t[:, :], self.cs("ones"), [self.con], [self.onesb])
        k.copy(self.blkb.t[:, :], self.cs("blk"), [self.con], [self.blkb])
        self.XT = self.scratch("XT", [D, T])
        self.YT = self.scratch("YT", [D, T], BF16)

        self.phase_mod()
        if self.upto == "mod":
            return self.finish()
        self.phase_x0()
        if self.upto == "x0":
            return self.finish()
        for i in range(DEPTH):
            if i % 2 == 0:
                self.phase_r1(i)
                if self.upto == "r1_%d" % i:
                    return self.finish()
                self.phase_r2(i)
                if self.upto == "r2_%d" % i:
                    return self.finish()
                self.phase_r3(i)
                if self.upto == "r3_%d" % i:
                    return self.finish()
            else:
                self.phase_a1(i)
                if self.upto == "a1_%d" % i:
                    return self.finish()
                self.phase_a2(i)
                if self.upto == "a2_%d" % i:
                    return self.finish()
            self.phase_p3(i)
            if self.upto == "p3_%d" % i:
                return self.finish()
        self.phase_final()
        return self.finish()

    def finish(self):
        self.k.barrier()
        self.k.es.close()
        return self.nc

    def cs(self, name, rows=128):
        o, w = COFF[name]
        return self.con.t[0:rows, o:o + w]

    def vc(self, name):
        o, w = VOFF[name]
        return self.vecs.t[:, o:o + w]

    def blocks(self, n=256, ctx=True):
        bl = []
        if ctx:
            for t0 in range(0, TC, n):
                bl.append((t0, min(n, TC - t0), 0, TC))
        for t0 in range(TC, T, n):
            bl.append((t0, n, TC, T))
        return bl

    def phase_mod(self):
        k = self.k
        k.begin()
        cc = k.sb([128, 16], F32, "cc")
        sc = k.sb([128, 8, 2], F32, "sc")
        k.load(cc, cc.t[:, :], self.cc[:, :])
        sg = k.sb([128, 16], F32, "sg")
        k.act(sg.t[:, :], cc.t[:, :], AF.Sigmoid, [cc], [sg])
        k.tt(sg.t[:, :], sg.t[:, :], cc.t[:, :], ALU.mult, [cc, sg], [sg])
        k.copy(sc.t[:, :, 0], sg.t[:, 0:8], [sg], [sc])
        k.copy(sc.t[:, :, 1], sg.t[:, 8:16], [sg], [sc])
        wst = [k.sb([128, 8, 768], F32, "adaw") for _ in range(2)]
        pm = k.ps([128, 48, 2], F32, "pm")
        for i in range(DEPTH):
            for g in range(8):
                w = wst[(i * 8 + g) % 2]
                k.load(w, w.t[:, :, :], self.W["ada_w"][i, :, g * 768:(g + 1) * 768].rearrange("(c p) n -> p c n", p=128))
                for jj in range(6):
                    j = g * 6 + jj
                    for c in range(8):
                        k.mm(pm.t[:, j, :], w.t[:, c, jj * 128:(jj + 1) * 128], sc.t[:, c, :], [w, sc], [pm],
                             start=(c == 0), stop=(c == 7), inc=(c == 7))
            o, wd = VOFF["adab%d" % i]
            for wch in range(2):
                k.tt(self.mod.t[:, i, wch, :], pm.t[:, :, wch], self.vecs.t[:, o:o + wd], ALU.add, [pm, self.vecs], [self.mod])
        if "MODD" in self.dumps:
            k.store(self.scratch("MODD", [128, DEPTH * 2 * 48]), self.mod, self.mod.t[:, :, :, :].rearrange("p a b c -> p (a b c)"))
        k.end()

    def phase_x0(self):
        k = self.k
        k.begin()
        xin = [k.sb([128, D], F32, "xin") for _ in range(2)]
        xo = [k.sb([128, 8, 128], F32, "xo") for _ in range(2)]
        pt = [k.ps([128, 4, 128], F32, "pt") for _ in range(2)]
        ident = self.cs("ident")
        for it in range(T // 128):
            t0 = it * 128
            xi = xin[it % 2]
            src = self.ctx[t0:t0 + 128, :] if t0 < TC else self.x[t0 - TC:t0 - TC + 128, :]
            k.load(xi, xi.t[:, :], src)
            o = xo[it % 2]
            for half in range(2):
                p = pt[half]
                for c4 in range(4):
                    c = half * 4 + c4
                    k.tr(p.t[:, c4, :], xi.t[:, c * 128:(c + 1) * 128], ident, [xi, self.con], [p], inc=(c4 == 3))
                if half == 0:
                    k.copy(o.t[:, 0:4, :], p.t[:, :, :], [p], [o])
                else:
                    k.act(o.t[:, 4:8, :], p.t[:, :, :], AF.Copy, [p], [o])
            k.store(fm(self.XT, t0, 128), o, o.t[:, :, :])
        k.end()

    def norm_mod(self, xt, n, h_out, h_reads, gname, i, wch, which, tmp_sq, ps_ss, rstd, out_dt_buf=None):
        k = self.k
        k.act(tmp_sq.t[:, :, 0:n], xt.t[:, :, 0:n], AF.Square, [xt], [tmp_sq])
        for c in range(8):
            k.mm(ps_ss.t[:, 0:n], self.onesb.t[:, :], tmp_sq.t[:, c, 0:n], [self.onesb, tmp_sq], [ps_ss],
                 start=(c == 0), stop=(c == 7), inc=(c == 7))
        k.act(rstd.t[:, 0:n], ps_ss.t[:, 0:n], AF.Ln, [ps_ss], [rstd], bias=1e-6, scale=1.0 / D)
        k.act(rstd.t[:, 0:n], rstd.t[:, 0:n], AF.Exp, [rstd], [rstd], scale=-0.5)

    def gs_cols(self, i, wch, which, gname, gs, sh):
        k = self.k
        base = 0 if which == 0 else 24
        shv = self.mod.t[:, i, wch, base:base + 8]
        scv = self.mod.t[:, i, wch, base + 8:base + 16]
        k.stt(gs.t[:, wch, :], scv, 1.0, self.vc(gname), ALU.add, ALU.mult, [self.mod, self.vecs], [gs])
        k.copy(sh.t[:, wch, :], shv, [self.mod], [sh])

    def phase_p3(self, i):
        k = self.k
        last = i == DEPTH - 1
        j = i // 2
        n = 256
        k.begin()
        stage = [k.sb([128, 512], F32, "stg") for _ in range(2)]
        wo = k.sb([128, 8, D], BF16, "wo")
        w1g = [k.sb([128, 8, DFF // 4], BF16, "w1") for _ in range(4)]
        w2g = [k.sb([128, 8, D], BF16, "w2") for _ in range(4)]
        wo_src = self.W["rw_w_o"][j] if i % 2 == 0 else self.W["da_w_o"][j]
        self.wload_mat(wo, wo_src, 8, D, stage, piece=512)
        for g_ in range(4):
            self.wload_mat(w1g[g_], self.W["mlp_w1"][i], 8, DFF // 4, stage, col0=g_ * (DFF // 4), piece=512)
        for g_ in range(4):
            self.wload_mat(w2g[g_], self.W["mlp_w2"][i][g_ * 1024:(g_ + 1) * 1024, :], 8, D, stage, piece=512)
        gs = k.sb([128, 2, 8], F32, "gs")
        sh = k.sb([128, 2, 8], F32, "sh")
        for wch in range(2):
            self.gs_cols(i, wch, 1, "ng%d_1" % i, gs, sh)
        xts = [k.sb([128, 8, n], F32, "xt") for _ in range(1)]
        ys = [k.sb([128, 8, n], BF16, "y") for _ in range(1)]
        xn = k.sb([128, 8, n], F32, "xn")
        hb = k.sb([128, 8, n], BF16, "hb")
        hid = k.sb([128, 32, n], BF16, "hid")
        rl = [k.sb([128, n], F32, "rl") for _ in range(2)]
        rstd = k.sb([128, n], F32, "rstd")
        pss = [k.ps([128, 512], F32, "pp") for _ in range(6)]
        ps_ss = k.ps([128, 512], F32, "pss")
        pi = 0
        for bi, (t0, nb, s0, s1) in enumerate(self.blocks(n, ctx=not last)):
            wch = 1 if t0 < TC else 0
            xt = xts[0]
            y = ys[0]
            k.load(xt, xt.t[:, :, :], fm(self.XT, t0, n))
            k.load(y, y.t[:, :, :], fm(self.YT, t0, n))
            g1 = self.mod.t[:, i, wch, 16:24]
            g2 = self.mod.t[:, i, wch, 40:48]
            for oc in range(8):
                p = pss[pi % 6]
                pi += 1
                for c in range(8):
                    k.mm(p.t[:, 0:n], wo.t[:, c, oc * 128:(oc + 1) * 128], y.t[:, c, :], [wo, y], [p],
                         start=(c == 0), stop=(c == 7), inc=(c == 7))
                k.stt(xt.t[:, oc, :], p.t[:, 0:n], g1[:, oc:oc + 1], xt.t[:, oc, :], ALU.mult, ALU.add, [p, self.mod, xt], [xt])
            self.norm_mod(xt, n, None, None, None, i, wch, 1, hb, ps_ss, rstd)
            k.tt(xn.t[:, :, :], xt.t[:, :, :], bc(rstd.t[:, 0:n].unsqueeze(1), [128, 8, n]), ALU.mult, [xt, rstd], [xn], eng="pool")
            for c in range(8):
                k.act(hb.t[:, c, :], xn.t[:, c, :], AF.Identity, [xn, gs, sh], [hb], bias=sh.t[:, wch, c:c + 1], scale=gs.t[:, wch, c:c + 1])
            for hc in range(32):
                p = pss[pi % 6]
                pi += 1
                for c in range(8):
                    w1 = w1g[hc // 8]
                    k.mm(p.t[:, 0:n], w1.t[:, c, (hc % 8) * 128:(hc % 8 + 1) * 128], hb.t[:, c, :], [w1, hb], [p],
                         start=(c == 0), stop=(c == 7), inc=(c == 7))
                r = rl[hc % 2]
                k.act(r.t[:, :], p.t[:, 0:n], AF.Relu, [p], [r])
                k.tt(hid.t[:, hc, :], r.t[:, :], r.t[:, :], ALU.mult, [r], [hid], eng=("pool" if hc % 2 else "dve"))
            for oc in range(8):
                p = pss[pi % 6]
                pi += 1
                for hc in range(32):
                    w2 = w2g[hc // 8]
                    k.mm(p.t[:, 0:n], w2.t[:, hc % 8, oc * 128:(oc + 1) * 128], hid.t[:, hc, :], [w2, hid], [p],
                         start=(hc == 0), stop=(hc == 31), inc=(hc == 31))
                k.stt(xt.t[:, oc, :], p.t[:, 0:n], g2[:, oc:oc + 1], xt.t[:, oc, :], ALU.mult, ALU.add, [p, self.mod, xt], [xt])
            k.store(fm(self.XT, t0, n), xt, xt.t[:, :, :])
        k.end()

    def phase_final(self):
        k = self.k
        n = 128
        k.begin()
        xts = [k.sb([128, 8, n], F32, "xt") for _ in range(2)]
        sq = k.sb([128, 8, n], BF16, "sq")
        xn = k.sb([128, 8, n], F32, "xn")
        rstd = k.sb([128, n], F32, "rstd")
        ps_ss = k.ps([128, 512], F32, "pss")
        pt = [k.ps([128, 4, 128], F32, "pt") for _ in range(2)]
        outs = [k.sb([128, D], F32, "ot") for _ in range(2)]
        ident = self.cs("ident")
        fg = self.vc("fing")
        for bi in range(TL // n):
            t0 = TC + bi * n
            xt = xts[bi % 2]
            k.load(xt, xt.t[:, :, :], fm(self.XT, t0, n))
            self.norm_mod(xt, n, None, None, None, 0, 0, 0, sq, ps_ss, rstd)
            k.tt(xn.t[:, :, :], xt.t[:, :, :], bc(rstd.t[:, 0:n].unsqueeze(1), [128, 8, n]), ALU.mult, [xt, rstd], [xn])
            for c in range(8):
                k.act(xn.t[:, c, :], xn.t[:, c, :], AF.Copy, [xn, self.vecs], [xn], scale=fg[:, c:c + 1])
            o = outs[bi % 2]
            for half in range(2):
                p = pt[half]
                for c4 in range(4):
                    c = half * 4 + c4
                    k.tr(p.t[:, c4, :], xn.t[:, c, :], ident, [xn, self.con], [p], inc=(c4 == 3))
                if half == 0:
                    k.copy(o.t[:, 0:512], p.t[:, :, :].rearrange("p a b -> p (a b)"), [p], [o])
                else:
                    k.act(o.t[:, 512:1024], p.t[:, :, :].rearrange("p a b -> p (a b)"), AF.Copy, [p], [o])
            k.store(self.out[t0 - TC:t0 - TC + n, :], o, o.t[:, :])
        k.end()

    def phase_r1(self, i):
        k = self.k
        j = i // 2
        n = 128
        nh = n + 2
        vres = j > 0
        S = self.scratch
        RT = [S("RT%d" % d, [D, T], BF16) for d in range(2)]
        KT = [S("KT%d" % d, [D, T], BF16) for d in range(2)]
        BT = [S("BT%d" % d, [D, T], BF16) for d in range(2)]
        AT = [S("AT%d" % d, [D, T], BF16) for d in range(2)]
        GT = [S("GT%d" % d, [D, T], BF16) for d in range(2)]
        GAM = [S("GAM%d" % d, [D, NCHUNK]) for d in range(2)]
        BON = S("BON", [D, T], BF16)
        VTOK = S("VTOK", [T, D], BF16)
        VFIRST = S("VFIRST", [D, T])
        k.begin()
        stage = [k.sb([128, 1024], F32, "stg") for _ in range(2)]
        wr = k.sb([128, 8, D], BF16, "wr")
        wk = k.sb([128, 8, D], BF16, "wk")
        wv = k.sb([128, 8, D], BF16, "wv")
        for wt, si in ((wr, 0), (wk, 1), (wv, 2)):
            self.wload_mat(wt, self.W["rw_w_rkv"][j, si], 8, D, stage)
        w1 = k.sb([128, 8, 128], BF16, "w1")
        w2 = k.sb([64, 2, D], BF16, "w2")
        a1 = k.sb([128, 8, 64], BF16, "a1")
        a2 = k.sb([64, D], BF16, "a2")
        g1 = k.sb([128, 8, 320], BF16, "g1")
        g2a = k.sb([128, 2, D], BF16, "g2a")
        g2b = k.sb([128, 2, D], BF16, "g2b")
        k.memset(g2b.t[:, :, :], 0.0, [g2b], eng="pool")
        for d in range(2):
            self.wload_mat(w1, self.W["rw_w1"][j, d], 8, 64, stage, dcol0=d * 64)
            self.wload(w2, w2.t[0:64, d, :], self.W["rw_w2"][j, d], (64, D), stage)
            self.wload_mat(g1, self.W["rw_g1"][j, d], 8, 160, stage, dcol0=d * 160)
            self.wload(g2a, g2a.t[:, d, :], self.W["rw_g2"][j, d, 0:128, :], (128, D), stage)
            self.wload(g2b, g2b.t[0:32, d, :], self.W["rw_g2"][j, d, 128:160, :], (32, D), stage)
        self.wload_mat(a1, self.W["rw_a1"][j], 8, 64, stage)
        self.wload(a2, a2.t[0:64, :], self.W["rw_a2"][j], (64, D), stage)
        if vres:
            v1 = k.sb([128, 8, 32], BF16, "v1")
            v2 = k.sb([64, D], BF16, "v2")
            k.memset(v2.t[:, :], 0.0, [v2], eng="pool")
            self.wload_mat(v1, self.W["rw_v1"][0], 8, 32, stage)
            self.wload(v2, v2.t[0:32, :], self.W["rw_v2"][0], (32, D), stage)
        gs = k.sb([128, 2, 8], F32, "gs")
        sh = k.sb([128, 2, 8], F32, "sh")
        for wch in range(2):
            self.gs_cols(i, wch, 0, "ng%d_0" % i, gs, sh)
        F = lambda nm, dt=F32: k.sb([128, 8, n], dt, nm)
        xhs = [k.sb([128, 8, nh], F32, "xh") for _ in range(2)]
        sqh = k.sb([128, 8, nh], BF16, "sqh")
        rstd = k.sb([128, nh], F32, "rstd")
        xx = F("xx")
        tA = F("tA")
        xm = [F("xm%d" % m, BF16) for m in range(6)]
        R, Kf, Vf, Af = F("R"), F("Kf"), F("Vf"), F("Af")
        SG = [F("SG0"), F("SG1")]
        kk, inv, bq = F("kk"), F("inv"), F("bq")
        sqk = F("sqk", BF16)
        cum, cum2 = F("cum"), F("cum2")
        E1, E2, E3 = F("E1"), F("E2"), F("E3")
        if vres:
            vfl, vg = inv, bq
        outs = [F("o%d" % q, BF16) for q in range(8)]
        gam = [k.sb([128, 8, 2], F32, "gam") for _ in range(2)]
        vb = F("vb", BF16)
        vtok = [k.sb([128, D], BF16, "vtok") for _ in range(2)]
        th = k.sb([64, 2, n], BF16, "th")
        am = k.sb([64, n], BF16, "am")
        gm = k.sb([128, 2, n], BF16, "gm")
        gm2 = k.sb([128, 2, n], BF16, "gm2")
        k.memset(gm2.t[:, :, :], 0.0, [gm2], eng="pool")
        vm = k.sb([64, n], BF16, "vm")
        k.memset(vm.t[:, :], 0.0, [vm], eng="pool")
        pss = [k.ps([128, 512], F32, "pp") for _ in range(6)]
        ps_ss = k.ps([128, 512], F32, "pss")
        ptb = k.ps([128, D], BF16, "ptb")
        st = {"pi": 0, "oi": 0}

        def nextp():
            st["pi"] += 1
            return pss[st["pi"] % 6]

        def nexto():
            st["oi"] += 1
            return outs[st["oi"] % 8]

        def proj(wt, col0, M, x):
            p = nextp()
            for c in range(8):
                k.mm(p.t[0:M, 0:n], wt.t[:, c, col0:col0 + M], x.t[:, c, :], [wt, x], [p], start=(c == 0), stop=(c == 7), inc=(c == 7))
            return p

        rmask = self.cs("rmask")[:, 0:n]
        kkv = bc(self.vc("kk_%d" % j).unsqueeze(2), [128, 8, n])
        kav = bc(self.vc("ka_%d" % j).unsqueeze(2), [128, 8, n])
        rkv = bc(self.vc("rk_%d" % j).unsqueeze(2), [128, 8, n])
        import os
        STG = int(os.environ.get('R1_STAGE', '99'))
        NBLK = int(os.environ.get('R1_NBLK', '999'))
        for bi, (t0, nb, s0, s1) in enumerate(self.blocks(n)[:NBLK]):
            wch = 1 if t0 < TC else 0
            left = t0 > s0
            right = t0 + n < s1
            c0 = 0 if left else 1
            c1 = nh if right else n + 1
            xh = xhs[bi % 2]
            k.load(xh, xh.t[:, :, c0:c1], fm(self.XT, t0 - 1 + c0, c1 - c0))
            if not left:
                k.memset(xh.t[:, :, 0:1], 1.0, [xh], eng="pool")
            if not right:
                k.memset(xh.t[:, :, n + 1:nh], 1.0, [xh], eng="pool")
            self.norm_mod(xh, nh, None, None, None, i, wch, 0, sqh, ps_ss, rstd)
            k.tt(xh.t[:, :, :], xh.t[:, :, :], bc(rstd.t[:, 0:nh].unsqueeze(1), [128, 8, nh]), ALU.mult, [xh, rstd], [xh])
            for c in range(8):
                k.act(xh.t[:, c, :], xh.t[:, c, :], AF.Identity, [xh, gs, sh], [xh], bias=sh.t[:, wch, c:c + 1], scale=gs.t[:, wch, c:c + 1])
            if not left:
                k.memset(xh.t[:, :, 0:1], 0.0, [xh], eng="act" if False else "dve")
            if not right:
                k.memset(xh.t[:, :, n + 1:nh], 0.0, [xh], eng="dve")
            hc_ = xh.t[:, :, 1:n + 1]
            k.tt(tA.t[:, :, :], xh.t[:, :, 0:n], xh.t[:, :, 2:nh], ALU.add, [xh], [tA])
            k.stt(xx.t[:, :, :], tA.t[:, :, :], 0.5, hc_, ALU.mult, ALU.subtract, [tA, xh], [xx])
            for m in range(6):
                e = "pool" if m % 2 else "dve"
                mixv = bc(self.vc("mix%d_%d" % (j, m)).unsqueeze(2), [128, 8, n])
                k.tt(tA.t[:, :, :], xx.t[:, :, :], mixv, ALU.mult, [xx, self.vecs], [tA], eng=e)
                k.tt(xm[m].t[:, :, :], tA.t[:, :, :], hc_, ALU.add, [tA, xh], [xm[m]], eng=e)
            if STG < 1:
                continue
            for oc in range(8):
                p = proj(wr, oc * 128, 128, xm[0])
                k.act(R.t[:, oc, :], p.t[:, 0:n], AF.Copy, [p], [R])
            for oc in range(8):
                p = proj(wk, oc * 128, 128, xm[2])
                k.copy(Kf.t[:, oc, :], p.t[:, 0:n], [p], [Kf])
            for oc in range(8):
                p = proj(wv, oc * 128, 128, xm[3])
                k.act(Vf.t[:, oc, :], p.t[:, 0:n], AF.Copy, [p], [Vf])
            if STG < 2:
                continue
            for d in range(2):
                p = proj(w1, d * 64, 64, xm[1])
                k.act(th.t[0:64, d, :], p.t[0:64, 0:n], AF.Tanh, [p], [th])
                o, _ = VOFF["w0_%d_%d" % (j, d)]
                for oc in range(8):
                    p = nextp()
                    k.mm(p.t[:, 0:n], w2.t[0:64, d, oc * 128:(oc + 1) * 128], th.t[0:64, d, :], [w2, th], [p])
                    k.act(SG[d].t[:, oc, :], p.t[:, 0:n], AF.Sigmoid, [p, self.vecs], [SG[d]], bias=self.vecs.t[:, o + oc:o + oc + 1])
            p = proj(a1, 0, 64, xm[4])
            k.copy(am.t[0:64, :], p.t[0:64, 0:n], [p], [am])
            o, _ = VOFF["a0_%d" % j]
            for oc in range(8):
                p = nextp()
                k.mm(p.t[:, 0:n], a2.t[0:64, oc * 128:(oc + 1) * 128], am.t[0:64, :], [a2, am], [p])
                k.act(Af.t[:, oc, :], p.t[:, 0:n], AF.Sigmoid, [p, self.vecs], [Af], bias=self.vecs.t[:, o + oc:o + oc + 1])
            if STG < 3:
                continue
            for d in range(2):
                p = proj(g1, d * 160, 128, xm[5])
                k.act(gm.t[:, d, :], p.t[:, 0:n], AF.Sigmoid, [p], [gm])
                p = proj(g1, d * 160 + 128, 32, xm[5])
                k.act(gm2.t[0:32, d, :], p.t[0:32, 0:n], AF.Sigmoid, [p], [gm2])
                go = nexto()
                for oc in range(8):
                    p = nextp()
                    k.mm(p.t[:, 0:n], g2a.t[:, d, oc * 128:(oc + 1) * 128], gm.t[:, d, :], [g2a, gm], [p], start=True, stop=False, inc=False)
                    k.mm(p.t[:, 0:n], g2b.t[:, d, oc * 128:(oc + 1) * 128], gm2.t[:, d, :], [g2b, gm2], [p], start=False, stop=True)
                    if oc % 2:
                        k.copy(go.t[:, oc, :], p.t[:, 0:n], [p], [go])
                    else:
                        k.act(go.t[:, oc, :], p.t[:, 0:n], AF.Copy, [p], [go])
                k.store(fm(GT[d], t0, n), go, go.t[:, :, :])
            if vres:
                p = proj(v1, 0, 32, xm[3])
                k.copy(vm.t[0:32, :], p.t[0:32, 0:n], [p], [vm])
                o, _ = VOFF["v0"]
                for oc in range(8):
                    p = nextp()
                    k.mm(p.t[:, 0:n], v2.t[0:64, oc * 128:(oc + 1) * 128], vm.t[0:64, :], [v2, vm], [p])
                    k.act(vg.t[:, oc, :], p.t[:, 0:n], AF.Sigmoid, [p, self.vecs], [vg], bias=self.vecs.t[:, o + oc:o + oc + 1])
                k.load(vfl, vfl.t[:, :, :], fm(VFIRST, t0, n))
                k.tt(vfl.t[:, :, :], vfl.t[:, :, :], Vf.t[:, :, :], ALU.subtract, [vfl, Vf], [vfl], eng="pool")
                k.tt(vfl.t[:, :, :], vfl.t[:, :, :], vg.t[:, :, :], ALU.mult, [vfl, vg], [vfl], eng="pool")
                k.tt(Vf.t[:, :, :], Vf.t[:, :, :], vfl.t[:, :, :], ALU.add, [vfl, Vf], [Vf], eng="pool")
            else:
                k.store(fm(VFIRST, t0, n), Vf, Vf.t[:, :, :])
            if STG < 4:
                continue
            k.copy(vb.t[:, :, :], Vf.t[:, :, :], [Vf], [vb], eng="pool")
            for c in range(8):
                k.tr(ptb.t[:, c * 128:(c + 1) * 128], vb.t[:, c, :], self.identb.t[:, :], [vb, self.identb], [ptb], inc=(c == 7))
            vt = vtok[bi % 2]
            k.act(vt.t[:, :], ptb.t[:, :], AF.Copy, [ptb], [vt])
            k.store(VTOK[t0:t0 + n, :], vt, vt.t[:, :])
            if STG < 5:
                continue
            k.tt(kk.t[:, :, :], Kf.t[:, :, :], kkv, ALU.mult, [Kf, self.vecs], [kk])
            k.tt(sqk.t[:, :, :], kk.t[:, :, :], kk.t[:, :, :], ALU.mult, [kk], [sqk], eng="pool")
            for c in range(8):
                p = nextp()
                k.mm(p.t[:, 0:n], self.blkb.t[:, :], sqk.t[:, c, :], [self.blkb, sqk], [p])
                k.act(inv.t[:, c, :], p.t[:, 0:n], AF.Ln, [p], [inv], bias=1e-24)
            k.act(inv.t[:, :, :], inv.t[:, :, :], AF.Exp, [inv], [inv], scale=-0.5)
            k.tt(kk.t[:, :, :], kk.t[:, :, :], inv.t[:, :, :], ALU.mult, [kk, inv], [kk])
            k.ts(tA.t[:, :, :], Af.t[:, :, :], -1.0, None, ALU.add, None, [Af], [tA], eng="pool")
            k.tt(tA.t[:, :, :], tA.t[:, :, :], kav, ALU.mult, [tA, self.vecs], [tA], eng="pool")
            k.stt(Kf.t[:, :, :], tA.t[:, :, :], 1.0, Kf.t[:, :, :], ALU.add, ALU.mult, [tA, Kf], [Kf])
            k.tt(bq.t[:, :, :], kk.t[:, :, :], Af.t[:, :, :], ALU.mult, [kk, Af], [bq], eng="pool")
            if STG < 6:
                continue
            k.tt(tA.t[:, :, :], R.t[:, :, :], Kf.t[:, :, :], ALU.mult, [R, Kf], [tA])
            k.tt(sqk.t[:, :, :], tA.t[:, :, :], rkv, ALU.mult, [tA, self.vecs], [sqk])
            bo = nexto()
            for c in range(8):
                p = nextp()
                k.mm(p.t[:, 0:n], self.blkb.t[:, :], sqk.t[:, c, :], [self.blkb, sqk], [p])
                k.tt(bo.t[:, c, :], p.t[:, 0:n], Vf.t[:, c, :], ALU.mult, [p, Vf], [bo])
            k.store(fm(BON, t0, n), bo, bo.t[:, :, :])
            if STG < 7:
                continue
            nq = n // CH
            for d in range(2):
                sg = SG[d]
                for c in range(8):
                    k.op("dve", lambda h, c=c: h.tensor_tensor_scan(out=cum.t[:, c, :], data0=rmask, data1=sg.t[:, c, :], initial=0.0,
                                                                    op0=ALU.mult, op1=ALU.add), [sg, self.con], [cum])
                if d == 0:
                    cd = cum
                else:
                    k.tt(tA.t[:, :, :], sg.t[:, :, :], cum.t[:, :, :], ALU.subtract, [sg, cum], [tA], eng="pool")
                    c4 = cum.t[:, :, :].rearrange("p c (q t) -> p (c q) t", t=CH)
                    k.tt(cum2.t[:, :, :].rearrange("p c (q t) -> p (c q) t", t=CH), tA.t[:, :, :].rearrange("p c (q t) -> p (c q) t", t=CH),
                         bc(c4[:, :, CH - 1:CH], [128, 8 * nq, CH]), ALU.add, [tA, cum], [cum2], eng="pool")
                    cd = cum2
                k.act(E1.t[:, :, :], cd.t[:, :, :], AF.Exp, [cd], [E1], scale=-DEC)
                k.act(E2.t[:, :, :], cd.t[:, :, :], AF.Exp, [cd], [E2], scale=DEC)
                k.tt(tA.t[:, :, :], cd.t[:, :, :], sg.t[:, :, :], ALU.subtract, [cd, sg], [tA], eng="pool")
                k.act(E3.t[:, :, :], tA.t[:, :, :], AF.Exp, [tA], [E3], scale=-DEC)
                g = gam[d]
                e4 = E1.t[:, :, :].rearrange("p c (q t) -> p c q t", t=CH)
                col = CH - 1 if d == 0 else 0
                k.copy(g.t[:, :, :], e4[:, :, :, col], [E1], [g], eng="pool")
                k.store(GAM[d].rearrange("(c p) q -> p c q", p=128)[:, :, t0 // CH:t0 // CH + nq], g, g.t[:, :, :])
                o_ = nexto()
                k.tt(o_.t[:, :, :], R.t[:, :, :], E1.t[:, :, :], ALU.mult, [R, E1], [o_])
                k.store(fm(RT[d], t0, n), o_, o_.t[:, :, :])
                o_ = nexto()
                k.tt(o_.t[:, :, :], Kf.t[:, :, :], E2.t[:, :, :], ALU.mult, [Kf, E2], [o_], eng="pool")
                k.store(fm(KT[d], t0, n), o_, o_.t[:, :, :])
                o_ = nexto()
                k.tt(o_.t[:, :, :], bq.t[:, :, :], E2.t[:, :, :], ALU.mult, [bq, E2], [o_])
                k.store(fm(BT[d], t0, n), o_, o_.t[:, :, :])
                o_ = nexto()
                k.stt(o_.t[:, :, :], kk.t[:, :, :], -1.0, E3.t[:, :, :], ALU.mult, ALU.mult, [kk, E3], [o_])
                k.store(fm(AT[d], t0, n), o_, o_.t[:, :, :])
        k.end()

    def phase_r2(self, i):
        k = self.k
        S = self.scratch
        RT = [S("RT%d" % d, [D, T], BF16) for d in range(2)]
        KT = [S("KT%d" % d, [D, T], BF16) for d in range(2)]
        BT = [S("BT%d" % d, [D, T], BF16) for d in range(2)]
        AT = [S("AT%d" % d, [D, T], BF16) for d in range(2)]
        GAM = [S("GAM%d" % d, [D, NCHUNK]) for d in range(2)]
        VTOK = S("VTOK", [T, D], BF16)
        OT = [S("OT%d" % d, [D, T]) for d in range(2)]
        hv = lambda ap: ap.rearrange("(h k) t -> k h t", k=64)
        k.begin()
        H8 = 8
        co = COFF
        mk = k.sb([128, 2, 128], F32, "mk")
        mt = k.sb([128, 2, 64], F32, "mt")
        idf = k.sb([128, 64], F32, "idf")
        for e in range(2):
            rows = slice(64 * e, 64 * e + 64)
            for d, (ms, mi, mtn) in enumerate((("mf_s", "mf_i", "mb_s"), ("mb_s", "mb_i", "mf_s"))):
                k.load(mk, mk.t[rows, d, 0:64], self.consts_d[0:64, co[ms][0]:co[ms][0] + 64])
                k.load(mk, mk.t[rows, d, 64:128], self.consts_d[0:64, co[mi][0]:co[mi][0] + 64])
                k.load(mt, mt.t[rows, d, :], self.consts_d[0:64, co[mtn][0]:co[mtn][0] + 64])
            k.load(idf, idf.t[rows, :], self.consts_d[0:64, co["ident"][0]:co["ident"][0] + 64])
        gam = [k.sb([128, H8, NCHUNK], F32, "gam") for _ in range(2)]
        for d in range(2):
            for e in range(2):
                k.load(gam[d], gam[d].t[64 * e:64 * e + 64, :, :], GAM[d].rearrange("(h k) q -> k h q", k=64)[:, e * H8:(e + 1) * H8, :])
        pg = [[k.ps([128, H8, 64], F32, "pg") for _ in range(3)] for _ in range(2)]
        ptk = [k.ps([128, H8, 2, 64], BF16, "ptk") for _ in range(2)]
        gi = [0, 0]

        def nextg(e):
            gi[e] += 1
            return pg[e][gi[e] % 3]

        ch = {}
        for d in range(2):
            for e in range(2):
                c = {"d": d, "e": e, "rows": slice(64 * e, 64 * e + 64)}
                A = lambda shape, dt, nm: k.sb([128] + shape, dt, nm)
                c["kb"] = [A([H8, 2, 64], BF16, "kb") for _ in range(2)]
                c["ar"] = [A([H8, 2, 64], BF16, "ar") for _ in range(2)]
                c["v"] = [A([H8, 64], BF16, "v") for _ in range(2)]
                c["MK"] = A([H8, 128], BF16, "MK")
                c["MB"] = A([H8, 64], BF16, "MB")
                c["Ub"] = A([H8, 64], BF16, "Ub")
                c["XT"] = [A([H8, 64], F32, "XT") for _ in range(2)]
                c["X"] = [A([H8, 64], F32, "X") for _ in range(2)]
                c["P"] = A([H8, 64], F32, "P")
                c["Z"] = A([H8, 64], F32, "Z")
                c["kbtok"] = A([H8, 2, 64], BF16, "kbtok")
                c["Hf"] = A([H8, 64], F32, "Hf")
                c["Hb"] = A([H8, 64], BF16, "Hb")
                c["tmp"] = A([H8, 64], F32, "tmp")
                c["ot"] = [A([H8, 64], F32, "ot") for _ in range(2)]
                r_ = c["rows"]
                k.memset(c["Hf"].t[r_, :, :], 0.0, [c["Hf"]])
                k.memset(c["Hb"].t[r_, :, :], 0.0, [c["Hb"]], eng="pool")
                ch[(d, e)] = c
        order = [list(range(NCHUNK)), [3, 2, 1, 0] + list(range(NCHUNK - 1, 3, -1))]
        import os
        NSTEP = int(os.environ.get("R2_NSTEP", str(NCHUNK)))

        def flat(ap):
            return ap.rearrange("p a b -> p (a b)")

        def group(d, fn_mm, n_acc=1):
            ps = [nextg(0), nextg(1)]
            for h in range(H8):
                for e in range(2):
                    fn_mm(ch[(d, e)], ps[e], h, last=(h == H8 - 1))
            return ps

        for step in range(NSTEP):
            for d in range(2):
                for e in range(2):
                    c = ch[(d, e)]
                    r_ = c["rows"]
                    q = order[d][step]
                    c0 = q * CH
                    h0 = e * H8
                    kb = c["kb"][step % 2]
                    ar = c["ar"][step % 2]
                    v = c["v"][step % 2]
                    k.load(kb, kb.t[r_, :, 0, :], hv(KT[d])[:, h0:h0 + H8, c0:c0 + CH])
                    k.load(kb, kb.t[r_, :, 1, :], hv(BT[d])[:, h0:h0 + H8, c0:c0 + CH])
                    k.load(ar, ar.t[r_, :, 0, :], hv(AT[d])[:, h0:h0 + H8, c0:c0 + CH])
                    k.load(ar, ar.t[r_, :, 1, :], hv(RT[d])[:, h0:h0 + H8, c0:c0 + CH])
                    k.load(v, v.t[r_, :, :], VTOK[c0:c0 + CH, h0 * 64:(h0 + H8) * 64].rearrange("t (h e) -> t h e", e=64))
                    c["cur"] = (kb, ar, v, q, c0, h0)
            for d in range(2):
                for (li, ri, dst) in ((0, 0, "ak"), (0, 1, "rk"), (1, 0, "x"), (1, 1, "rb")):
                    def f(c, p, h, last, li=li, ri=ri):
                        kb, ar = c["cur"][0], c["cur"][1]
                        r_ = c["rows"]
                        k.mm(p.t[r_, h, :], kb.t[r_, h, li, :], ar.t[r_, h, ri, :], [kb, ar], [p], inc=last)
                    ps = group(d, f)
                    for e in range(2):
                        c = ch[(d, e)]
                        r_ = c["rows"]
                        p = ps[e]
                        if dst == "ak":
                            k.tt(c["MK"].t[r_, :, 0:64], p.t[r_, :, :], bc(mk.t[r_, d, 0:64].unsqueeze(1), [64, H8, 64]), ALU.mult, [p, mk], [c["MK"]])
                        elif dst == "rk":
                            k.tt(c["MK"].t[r_, :, 64:128], p.t[r_, :, :], bc(mk.t[r_, d, 64:128].unsqueeze(1), [64, H8, 64]), ALU.mult, [p, mk], [c["MK"]])
                        elif dst == "x":
                            k.tt(c["X"][0].t[r_, :, :], p.t[r_, :, :], bc(mk.t[r_, d, 0:64].unsqueeze(1), [64, H8, 64]), ALU.mult, [p, mk], [c["X"][0]])
                            k.tt(c["P"].t[r_, :, :], c["X"][0].t[r_, :, :], bc(idf.t[r_, :].unsqueeze(1), [64, H8, 64]), ALU.add,
                                 [c["X"][0], idf], [c["P"]], eng="pool")
                        else:
                            k.tt(c["MB"].t[r_, :, :], p.t[r_, :, :], bc(mk.t[r_, d, 64:128].unsqueeze(1), [64, H8, 64]), ALU.mult, [p, mk], [c["MB"]])

                def fxt(c, p, h, last):
                    kb, ar = c["cur"][0], c["cur"][1]
                    r_ = c["rows"]
                    k.mm(p.t[r_, h, :], ar.t[r_, h, 0, :], kb.t[r_, h, 1, :], [kb, ar], [p], inc=last)
                ps = group(d, fxt)
                for e in range(2):
                    c = ch[(d, e)]
                    r_ = c["rows"]
                    k.tt(c["XT"][0].t[r_, :, :], ps[e].t[r_, :, :], bc(mt.t[r_, d, :].unsqueeze(1), [64, H8, 64]), ALU.mult, [ps[e], mt], [c["XT"][0]])
                for h in range(H8):
                    for a_ in range(2):
                        for e in range(2):
                            c = ch[(d, e)]
                            r_ = c["rows"]
                            kb = c["cur"][0]
                            k.tr(ptk[e].t[r_, h, a_, :], kb.t[r_, h, a_, :], self.identb.t[r_, 64 * e:64 * e + 64], [kb, self.identb], [ptk[e]],
                                 inc=(h == H8 - 1 and a_ == 1))
                for e in range(2):
                    c = ch[(d, e)]
                    r_ = c["rows"]
                    k.act(c["kbtok"].t[r_, :, :, :], ptk[e].t[r_, :, :, :], AF.Copy, [ptk[e]], [c["kbtok"]])
            for lv in range(5):
                for d in range(2):
                    def fa(c, p, h, last, lv=lv):
                        r_ = c["rows"]
                        Xc, XTc = c["X"][lv % 2], c["XT"][lv % 2]
                        k.mm(p.t[r_, h, :], Xc.t[r_, h, :], XTc.t[r_, h, :], [XTc, Xc], [p], inc=last)
                    ps = group(d, fa)
                    for e in range(2):
                        c = ch[(d, e)]
                        r_ = c["rows"]
                        k.act(c["XT"][(lv + 1) % 2].t[r_, :, :], ps[e].t[r_, :, :], AF.Copy, [ps[e]], [c["XT"][(lv + 1) % 2]])
                    if lv < 4:
                        def fb(c, p, h, last, lv=lv):
                            r_ = c["rows"]
                            Xc, XTc = c["X"][lv % 2], c["XT"][lv % 2]
                            k.mm(p.t[r_, h, :], XTc.t[r_, h, :], Xc.t[r_, h, :], [XTc, Xc], [p], inc=last)
                        ps = group(d, fb)
                        for e in range(2):
                            c = ch[(d, e)]
                            r_ = c["rows"]
                            k.copy(c["X"][(lv + 1) % 2].t[r_, :, :], ps[e].t[r_, :, :], [ps[e]], [c["X"][(lv + 1) % 2]])

                    def fc(c, p, h, last, lv=lv):
                        r_ = c["rows"]
                        XTn = c["XT"][(lv + 1) % 2]
                        k.mm(p.t[r_, h, :], XTn.t[r_, h, :], c["P"].t[r_, h, :], [XTn, c["P"]], [p], inc=last)
                    ps = group(d, fc)
                    for e in range(2):
                        c = ch[(d, e)]
                        r_ = c["rows"]
                        k.tt(c["P"].t[r_, :, :], ps[e].t[r_, :, :], c["P"].t[r_, :, :], ALU.add, [ps[e], c["P"]], [c["P"]])
            for d in range(2):
                def fy(c, p, h, last):
                    kb, ar, v = c["cur"][0], c["cur"][1], c["cur"][2]
                    r_ = c["rows"]
                    k.mm(p.t[r_, h, :], ar.t[r_, h, 0, :], c["Hb"].t[r_, h, :], [ar, c["Hb"]], [p], start=True, stop=False, inc=False)
                    k.mm(p.t[r_, h, :], c["MK"].t[r_, h, 0:64], v.t[r_, h, :], [c["MK"], v], [p], start=False, stop=True, inc=last)
                ps = group(d, fy)
                for e in range(2):
                    c = ch[(d, e)]
                    r_ = c["rows"]
                    k.act(c["Z"].t[r_, :, :], ps[e].t[r_, :, :], AF.Copy, [ps[e]], [c["Z"]])
            for d in range(2):
                def fu(c, p, h, last):
                    r_ = c["rows"]
                    k.mm(p.t[r_, h, :], c["P"].t[r_, h, :], c["Z"].t[r_, h, :], [c["P"], c["Z"]], [p], inc=last)
                ps = group(d, fu)
                for e in range(2):
                    c = ch[(d, e)]
                    r_ = c["rows"]
                    k.act(c["Ub"].t[r_, :, :], ps[e].t[r_, :, :], AF.Copy, [ps[e]], [c["Ub"]])
            for d in range(2):
                def fo(c, p, h, last):
                    kb, ar, v = c["cur"][0], c["cur"][1], c["cur"][2]
                    r_ = c["rows"]
                    k.mm(p.t[r_, h, :], c["Hb"].t[r_, h, :], ar.t[r_, h, 1, :], [c["Hb"], ar], [p], start=True, stop=False, inc=False)
                    k.mm(p.t[r_, h, :], c["Ub"].t[r_, h, :], c["MB"].t[r_, h, :], [c["Ub"], c["MB"]], [p], start=False, stop=False, inc=False)
                    k.mm(p.t[r_, h, :], v.t[r_, h, :], c["MK"].t[r_, h, 64:128], [v, c["MK"]], [p], start=False, stop=True, inc=last)
                ps = group(d, fo)
                for e in range(2):
                    c = ch[(d, e)]
                    r_ = c["rows"]
                    kb, ar, v, q, c0, h0 = c["cur"]
                    ot = c["ot"][step % 2]
                    k.act(ot.t[r_, :, :], ps[e].t[r_, :, :], AF.Copy, [ps[e]], [ot])
                    k.store(hv(OT[d])[:, h0:h0 + H8, c0:c0 + CH], ot, ot.t[r_, :, :])

                def fh(c, p, h, last):
                    v = c["cur"][2]
                    r_ = c["rows"]
                    k.mm(p.t[r_, h, :], c["kbtok"].t[r_, h, 1, :], c["Ub"].t[r_, h, :], [c["kbtok"], c["Ub"]], [p], start=True, stop=False, inc=False)
                    k.mm(p.t[r_, h, :], c["kbtok"].t[r_, h, 0, :], v.t[r_, h, :], [c["kbtok"], v], [p], start=False, stop=True, inc=last)
                ps = group(d, fh)
                for e in range(2):
                    c = ch[(d, e)]
                    r_ = c["rows"]
                    q = c["cur"][3]
                    k.tt(c["tmp"].t[r_, :, :], ps[e].t[r_, :, :], c["Hf"].t[r_, :, :], ALU.add, [ps[e], c["Hf"]], [c["tmp"]])
                    k.tt(c["Hf"].t[r_, :, :], c["tmp"].t[r_, :, :], bc(gam[d].t[r_, :, q:q + 1], [64, H8, 64]), ALU.mult,
                         [c["tmp"], gam[d]], [c["Hf"]])
                    k.copy(c["Hb"].t[r_, :, :], c["Hf"].t[r_, :, :], [c["Hf"]], [c["Hb"]], eng="pool")
        k.end()

    def phase_r3(self, i):
        k = self.k
        j = i // 2
        n = 256
        S = self.scratch
        OT = [S("OT%d" % d, [D, T]) for d in range(2)]
        GT = [S("GT%d" % d, [D, T], BF16) for d in range(2)]
        BON = S("BON", [D, T], BF16)
        k.begin()
        F = lambda nm, dt=F32: k.sb([128, 8, n], dt, nm)
        o_in = [[F("o_in") for _ in range(2)] for _ in range(2)]
        g_in = [[F("g_in", BF16) for _ in range(2)] for _ in range(2)]
        b_in = [F("b_in", BF16) for _ in range(2)]
        sqs, Mts, E2ts, dds = [[F(nm) for _ in range(2)] for nm in ("sq", "Mt", "E2t", "dd")]
        yaccs = [F("yacc") for _ in range(2)]
        yo = [F("yo", BF16) for _ in range(2)]
        pss = [k.ps([128, 512], F32, "pp") for _ in range(6)]
        blk = self.cs("blk")
        lng = bc(self.vc("lng_%d" % j).unsqueeze(2), [128, 8, n])
        lnb = bc(self.vc("lnb_%d" % j).unsqueeze(2), [128, 8, n])
        pi = 0
        for bi, (t0, nb, s0, s1) in enumerate(self.blocks(n)):
            bon = b_in[bi % 2]
            k.load(bon, bon.t[:, :, :], fm(BON, t0, n))
            for d in range(2):
                o = o_in[d][bi % 2]
                g = g_in[d][bi % 2]
                sq, Mt, E2t, dd = sqs[d], Mts[d], E2ts[d], dds[d]
                yacc = yaccs[bi % 2]
                k.load(o, o.t[:, :, :], fm(OT[d], t0, n))
                k.load(g, g.t[:, :, :], fm(GT[d], t0, n))
                k.act(sq.t[:, :, :], o.t[:, :, :], AF.Square, [o], [sq])
                for c in range(8):
                    p = pss[pi % 6]
                    pi += 1
                    k.mm(p.t[:, 0:n], blk, o.t[:, c, :], [self.con, o], [p])
                    k.act(Mt.t[:, c, :], p.t[:, 0:n], AF.Copy, [p], [Mt], scale=1.0 / 64)
                    p = pss[pi % 6]
                    pi += 1
                    k.mm(p.t[:, 0:n], blk, sq.t[:, c, :], [self.con, sq], [p])
                    k.copy(E2t.t[:, c, :], p.t[:, 0:n], [p], [E2t])
                k.tt(sq.t[:, :, :], Mt.t[:, :, :], Mt.t[:, :, :], ALU.mult, [Mt], [sq], eng="pool")
                k.stt(E2t.t[:, :, :], E2t.t[:, :, :], 1.0 / 64, sq.t[:, :, :], ALU.mult, ALU.subtract, [E2t, sq], [E2t])
                k.act(E2t.t[:, :, :], E2t.t[:, :, :], AF.Ln, [E2t], [E2t], bias=64e-5)
                k.act(E2t.t[:, :, :], E2t.t[:, :, :], AF.Exp, [E2t], [E2t], scale=-0.5)
                k.tt(dd.t[:, :, :], o.t[:, :, :], Mt.t[:, :, :], ALU.subtract, [o, Mt], [dd])
                k.tt(dd.t[:, :, :], dd.t[:, :, :], E2t.t[:, :, :], ALU.mult, [dd, E2t], [dd])
                k.tt(dd.t[:, :, :], dd.t[:, :, :], lng, ALU.mult, [dd, self.vecs], [dd], eng="pool")
                k.tt(dd.t[:, :, :], dd.t[:, :, :], lnb, ALU.add, [dd, self.vecs], [dd], eng="pool")
                k.tt(dd.t[:, :, :], dd.t[:, :, :], bon.t[:, :, :], ALU.add, [dd, bon], [dd])
                if d == 0:
                    k.tt(yacc.t[:, :, :], dd.t[:, :, :], g.t[:, :, :], ALU.mult, [dd, g], [yacc])
                else:
                    k.tt(dd.t[:, :, :], dd.t[:, :, :], g.t[:, :, :], ALU.mult, [dd, g], [dd])
                    y = yo[bi % 2]
                    k.tt(y.t[:, :, :], dd.t[:, :, :], yacc.t[:, :, :], ALU.add, [dd, yacc], [y], eng="pool")
                    k.store(fm(self.YT, t0, n), y, y.t[:, :, :])
        k.end()

    def phase_a1(self, i):
        k = self.k
        j = i // 2
        last = i == DEPTH - 1
        n = 256
        S = self.scratch
        QA = S("QA", [D, T], BF16)
        KA = S("KA", [D, T], BF16)
        VA = S("VA", [T, 8 * 130], BF16)
        k.begin()
        stage = [k.sb([128, 1024], F32, "stg") for _ in range(2)]
        w = k.sb([128, 8, 3 * D], BF16, "wqkv")
        self.wload_mat(w, self.W["da_w_qkv"][j], 8, 3 * D, stage)
        permb = k.sb([128, 128], BF16, "permb")
        k.copy(permb.t[:, :], self.cs("perm"), [self.con], [permb])
        gs = k.sb([128, 2, 8], F32, "gs")
        sh = k.sb([128, 2, 8], F32, "sh")
        for wch in range(2):
            self.gs_cols(i, wch, 0, "ng%d_0" % i, gs, sh)
        xts = [k.sb([128, 8, n], F32, "xt") for _ in range(2)]
        sqs = [k.sb([128, 8, n], BF16, "sq") for _ in range(2)]
        hbs = [k.sb([128, 8, n], BF16, "hb") for _ in range(2)]
        rstds = [k.sb([128, n], F32, "rstd") for _ in range(2)]
        cs_ = [k.sb([128, 2, n], F32, "cs") for _ in range(2)]
        qb = [k.sb([128, n], BF16, "qb") for _ in range(2)]
        t1 = [k.sb([128, n], F32, "t1") for _ in range(2)]
        t2 = [k.sb([128, n], F32, "t2") for _ in range(2)]
        qo = [k.sb([128, 8, n], BF16, "qo") for _ in range(2)]
        ko = [k.sb([128, 8, n], BF16, "ko") for _ in range(2)]
        vo = [k.sb([128, 8, 130], BF16, "vo") for _ in range(2)]
        for v_ in vo:
            k.memset(v_.t[:, :, 128:130], 1.0, [v_])
        pss = [k.ps([128, 512], F32, "pp") for _ in range(4)]
        pr = [k.ps([128, 512], F32, "pr") for _ in range(2)]
        ps_ss = k.ps([128, 512], F32, "pss")
        pi = 0
        co, _ = COFF["cos"]
        so, _ = COFF["sin"]
        vi = 0
        import os
        STG = int(os.environ.get('A1_STAGE', '99'))
        NBLK = int(os.environ.get('A1_NBLK', '999'))
        for bi, (t0, nb, s0, s1) in enumerate(self.blocks(n)[:NBLK]):
            wch = 1 if t0 < TC else 0
            islat = t0 >= TC
            xt = xts[bi % 2]
            sq, hb, rstd = sqs[bi % 2], hbs[bi % 2], rstds[bi % 2]
            k.load(xt, xt.t[:, :, :], fm(self.XT, t0, n))
            if islat:
                cs = cs_[bi % 2]
                k.load(cs, cs.t[:, 0, :], self.consts_d[:, co + t0 - TC:co + t0 - TC + n])
                k.load(cs, cs.t[:, 1, :], self.consts_d[:, so + t0 - TC:so + t0 - TC + n])
            self.norm_mod(xt, n, None, None, None, i, wch, 0, sq, ps_ss, rstd)
            k.tt(xt.t[:, :, :], xt.t[:, :, :], bc(rstd.t[:, 0:n].unsqueeze(1), [128, 8, n]), ALU.mult, [xt, rstd], [xt], eng="pool")
            for c in range(8):
                k.act(hb.t[:, c, :], xt.t[:, c, :], AF.Identity, [xt, gs, sh], [hb], bias=sh.t[:, wch, c:c + 1], scale=gs.t[:, wch, c:c + 1])
            if STG < 1:
                continue
            for which, dst_s, dst in ((0, qo, QA), (1, ko, KA)):
                if which == 0 and last and not islat:
                    continue
                ob = dst_s[bi % 2]
                for oc in range(8):
                    p = pss[pi % 4]
                    pi += 1
                    for c in range(8):
                        k.mm(p.t[:, 0:n], w.t[:, c, which * D + oc * 128:which * D + (oc + 1) * 128], hb.t[:, c, :], [w, hb], [p],
                             start=(c == 0), stop=(c == 7), inc=(c == 7))
                    if not islat:
                        k.act(ob.t[:, oc, :], p.t[:, 0:n], AF.Copy, [p], [ob])
                        continue
                    q_ = qb[oc % 2]
                    k.act(q_.t[:, :], p.t[:, 0:n], AF.Copy, [p], [q_])
                    p2 = pr[oc % 2]
                    if os.environ.get("A1_NOPERM") == "1":
                        p2 = p
                    else:
                        k.mm(p2.t[:, 0:n], permb.t[:, :], q_.t[:, :], [permb, q_], [p2])
                    a1_, a2_ = t1[oc % 2], t2[oc % 2]
                    k.tt(a1_.t[:, :], q_.t[:, :], cs.t[:, 0, :], ALU.mult, [q_, cs], [a1_])
                    k.tt(a2_.t[:, :], p2.t[:, 0:n], cs.t[:, 1, :], ALU.mult, [p2, cs], [a2_])
                    k.tt(ob.t[:, oc, :], a1_.t[:, :], a2_.t[:, :], ALU.add, [a1_, a2_], [ob], eng=os.environ.get("A1_ADD", "pool"))
                k.store(fm(dst, t0, n), ob, ob.t[:, :, :])
            if STG < 3:
                continue
            for ts in range(n // 128):
                v_ = vo[vi % 2]
                vi += 1
                for half in range(2):
                    p = pss[pi % 4]
                    pi += 1
                    for c in range(8):
                        k.mm(p.t[:, 0:512], hb.t[:, c, ts * 128:(ts + 1) * 128], w.t[:, c, 2 * D + half * 512:2 * D + (half + 1) * 512], [w, hb], [p],
                             start=(c == 0), stop=(c == 7), inc=(c == 7))
                    if half == 0:
                        k.act(v_.t[:, 0:4, 0:128], p.t[:, 0:512].rearrange("p (h e) -> p h e", e=128), AF.Copy, [p], [v_])
                    else:
                        k.copy(v_.t[:, 4:8, 0:128], p.t[:, 0:512].rearrange("p (h e) -> p h e", e=128), [p], [v_])
                k.store(VA[t0 + ts * 128:t0 + (ts + 1) * 128, :].rearrange("t (h e) -> t h e", e=130), v_, v_.t[:, :, :])
        k.end()

    def phase_a2(self, i):
        k = self.k
        j = i // 2
        last = i == DEPTH - 1
        lambda_init = 0.8 - 0.6 * math.exp(-0.3 * i)
        S = self.scratch
        QA = S("QA", [D, T], BF16)
        KA = S("KA", [D, T], BF16)
        VA = S("VA", [T, 8 * 130], BF16)
        NKT = T // 128
        k.begin()
        lv = k.sb([128, 4, 64], F32, "lv")
        k.load(lv, lv.t[:, :, :].rearrange("p a e -> p (a e)"), self.lam_d[j].rearrange("a e -> (a e)").partition_broadcast(128))
        lt = k.sb([128, 2, 64], F32, "lt")
        ls = k.sb([128, 4], F32, "ls")
        k.tt(lt.t[:, 0, :], lv.t[:, 0, :], lv.t[:, 1, :], ALU.mult, [lv], [lt])
        k.tt(lt.t[:, 1, :], lv.t[:, 2, :], lv.t[:, 3, :], ALU.mult, [lv], [lt])
        k.op("dve", lambda h: h.reduce_sum(out=ls.t[:, 0:2], in_=lt.t[:, :, :], axis=mybir.AxisListType.X), [lt], [ls])
        k.act(ls.t[:, 0:2], ls.t[:, 0:2], AF.Exp, [ls], [ls])
        k.tt(ls.t[:, 2:3], ls.t[:, 0:1], ls.t[:, 1:2], ALU.subtract, [ls], [ls])
        k.ts(ls.t[:, 3:4], ls.t[:, 2:3], float(lambda_init), None, ALU.add, None, [ls], [ls])
        lam = ls.t[:, 3:4]
        gsub = k.sb([128, 128], F32, "gsub")
        k.load(gsub, gsub.t[:, :], self.subg_d[j].partition_broadcast(128))
        k.ts(gsub.t[:, :], gsub.t[:, :], float(1.0 - lambda_init), None, ALU.mult, None, [gsub], [gsub])
        nq = 512
        Kh = [k.sb([128, T], BF16, "Kh") for _ in range(2)]
        Qh = [k.sb([128, T], BF16, "Qh") for _ in range(2)]
        Vh = [k.sb([128, NKT, 130], BF16, "Vh") for _ in range(2)]
        PT = [[k.sb([128, NKT, nq], BF16, "PT") for _ in range(2)] for _ in range(2)]
        psS = [[k.ps([128, 512], F32, "psS") for _ in range(2)] for _ in range(2)]
        psO = [k.ps([128, 512], F32, "psO") for _ in range(2)]
        ptr = k.ps([128, 4, 128], BF16, "ptr")
        rr = k.sb([128, 4], F32, "rr")
        tt_ = k.sb([128, 128], F32, "tt")
        oo = k.sb([128, 128], F32, "oo")
        junk = k.sb([128, 128], F32, "junk")
        on = k.sb([128, 128], BF16, "on")
        yT = [k.sb([128, nq], BF16, "yT") for _ in range(2)]
        qblocks = [] if last else [(0, TC, [0, 1])]
        qblocks += [(TC + b_ * nq, nq, list(range(NKT))) for b_ in range(TL // nq)]
        import os
        NHD = int(os.environ.get("A2_NH", "8"))
        st = {"si": 0, "yi": 0}

        def emit_s(item, pset):
            h, (q0, nqb, kts) = item
            kh, qh = Kh[h % 2], Qh[h % 2]
            for kt in kts:
                st["si"] += 1
                pp_ = [psS[s_][st["si"] % 2] for s_ in range(2)]
                for s_ in range(2):
                    k.mm(pp_[s_].t[:, 0:nqb], kh.t[64 * s_:64 * s_ + 64, kt * 128:(kt + 1) * 128], qh.t[64 * s_:64 * s_ + 64, q0:q0 + nqb], [kh, qh], [pp_[s_]])
                for s_ in range(2):
                    k.act(PT[pset][s_].t[:, kt, 0:nqb], pp_[s_].t[:, 0:nqb], AF.Exp, [pp_[s_]], [PT[pset][s_]], scale=0.125)

        def emit_pv(item, pset):
            h, (q0, nqb, kts) = item
            vh = Vh[h % 2]
            y = yT[st["yi"] % 2]
            st["yi"] += 1
            for qs in range(nqb // 128):
                for s_ in range(2):
                    for ki, kt in enumerate(kts):
                        k.mm(psO[s_].t[:, 0:130], PT[pset][s_].t[:, kt, qs * 128:(qs + 1) * 128], vh.t[:, kt, :], [PT[pset][s_], vh], [psO[s_]],
                             start=(ki == 0), stop=(ki == len(kts) - 1), inc=(ki == len(kts) - 1))
                k.op("dve", lambda h_: h_.reciprocal(out=rr.t[:, 0:1], in_=psO[0].t[:, 128:129]), [psO[0]], [rr])
                k.op("dve", lambda h_: h_.reciprocal(out=rr.t[:, 1:2], in_=psO[1].t[:, 128:129]), [psO[1]], [rr])
                k.tt(rr.t[:, 1:2], rr.t[:, 1:2], lam, ALU.mult, [rr, ls], [rr])
                k.ts(tt_.t[:, :], psO[1].t[:, 0:128], rr.t[:, 1:2], None, ALU.mult, None, [psO[1], rr], [tt_])
                k.stt(oo.t[:, :], psO[0].t[:, 0:128], rr.t[:, 0:1], tt_.t[:, :], ALU.mult, ALU.subtract, [psO[0], rr, tt_], [oo])
                k.tt(junk.t[:, :], oo.t[:, :], oo.t[:, :], ALU.mult, [oo], [junk], eng="pool")
                k.op("dve", lambda h_: h_.reduce_sum(out=rr.t[:, 2:3], in_=junk.t[:, :], axis=mybir.AxisListType.X), [junk], [rr])
                k.act(rr.t[:, 3:4], rr.t[:, 2:3], AF.Ln, [rr], [rr], bias=1e-5, scale=1.0 / 128)
                k.act(rr.t[:, 3:4], rr.t[:, 3:4], AF.Exp, [rr], [rr], scale=-0.5)
                k.stt(on.t[:, :], oo.t[:, :], rr.t[:, 3:4], gsub.t[:, :], ALU.mult, ALU.mult, [oo, rr, gsub], [on])
                k.tr(ptr.t[:, qs, :], on.t[:, :], self.identb.t[:, :], [on, self.identb], [ptr])
                k.copy(y.t[:, qs * 128:(qs + 1) * 128], ptr.t[:, qs, :], [ptr], [y])
            k.store(self.YT[h * 128:(h + 1) * 128, q0:q0 + nqb], y, y.t[:, 0:nqb])

        items = [(h, qb_) for h in range(NHD) for qb_ in qblocks]
        prev = None
        loaded = -1
        for it, item in enumerate(items):
            h = item[0]
            if h != loaded:
                kh, qh, vh = Kh[h % 2], Qh[h % 2], Vh[h % 2]
                k.load(kh, kh.t[:, :], KA[h * 128:(h + 1) * 128, :])
                k.load(qh, qh.t[:, :], QA[h * 128:(h + 1) * 128, :])
                k.load(vh, vh.t[:, :, :], VA.rearrange("(kt p) (h e) -> p kt h e", p=128, e=130)[:, :, h, :])
                loaded = h
            emit_s(item, it % 2)
            if prev is not None:
                emit_pv(prev, (it - 1) % 2)
            prev = item
        emit_pv(prev, (len(items) - 1) % 2)
        k.end()


def make_in_maps(inp, cores):
    inp = {k_: np.asarray(v) for k_, v in inp.items()}
    vecs = build_vecs(inp)
    consts = build_consts()
    lamv = np.ascontiguousarray(np.stack([inp["da_lq1"], inp["da_lk1"], inp["da_lq2"], inp["da_lk2"]], axis=1).astype(np.float32))
    subg = np.ascontiguousarray(inp["da_subln_g"].astype(np.float32))
    shared = {n: np.ascontiguousarray(inp[n], dtype=np.float32) for n in WNAMES}
    maps = []
    for b in cores:
        cc = np.concatenate([inp["c"][b].reshape(8, 128).T, inp["c_ctx"].reshape(8, 128).T], axis=1).astype(np.float32)
        m = {"x": np.ascontiguousarray(inp["x"][b]), "ctx": np.ascontiguousarray(inp["ctx"][b]), "cc": np.ascontiguousarray(cc),
             "vecs": vecs, "consts": consts, "lamv": lamv, "subg": subg}
        m.update(shared)
        maps.append(m)
    return maps


def run_debug(inp, upto, dumps, cores=(0,)):
    p = Prog(upto=upto, dumps=dumps)
    nc = p.build()
    res = run_bass_kernel_spmd(nc, make_in_maps(inp, list(cores)), core_ids=list(range(len(cores))))
    return res.results


def kernel(**inputs):
    p = Prog()
    nc = p.build()
    res = run_bass_kernel_spmd(nc, make_in_maps(inputs, list(range(8))), core_ids=list(range(8)))
    return np.stack([np.asarray(r["out"], dtype=np.float32) for r in res.results], axis=0)
```

```python
import math
from contextlib import ExitStack
import numpy as np
import concourse.bass as bass
import concourse.mybir as mybir
from concourse.bass_utils import run_bass_kernel_spmd

F32 = mybir.dt.float32
BF16 = mybir.dt.bfloat16
AF = mybir.ActivationFunctionType
ALU = mybir.AluOpType

D = 1024
FC = 8
TC = 256
TL = 4096
T = TC + TL
DFF = 4096
DEPTH = 4
NH = 16
CH = 64
NCHUNK = T // CH
DEC = math.exp(-0.5)


class Tok:
    __slots__ = ("sem", "val")

    def __init__(self, sem, val):
        self.sem = sem
        self.val = val


class Buf:
    def __init__(self, t, name):
        self.t = t
        self.name = name
        self.w = None
        self.r = {}
        self.dsem = None
        self.psum = False

    def __getitem__(self, k):
        return self.t[k]


class Eng:
    def __init__(self, h, sem):
        self.h = h
        self.sem = sem
        self.cnt = 0
        self.waited = {}
        self.pending = []


class K:
    def __init__(self, nc):
        self.nc = nc
        self.es = ExitStack()
        self.E = {}
        for nm, h in (("pe", nc.tensor), ("act", nc.scalar), ("dve", nc.vector), ("pool", nc.gpsimd), ("sp", nc.sync)):
            self.E[nm] = Eng(h, self.es.enter_context(nc.semaphore("e_" + nm)))
        self.dsems = [[self.es.enter_context(nc.semaphore("d%d" % i)), 0] for i in range(90)]
        self.dfree = list(range(len(self.dsems)))
        self.phase_es = None
        self.phase_bufs = []
        self.rr = 0
        self.nbuf = 0
        self.plog = []

    def begin(self):
        self.phase_es = ExitStack()
        self.phase_bufs = []

    def sb(self, shape, dt=F32, name=None):
        self.nbuf += 1
        name = "%s_%d" % (name or "sb", self.nbuf)
        b = Buf(self.phase_es.enter_context(self.nc.sbuf_tensor(name, list(shape), dt)), name)
        self.phase_bufs.append(b)
        return b

    def ps(self, shape, dt=F32, name=None):
        self.nbuf += 1
        name = "%s_%d" % (name or "ps", self.nbuf)
        b = Buf(self.phase_es.enter_context(self.nc.psum_tensor(name, list(shape), dt)), name)
        b.psum = True
        self.phase_bufs.append(b)
        return b

    def gsb(self, shape, dt=F32, name=None):
        self.nbuf += 1
        name = "%s_%d" % (name or "g", self.nbuf)
        return Buf(self.es.enter_context(self.nc.sbuf_tensor(name, list(shape), dt)), name)

    def barrier(self):
        toks = [Tok(e.sem, e.cnt) for e in self.E.values() if e.cnt > 0]
        toks += [Tok(s, c) for s, c in self.dsems if c > 0]
        for e in self.E.values():
            assert not e.pending
            for t in toks:
                self._wait(e, t)

    def end(self, name=""):
        self.plog.append((name, {n: e.cnt for n, e in self.E.items()}))
        self.barrier()
        for b in self.phase_bufs:
            if b.dsem is not None:
                self.dfree.append(b.dsem)
        self.phase_es.close()
        self.phase_es = None
        self.phase_bufs = []

    def _wait(self, e, tok, raw=False):
        if tok is None:
            return
        if tok.sem is e.sem and (not raw or e is self.E["pe"]):
            return
        k = id(tok.sem)
        if e.waited.get(k, 0) >= tok.val:
            return
        e.h.wait_ge(tok.sem, tok.val)
        e.waited[k] = tok.val

    def _deps(self, e, reads, writes):
        for r in reads:
            self._wait(e, r.w, raw=True)
            if r.psum:
                for t in r.r.values():
                    self._wait(e, t)
        for w in writes:
            self._wait(e, w.w)
            for t in w.r.values():
                self._wait(e, t)

    @staticmethod
    def _commit(tok, reads, writes):
        for r in reads:
            r.r[id(tok.sem)] = tok
        for w in writes:
            w.w = tok
            w.r = {}

    def op(self, eng, fn, reads, writes, inc=True):
        e = self.E[eng]
        self._deps(e, reads, writes)
        ins = fn(e.h)
        if not inc:
            e.pending.append((reads, writes))
            return
        e.cnt += 1
        ins.then_inc(e.sem, 1)
        tok = Tok(e.sem, e.cnt)
        for (r, w) in e.pending:
            self._commit(tok, r, w)
        e.pending = []
        self._commit(tok, reads, writes)

    def dma(self, q, out, in_, reads, writes, owner):
        e = self.E[q]
        self._deps(e, reads, writes)
        if owner.dsem is None:
            owner.dsem = self.dfree.pop()
        ent = self.dsems[owner.dsem]
        ins = e.h.dma_start(out=out, in_=in_)
        ent[1] += 16
        ins.then_inc(ent[0], 16)
        tok = Tok(ent[0], ent[1])
        self._commit(tok, reads, writes)

    def load(self, buf, dst_ap, src_ap, q="sp"):
        self.dma(q, dst_ap, src_ap, [], [buf], buf)

    def store(self, dst_ap, buf, src_ap, q=None):
        import os
        self.dma(q or os.environ.get("STQ", "sp"), dst_ap, src_ap, [buf], [], buf)

    def mm(self, out, lhsT, rhs, reads, writes, start=True, stop=True, inc=True):
        self.op("pe", lambda h: h.matmul(out, lhsT, rhs, start=start, stop=stop), reads, writes, inc=inc)

    def tr(self, out, in_, ident, reads, writes, inc=True):
        self.op("pe", lambda h: h.transpose(out, in_, ident), reads, writes, inc=inc)

    def act(self, out, in_, func, reads, writes, bias=0.0, scale=1.0, accum=None):
        if accum is None:
            self.op("act", lambda h: h.activation(out=out, in_=in_, func=func, bias=bias, scale=scale), reads, writes)
        else:
            self.op("act", lambda h: h.activation(out=out, in_=in_, func=func, bias=bias, scale=scale, accum_out=accum), reads, writes)

    def veng(self):
        self.rr += 1
        return "pool" if self.rr % 3 == 0 else "dve"

    def tt(self, out, in0, in1, op, reads, writes, eng="dve"):
        self.op(eng, lambda h: h.tensor_tensor(out=out, in0=in0, in1=in1, op=op), reads, writes)

    def ts(self, out, in0, s1, s2, op0, op1, reads, writes, eng="dve"):
        if s2 is None:
            self.op(eng, lambda h: h.tensor_scalar(out=out, in0=in0, scalar1=s1, scalar2=None, op0=op0), reads, writes)
        else:
            self.op(eng, lambda h: h.tensor_scalar(out=out, in0=in0, scalar1=s1, scalar2=s2, op0=op0, op1=op1), reads, writes)

    def stt(self, out, in0, scalar, in1, op0, op1, reads, writes, eng="dve"):
        self.op(eng, lambda h: h.scalar_tensor_tensor(out=out, in0=in0, scalar=scalar, in1=in1, op0=op0, op1=op1), reads, writes)

    def copy(self, out, in_, reads, writes, eng="dve"):
        self.op(eng, lambda h: h.tensor_copy(out, in_), reads, writes)

    def memset(self, out, val, writes, eng="dve"):
        self.op(eng, lambda h: h.memset(out, val), [], writes)


def fm(ap, c0, n):
    return ap[:, c0:c0 + n].rearrange("(c p) t -> p c t", p=128)


def bc(ap, shape):
    return ap.to_broadcast(list(shape))


class VecLayout:
    def __init__(self):
        self.cols = {}
        self.arrs = []
        self.n = 0

    def add(self, name, v):
        v = np.asarray(v, np.float32).reshape(-1)
        assert v.size % 128 == 0
        c = v.size // 128
        self.arrs.append(v.reshape(c, 128).T)
        self.cols[name] = (self.n, c)
        self.n += c

    def build(self):
        return np.ascontiguousarray(np.concatenate(self.arrs, axis=1))


def vec_names():
    names = []
    for i in range(DEPTH):
        names += [("ng%d_0" % i, 8), ("ng%d_1" % i, 8), ("adab%d" % i, 48)]
    names += [("fing", 8)]
    for j in range(2):
        names += [("mix%d_%d" % (j, m), 8) for m in range(6)]
        names += [("w0_%d_%d" % (j, d), 8) for d in range(2)]
        names += [("a0_%d" % j, 8), ("kk_%d" % j, 8), ("ka_%d" % j, 8), ("rk_%d" % j, 8), ("lng_%d" % j, 8), ("lnb_%d" % j, 8)]
    names += [("v0", 8)]
    off = {}
    n = 0
    for nm, c in names:
        off[nm] = (n, c)
        n += c
    return off, n


VOFF, NV = vec_names()


def build_vecs(inp):
    vl = VecLayout()
    for i in range(DEPTH):
        vl.add("ng%d_0" % i, inp["norm_g"][i, 0])
        vl.add("ng%d_1" % i, inp["norm_g"][i, 1])
        vl.add("adab%d" % i, inp["ada_b"][i])
    vl.add("fing", inp["final_g"])
    for j in range(2):
        for m in range(6):
            vl.add("mix%d_%d" % (j, m), inp["rw_mix"][j, m])
        for d in range(2):
            vl.add("w0_%d_%d" % (j, d), inp["rw_w0"][j, d])
        vl.add("a0_%d" % j, inp["rw_a0"][j])
        vl.add("kk_%d" % j, inp["rw_kk"][j])
        vl.add("ka_%d" % j, inp["rw_ka"][j])
        vl.add("rk_%d" % j, inp["rw_rk"][j])
        vl.add("lng_%d" % j, inp["rw_ln_g"][j])
        vl.add("lnb_%d" % j, inp["rw_ln_b"][j])
    vl.add("v0", inp["rw_v0"][0])
    assert vl.cols == VOFF, (vl.cols, VOFF)
    return vl.build()


def const_layout():
    off = {}
    n = 0
    for nm, c in (("ident", 128), ("ones", 128), ("blk", 128), ("rmask", 512), ("mf_s", 64), ("mf_i", 64), ("mb_s", 64), ("mb_i", 64),
                  ("perm", 128), ("cos", TL), ("sin", TL)):
        off[nm] = (n, c)
        n += c
    return off, n


COFF, NCON = const_layout()


def build_consts():
    c = np.zeros((128, NCON), np.float32)

    def put(nm, a):
        o, w = COFF[nm]
        c[:a.shape[0], o:o + w] = a

    put("ident", np.eye(128, dtype=np.float32))
    put("ones", np.ones((128, 128), np.float32))
    blk = np.zeros((128, 128), np.float32)
    blk[:64, :64] = 1
    blk[64:, 64:] = 1
    put("blk", blk)
    rm = np.ones((128, 512), np.float32)
    rm[:, ::CH] = 0
    put("rmask", rm)
    s = np.arange(64)[:, None]
    t = np.arange(64)[None, :]
    put("mf_s", (s < t).astype(np.float32))
    put("mf_i", (s <= t).astype(np.float32))
    put("mb_s", (s > t).astype(np.float32))
    put("mb_i", (s >= t).astype(np.float32))
    perm = np.zeros((128, 128), np.float32)
    for m in range(128):
        j = m % 64
        q = j % 32
        partner = m + 16 if q < 16 else m - 16
        perm[partner, m] = 1.0
    put("perm", perm)
    inv = (10000.0 ** (-np.arange(16, dtype=np.float32) / 16)).astype(np.float32)
    pos = np.arange(TL)
    row = (pos // 64).astype(np.float32)
    col = (pos % 64).astype(np.float32)
    ang_r = (row[None, :] * inv[:, None]).astype(np.float32)
    ang_c = (col[None, :] * inv[:, None]).astype(np.float32)
    cos = np.zeros((128, TL), np.float32)
    sin = np.zeros((128, TL), np.float32)
    for m in range(128):
        j = m % 64
        a = ang_r if j < 32 else ang_c
        q = j % 32
        f = q % 16
        cos[m] = np.cos(a[f])
        sin[m] = -np.sin(a[f]) if q < 16 else np.sin(a[f])
    put("cos", cos)
    put("sin", sin)
    return c


WNAMES = ["ada_w", "rw_w_rkv", "rw_w1", "rw_w2", "rw_a1", "rw_a2", "rw_g1", "rw_g2", "rw_w_o", "rw_v1", "rw_v2",
          "da_w_qkv", "da_w_o", "mlp_w1", "mlp_w2"]
WSHAPES = {"ada_w": [4, 1024, 6144], "rw_w_rkv": [2, 3, 1024, 1024], "rw_w1": [2, 2, 1024, 64], "rw_w2": [2, 2, 64, 1024],
           "rw_a1": [2, 1024, 64], "rw_a2": [2, 64, 1024], "rw_g1": [2, 2, 1024, 160], "rw_g2": [2, 2, 160, 1024],
           "rw_w_o": [2, 1024, 1024], "rw_v1": [1, 1024, 32], "rw_v2": [1, 32, 1024], "da_w_qkv": [2, 1024, 3072],
           "da_w_o": [2, 1024, 1024], "mlp_w1": [4, 1024, 4096], "mlp_w2": [4, 4096, 1024]}


class Prog:
    def __init__(self, upto="all", dumps=()):
        self.upto = upto
        self.dumps = set(dumps)
        nc = bass.Bass("TRN2", target_bir_lowering=False)
        self.nc = nc
        self.k = K(nc)

        def din(name, shape):
            return nc.dram_tensor(name, list(shape), F32, kind="ExternalInput").ap()

        self.x = din("x", [TL, D])
        self.ctx = din("ctx", [TC, D])
        self.cc = din("cc", [128, 16])
        self.vecs_d = din("vecs", [128, NV])
        self.consts_d = din("consts", [128, NCON])
        self.lam_d = din("lamv", [2, 4, 64])
        self.subg_d = din("subg", [2, 128])
        self.W = {n: din(n, WSHAPES[n]) for n in WNAMES}
        self.out = nc.dram_tensor("out", [TL, D], F32, kind="ExternalOutput").ap()
        self.scr = {}

    def scratch(self, name, shape, dt=F32):
        if name not in self.scr:
            kind = "ExternalOutput" if name in self.dumps else "Internal"
            self.scr[name] = self.nc.dram_tensor(name, list(shape), dt, kind=kind).ap()
        return self.scr[name]

    def wload(self, dst, dst_ap, src_ap, shape, stage):
        k = self.k
        st = stage[self._stg % len(stage)]
        self._stg += 1
        sl = tuple(slice(0, s) for s in shape)
        k.load(st, st.t[sl], src_ap)
        eng = ("dve", "pool", "act")[self._stg % 3]
        if eng == "act":
            k.act(dst_ap, st.t[sl], AF.Copy, [st], [dst])
        else:
            k.copy(dst_ap, st.t[sl], [st], [dst], eng=eng)

    def wload_mat(self, dst, src, kc, ncols, stage, col0=0, dcol0=0, piece=1024):
        for c in range(kc):
            for n0 in range(0, ncols, piece):
                n = min(piece, ncols - n0)
                self.wload(dst, dst.t[:, c, dcol0 + n0:dcol0 + n0 + n], src[c * 128:(c + 1) * 128, col0 + n0:col0 + n0 + n], (128, n), stage)

    def build(self):
        k = self.k
        nc = self.nc
        self._stg = 0
        self.vecs = k.gsb([128, NV], F32, "vecs")
        self.con = k.gsb([128, NCON - 2 * TL], F32, "con")
        self.mod = k.gsb([128, DEPTH, 2, 48], F32, "mod")
        self.identb = k.gsb([128, 128], BF16, "identb")
        self.onesb = k.gsb([128, 128], BF16, "onesb")
        self.blkb = k.gsb([128, 128], BF16, "blkb")
        k.load(self.vecs, self.vecs.t[:, :], self.vecs_d[:, :])
        k.load(self.con, self.con.t[:, :], self.consts_d[:, 0:NCON - 2 * TL])
        k.copy(self.identb.t[:, :], self.cs("ident"), [self.con], [self.identb])
        k.copy(self.onesb.t[:, :], self.cs("ones"), [self.con], [self.onesb])
        k.copy(self.blkb.t[:, :], self.cs("blk"), [self.con], [self.blkb])
        self.XT = self.scratch("XT", [D, T])
        self.YT = self.scratch("YT", [D, T], BF16)

        self.phase_mod()
        if self.upto == "mod":
            return self.finish()
        self.phase_x0()
        if self.upto == "x0":
            return self.finish()
        for i in range(DEPTH):
            if i % 2 == 0:
                self.phase_r1(i)
                if self.upto == "r1_%d" % i:
                    return self.finish()
                self.phase_r2(i)
                if self.upto == "r2_%d" % i:
                    return self.finish()
                self.phase_r3(i)
                if self.upto == "r3_%d" % i:
                    return self.finish()
            else:
                self.phase_a1(i)
                if self.upto == "a1_%d" % i:
                    return self.finish()
                self.phase_a2(i)
                if self.upto == "a2_%d" % i:
                    return self.finish()
            self.phase_p3(i)
            if self.upto == "p3_%d" % i:
                return self.finish()
        self.phase_final()
        return self.finish()

    def finish(self):
        self.k.barrier()
        self.k.es.close()
        return self.nc

    def cs(self, name, rows=128):
        o, w = COFF[name]
        return self.con.t[0:rows, o:o + w]

    def vc(self, name):
        o, w = VOFF[name]
        return self.vecs.t[:, o:o + w]

    def blocks(self, n=256, ctx=True):
        bl = []
        if ctx:
            for t0 in range(0, TC, n):
                bl.append((t0, min(n, TC - t0), 0, TC))
        for t0 in range(TC, T, n):
            bl.append((t0, n, TC, T))
        return bl

    def phase_mod(self):
        k = self.k
        k.begin()
        cc = k.sb([128, 16], F32, "cc")
        sc = k.sb([128, 8, 2], F32, "sc")
        k.load(cc, cc.t[:, :], self.cc[:, :])
        sg = k.sb([128, 16], F32, "sg")
        k.act(sg.t[:, :], cc.t[:, :], AF.Sigmoid, [cc], [sg])
        k.tt(sg.t[:, :], sg.t[:, :], cc.t[:, :], ALU.mult, [cc, sg], [sg])
        k.copy(sc.t[:, :, 0], sg.t[:, 0:8], [sg], [sc])
        k.copy(sc.t[:, :, 1], sg.t[:, 8:16], [sg], [sc])
        wst = [k.sb([128, 8, 768], F32, "adaw") for _ in range(2)]
        pm = k.ps([128, 48, 2], F32, "pm")
        for i in range(DEPTH):
            for g in range(8):
                w = wst[(i * 8 + g) % 2]
                k.load(w, w.t[:, :, :], self.W["ada_w"][i, :, g * 768:(g + 1) * 768].rearrange("(c p) n -> p c n", p=128))
                for jj in range(6):
                    j = g * 6 + jj
                    for c in range(8):
                        k.mm(pm.t[:, j, :], w.t[:, c, jj * 128:(jj + 1) * 128], sc.t[:, c, :], [w, sc], [pm],
                             start=(c == 0), stop=(c == 7), inc=(c == 7))
            o, wd = VOFF["adab%d" % i]
            for wch in range(2):
                k.tt(self.mod.t[:, i, wch, :], pm.t[:, :, wch], self.vecs.t[:, o:o + wd], ALU.add, [pm, self.vecs], [self.mod])
        if "MODD" in self.dumps:
            k.store(self.scratch("MODD", [128, DEPTH * 2 * 48]), self.mod, self.mod.t[:, :, :, :].rearrange("p a b c -> p (a b c)"))
        k.end()

    def phase_x0(self):
        k = self.k
        k.begin()
        xin = [k.sb([128, D], F32, "xin") for _ in range(2)]
        xo = [k.sb([128, 8, 128], F32, "xo") for _ in range(2)]
        pt = [k.ps([128, 4, 128], F32, "pt") for _ in range(2)]
        ident = self.cs("ident")
        for it in range(T // 128):
            t0 = it * 128
            xi = xin[it % 2]
            src = self.ctx[t0:t0 + 128, :] if t0 < TC else self.x[t0 - TC:t0 - TC + 128, :]
            k.load(xi, xi.t[:, :], src)
            o = xo[it % 2]
            for half in range(2):
                p = pt[half]
                for c4 in range(4):
                    c = half * 4 + c4
                    k.tr(p.t[:, c4, :], xi.t[:, c * 128:(c + 1) * 128], ident, [xi, self.con], [p], inc=(c4 == 3))
                if half == 0:
                    k.copy(o.t[:, 0:4, :], p.t[:, :, :], [p], [o])
                else:
                    k.act(o.t[:, 4:8, :], p.t[:, :, :], AF.Copy, [p], [o])
            k.store(fm(self.XT, t0, 128), o, o.t[:, :, :])
        k.end()

    def norm_mod(self, xt, n, h_out, h_reads, gname, i, wch, which, tmp_sq, ps_ss, rstd, out_dt_buf=None):
        k = self.k
        k.act(tmp_sq.t[:, :, 0:n], xt.t[:, :, 0:n], AF.Square, [xt], [tmp_sq])
        for c in range(8):
            k.mm(ps_ss.t[:, 0:n], self.onesb.t[:, :], tmp_sq.t[:, c, 0:n], [self.onesb, tmp_sq], [ps_ss],
                 start=(c == 0), stop=(c == 7), inc=(c == 7))
        k.act(rstd.t[:, 0:n], ps_ss.t[:, 0:n], AF.Ln, [ps_ss], [rstd], bias=1e-6, scale=1.0 / D)
        k.act(rstd.t[:, 0:n], rstd.t[:, 0:n], AF.Exp, [rstd], [rstd], scale=-0.5)

    def gs_cols(self, i, wch, which, gname, gs, sh):
        k = self.k
        base = 0 if which == 0 else 24
        shv = self.mod.t[:, i, wch, base:base + 8]
        scv = self.mod.t[:, i, wch, base + 8:base + 16]
        k.stt(gs.t[:, wch, :], scv, 1.0, self.vc(gname), ALU.add, ALU.mult, [self.mod, self.vecs], [gs])
        k.copy(sh.t[:, wch, :], shv, [self.mod], [sh])

    def phase_p3(self, i):
        k = self.k
        last = i == DEPTH - 1
        j = i // 2
        n = 256
        k.begin()
        stage = [k.sb([128, 512], F32, "stg") for _ in range(2)]
        wo = k.sb([128, 8, D], BF16, "wo")
        w1g = [k.sb([128, 8, DFF // 4], BF16, "w1") for _ in range(4)]
        w2g = [k.sb([128, 8, D], BF16, "w2") for _ in range(4)]
        wo_src = self.W["rw_w_o"][j] if i % 2 == 0 else self.W["da_w_o"][j]
        self.wload_mat(wo, wo_src, 8, D, stage, piece=512)
        for g_ in range(4):
            self.wload_mat(w1g[g_], self.W["mlp_w1"][i], 8, DFF // 4, stage, col0=g_ * (DFF // 4), piece=512)
        for g_ in range(4):
            self.wload_mat(w2g[g_], self.W["mlp_w2"][i][g_ * 1024:(g_ + 1) * 1024, :], 8, D, stage, piece=512)
        gs = k.sb([128, 2, 8], F32, "gs")
        sh = k.sb([128, 2, 8], F32, "sh")
        for wch in range(2):
            self.gs_cols(i, wch, 1, "ng%d_1" % i, gs, sh)
        xts = [k.sb([128, 8, n], F32, "xt") for _ in range(1)]
        ys = [k.sb([128, 8, n], BF16, "y") for _ in range(1)]
        xn = k.sb([128, 8, n], F32, "xn")
        hb = k.sb([128, 8, n], BF16, "hb")
        hid = k.sb([128, 32, n], BF16, "hid")
        rl = [k.sb([128, n], F32, "rl") for _ in range(2)]
        rstd = k.sb([128, n], F32, "rstd")
        pss = [k.ps([128, 512], F32, "pp") for _ in range(6)]
        ps_ss = k.ps([128, 512], F32, "pss")
        pi = 0
        for bi, (t0, nb, s0, s1) in enumerate(self.blocks(n, ctx=not last)):
            wch = 1 if t0 < TC else 0
            xt = xts[0]
            y = ys[0]
            k.load(xt, xt.t[:, :, :], fm(self.XT, t0, n))
            k.load(y, y.t[:, :, :], fm(self.YT, t0, n))
            g1 = self.mod.t[:, i, wch, 16:24]
            g2 = self.mod.t[:, i, wch, 40:48]
            for oc in range(8):
                p = pss[pi % 6]
                pi += 1
                for c in range(8):
                    k.mm(p.t[:, 0:n], wo.t[:, c, oc * 128:(oc + 1) * 128], y.t[:, c, :], [wo, y], [p],
                         start=(c == 0), stop=(c == 7), inc=(c == 7))
                k.stt(xt.t[:, oc, :], p.t[:, 0:n], g1[:, oc:oc + 1], xt.t[:, oc, :], ALU.mult, ALU.add, [p, self.mod, xt], [xt])
            self.norm_mod(xt, n, None, None, None, i, wch, 1, hb, ps_ss, rstd)
            k.tt(xn.t[:, :, :], xt.t[:, :, :], bc(rstd.t[:, 0:n].unsqueeze(1), [128, 8, n]), ALU.mult, [xt, rstd], [xn], eng="pool")
            for c in range(8):
                k.act(hb.t[:, c, :], xn.t[:, c, :], AF.Identity, [xn, gs, sh], [hb], bias=sh.t[:, wch, c:c + 1], scale=gs.t[:, wch, c:c + 1])
            for hc in range(32):
                p = pss[pi % 6]
                pi += 1
                for c in range(8):
                    w1 = w1g[hc // 8]
                    k.mm(p.t[:, 0:n], w1.t[:, c, (hc % 8) * 128:(hc % 8 + 1) * 128], hb.t[:, c, :], [w1, hb], [p],
                         start=(c == 0), stop=(c == 7), inc=(c == 7))
                r = rl[hc % 2]
                k.act(r.t[:, :], p.t[:, 0:n], AF.Relu, [p], [r])
                k.tt(hid.t[:, hc, :], r.t[:, :], r.t[:, :], ALU.mult, [r], [hid], eng=("pool" if hc % 2 else "dve"))
            for oc in range(8):
                p = pss[pi % 6]
                pi += 1
                for hc in range(32):
                    w2 = w2g[hc // 8]
                    k.mm(p.t[:, 0:n], w2.t[:, hc % 8, oc * 128:(oc + 1) * 128], hid.t[:, hc, :], [w2, hid], [p],
                         start=(hc == 0), stop=(hc == 31), inc=(hc == 31))
                k.stt(xt.t[:, oc, :], p.t[:, 0:n], g2[:, oc:oc + 1], xt.t[:, oc, :], ALU.mult, ALU.add, [p, self.mod, xt], [xt])
            k.store(fm(self.XT, t0, n), xt, xt.t[:, :, :])
        k.end()

    def phase_final(self):
        k = self.k
        n = 128
        k.begin()
        xts = [k.sb([128, 8, n], F32, "xt") for _ in range(2)]
        sq = k.sb([128, 8, n], BF16, "sq")
        xn = k.sb([128, 8, n], F32, "xn")
        rstd = k.sb([128, n], F32, "rstd")
        ps_ss = k.ps([128, 512], F32, "pss")
        pt = [k.ps([128, 4, 128], F32, "pt") for _ in range(2)]
        outs = [k.sb([128, D], F32, "ot") for _ in range(2)]
        ident = self.cs("ident")
        fg = self.vc("fing")
        for bi in range(TL // n):
            t0 = TC + bi * n
            xt = xts[bi % 2]
            k.load(xt, xt.t[:, :, :], fm(self.XT, t0, n))
            self.norm_mod(xt, n, None, None, None, 0, 0, 0, sq, ps_ss, rstd)
            k.tt(xn.t[:, :, :], xt.t[:, :, :], bc(rstd.t[:, 0:n].unsqueeze(1), [128, 8, n]), ALU.mult, [xt, rstd], [xn])
            for c in range(8):
                k.act(xn.t[:, c, :], xn.t[:, c, :], AF.Copy, [xn, self.vecs], [xn], scale=fg[:, c:c + 1])
            o = outs[bi % 2]
            for half in range(2):
                p = pt[half]
                for c4 in range(4):
                    c = half * 4 + c4
                    k.tr(p.t[:, c4, :], xn.t[:, c, :], ident, [xn, self.con], [p], inc=(c4 == 3))
                if half == 0:
                    k.copy(o.t[:, 0:512], p.t[:, :, :].rearrange("p a b -> p (a b)"), [p], [o])
                else:
                    k.act(o.t[:, 512:1024], p.t[:, :, :].rearrange("p a b -> p (a b)"), AF.Copy, [p], [o])
            k.store(self.out[t0 - TC:t0 - TC + n, :], o, o.t[:, :])
        k.end()

    def phase_r1(self, i):
        k = self.k
        j = i // 2
        n = 128
        nh = n + 2
        vres = j > 0
        S = self.scratch
        RT = [S("RT%d" % d, [D, T], BF16) for d in range(2)]
        KT = [S("KT%d" % d, [D, T], BF16) for d in range(2)]
        BT = [S("BT%d" % d, [D, T], BF16) for d in range(2)]
        AT = [S("AT%d" % d, [D, T], BF16) for d in range(2)]
        GT = [S("GT%d" % d, [D, T], BF16) for d in range(2)]
        GAM = [S("GAM%d" % d, [D, NCHUNK]) for d in range(2)]
        BON = S("BON", [D, T], BF16)
        VTOK = S("VTOK", [T, D], BF16)
        VFIRST = S("VFIRST", [D, T])
        k.begin()
        stage = [k.sb([128, 1024], F32, "stg") for _ in range(2)]
        wr = k.sb([128, 8, D], BF16, "wr")
        wk = k.sb([128, 8, D], BF16, "wk")
        wv = k.sb([128, 8, D], BF16, "wv")
        for wt, si in ((wr, 0), (wk, 1), (wv, 2)):
            self.wload_mat(wt, self.W["rw_w_rkv"][j, si], 8, D, stage)
        w1 = k.sb([128, 8, 128], BF16, "w1")
        w2 = k.sb([64, 2, D], BF16, "w2")
        a1 = k.sb([128, 8, 64], BF16, "a1")
        a2 = k.sb([64, D], BF16, "a2")
        g1 = k.sb([128, 8, 320], BF16, "g1")
        g2a = k.sb([128, 2, D], BF16, "g2a")
        g2b = k.sb([128, 2, D], BF16, "g2b")
        k.memset(g2b.t[:, :, :], 0.0, [g2b], eng="pool")
        for d in range(2):
            self.wload_mat(w1, self.W["rw_w1"][j, d], 8, 64, stage, dcol0=d * 64)
            self.wload(w2, w2.t[0:64, d, :], self.W["rw_w2"][j, d], (64, D), stage)
            self.wload_mat(g1, self.W["rw_g1"][j, d], 8, 160, stage, dcol0=d * 160)
            self.wload(g2a, g2a.t[:, d, :], self.W["rw_g2"][j, d, 0:128, :], (128, D), stage)
            self.wload(g2b, g2b.t[0:32, d, :], self.W["rw_g2"][j, d, 128:160, :], (32, D), stage)
        self.wload_mat(a1, self.W["rw_a1"][j], 8, 64, stage)
        self.wload(a2, a2.t[0:64, :], self.W["rw_a2"][j], (64, D), stage)
        if vres:
            v1 = k.sb([128, 8, 32], BF16, "v1")
            v2 = k.sb([64, D], BF16, "v2")
            k.memset(v2.t[:, :], 0.0, [v2], eng="pool")
            self.wload_mat(v1, self.W["rw_v1"][0], 8, 32, stage)
            self.wload(v2, v2.t[0:32, :], self.W["rw_v2"][0], (32, D), stage)
        gs = k.sb([128, 2, 8], F32, "gs")
        sh = k.sb([128, 2, 8], F32, "sh")
        for wch in range(2):
            self.gs_cols(i, wch, 0, "ng%d_0" % i, gs, sh)
        F = lambda nm, dt=F32: k.sb([128, 8, n], dt, nm)
        xhs = [k.sb([128, 8, nh], F32, "xh") for _ in range(2)]
        sqh = k.sb([128, 8, nh], BF16, "sqh")
        rstd = k.sb([128, nh], F32, "rstd")
        xx = F("xx")
        tA = F("tA")
        xm = [F("xm%d" % m, BF16) for m in range(6)]
        R, Kf, Vf, Af = F("R"), F("Kf"), F("Vf"), F("Af")
        SG = [F("SG0"), F("SG1")]
        kk, inv, bq = F("kk"), F("inv"), F("bq")
        sqk = F("sqk", BF16)
        cum, cum2 = F("cum"), F("cum2")
        E1, E2, E3 = F("E1"), F("E2"), F("E3")
        if vres:
            vfl, vg = inv, bq
        outs = [F("o%d" % q, BF16) for q in range(8)]
        gam = [k.sb([128, 8, 2], F32, "gam") for _ in range(2)]
        vb = F("vb", BF16)
        vtok = [k.sb([128, D], BF16, "vtok") for _ in range(2)]
        th = k.sb([64, 2, n], BF16, "th")
        am = k.sb([64, n], BF16, "am")
        gm = k.sb([128, 2, n], BF16, "gm")
        gm2 = k.sb([128, 2, n], BF16, "gm2")
        k.memset(gm2.t[:, :, :], 0.0, [gm2], eng="pool")
        vm = k.sb([64, n], BF16, "vm")
        k.memset(vm.t[:, :], 0.0, [vm], eng="pool")
        pss = [k.ps([128, 512], F32, "pp") for _ in range(6)]
        ps_ss = k.ps([128, 512], F32, "pss")
        ptb = k.ps([128, D], BF16, "ptb")
        st = {"pi": 0, "oi": 0}

        def nextp():
            st["pi"] += 1
            return pss[st["pi"] % 6]

        def nexto():
            st["oi"] += 1
            return outs[st["oi"] % 8]

        def proj(wt, col0, M, x):
            p = nextp()
            for c in range(8):
                k.mm(p.t[0:M, 0:n], wt.t[:, c, col0:col0 + M], x.t[:, c, :], [wt, x], [p], start=(c == 0), stop=(c == 7), inc=(c == 7))
            return p

        rmask = self.cs("rmask")[:, 0:n]
        kkv = bc(self.vc("kk_%d" % j).unsqueeze(2), [128, 8, n])
        kav = bc(self.vc("ka_%d" % j).unsqueeze(2), [128, 8, n])
        rkv = bc(self.vc("rk_%d" % j).unsqueeze(2), [128, 8, n])
        import os
        STG = int(os.environ.get('R1_STAGE', '99'))
        NBLK = int(os.environ.get('R1_NBLK', '999'))
        blist = self.blocks(n)[:NBLK]

        def issue_load(bj):
            t0_, nb_, s0_, s1_ = blist[bj]
            left_ = t0_ > s0_
            right_ = t0_ + n < s1_
            c0_ = 0 if left_ else 1
            c1_ = nh if right_ else n + 1
            xh_ = xhs[bj % 2]
            k.load(xh_, xh_.t[:, :, c0_:c1_], fm(self.XT, t0_ - 1 + c0_, c1_ - c0_))
            if not left_:
                k.memset(xh_.t[:, :, 0:1], 1.0, [xh_], eng="pool")
            if not right_:
                k.memset(xh_.t[:, :, n + 1:nh], 1.0, [xh_], eng="pool")

        issue_load(0)
        for bi, (t0, nb, s0, s1) in enumerate(blist):
            wch = 1 if t0 < TC else 0
            left = t0 > s0
            right = t0 + n < s1
            xh = xhs[bi % 2]
            if bi + 1 < len(blist):
                issue_load(bi + 1)
            self.norm_mod(xh, nh, None, None, None, i, wch, 0, sqh, ps_ss, rstd)
            k.tt(xh.t[:, :, :], xh.t[:, :, :], bc(rstd.t[:, 0:nh].unsqueeze(1), [128, 8, nh]), ALU.mult, [xh, rstd], [xh])
            for c in range(8):
                k.act(xh.t[:, c, :], xh.t[:, c, :], AF.Identity, [xh, gs, sh], [xh], bias=sh.t[:, wch, c:c + 1], scale=gs.t[:, wch, c:c + 1])
            if not left:
                k.memset(xh.t[:, :, 0:1], 0.0, [xh], eng="act" if False else "dve")
            if not right:
                k.memset(xh.t[:, :, n + 1:nh], 0.0, [xh], eng="dve")
            hc_ = xh.t[:, :, 1:n + 1]
            k.tt(tA.t[:, :, :], xh.t[:, :, 0:n], xh.t[:, :, 2:nh], ALU.add, [xh], [tA])
            k.stt(xx.t[:, :, :], tA.t[:, :, :], 0.5, hc_, ALU.mult, ALU.subtract, [tA, xh], [xx])
            for m in range(6):
                e = "pool" if m % 2 else "dve"
                mixv = bc(self.vc("mix%d_%d" % (j, m)).unsqueeze(2), [128, 8, n])
                k.tt(tA.t[:, :, :], xx.t[:, :, :], mixv, ALU.mult, [xx, self.vecs], [tA], eng=e)
                k.tt(xm[m].t[:, :, :], tA.t[:, :, :], hc_, ALU.add, [tA, xh], [xm[m]], eng=e)
            if STG < 1:
                continue
            for oc in range(8):
                p = proj(wr, oc * 128, 128, xm[0])
                k.act(R.t[:, oc, :], p.t[:, 0:n], AF.Copy, [p], [R])
            for oc in range(8):
                p = proj(wk, oc * 128, 128, xm[2])
                k.copy(Kf.t[:, oc, :], p.t[:, 0:n], [p], [Kf])
            for oc in range(8):
                p = proj(wv, oc * 128, 128, xm[3])
                k.act(Vf.t[:, oc, :], p.t[:, 0:n], AF.Copy, [p], [Vf])
            if STG < 2:
                continue
            for d in range(2):
                p = proj(w1, d * 64, 64, xm[1])
                k.act(th.t[0:64, d, :], p.t[0:64, 0:n], AF.Tanh, [p], [th])
                o, _ = VOFF["w0_%d_%d" % (j, d)]
                for oc in range(8):
                    p = nextp()
                    k.mm(p.t[:, 0:n], w2.t[0:64, d, oc * 128:(oc + 1) * 128], th.t[0:64, d, :], [w2, th], [p])
                    k.act(SG[d].t[:, oc, :], p.t[:, 0:n], AF.Sigmoid, [p, self.vecs], [SG[d]], bias=self.vecs.t[:, o + oc:o + oc + 1])
            p = proj(a1, 0, 64, xm[4])
            k.copy(am.t[0:64, :], p.t[0:64, 0:n], [p], [am])
            o, _ = VOFF["a0_%d" % j]
            for oc in range(8):
                p = nextp()
                k.mm(p.t[:, 0:n], a2.t[0:64, oc * 128:(oc + 1) * 128], am.t[0:64, :], [a2, am], [p])
                k.act(Af.t[:, oc, :], p.t[:, 0:n], AF.Sigmoid, [p, self.vecs], [Af], bias=self.vecs.t[:, o + oc:o + oc + 1])
            if STG < 3:
                continue
            for d in range(2):
                p = proj(g1, d * 160, 128, xm[5])
                k.act(gm.t[:, d, :], p.t[:, 0:n], AF.Sigmoid, [p], [gm])
                p = proj(g1, d * 160 + 128, 32, xm[5])
                k.act(gm2.t[0:32, d, :], p.t[0:32, 0:n], AF.Sigmoid, [p], [gm2])
                go = nexto()
                for oc in range(8):
                    p = nextp()
                    k.mm(p.t[:, 0:n], g2a.t[:, d, oc * 128:(oc + 1) * 128], gm.t[:, d, :], [g2a, gm], [p], start=True, stop=False, inc=False)
                    k.mm(p.t[:, 0:n], g2b.t[:, d, oc * 128:(oc + 1) * 128], gm2.t[:, d, :], [g2b, gm2], [p], start=False, stop=True)
                    if oc % 2:
                        k.copy(go.t[:, oc, :], p.t[:, 0:n], [p], [go])
                    else:
                        k.act(go.t[:, oc, :], p.t[:, 0:n], AF.Copy, [p], [go])
                k.store(fm(GT[d], t0, n), go, go.t[:, :, :])
            if vres:
                p = proj(v1, 0, 32, xm[3])
                k.copy(vm.t[0:32, :], p.t[0:32, 0:n], [p], [vm])
                o, _ = VOFF["v0"]
                for oc in range(8):
                    p = nextp()
                    k.mm(p.t[:, 0:n], v2.t[0:64, oc * 128:(oc + 1) * 128], vm.t[0:64, :], [v2, vm], [p])
                    k.act(vg.t[:, oc, :], p.t[:, 0:n], AF.Sigmoid, [p, self.vecs], [vg], bias=self.vecs.t[:, o + oc:o + oc + 1])
                k.load(vfl, vfl.t[:, :, :], fm(VFIRST, t0, n))
                k.tt(vfl.t[:, :, :], vfl.t[:, :, :], Vf.t[:, :, :], ALU.subtract, [vfl, Vf], [vfl], eng="pool")
                k.tt(vfl.t[:, :, :], vfl.t[:, :, :], vg.t[:, :, :], ALU.mult, [vfl, vg], [vfl], eng="pool")
                k.tt(Vf.t[:, :, :], Vf.t[:, :, :], vfl.t[:, :, :], ALU.add, [vfl, Vf], [Vf], eng="pool")
            else:
                k.store(fm(VFIRST, t0, n), Vf, Vf.t[:, :, :])
            if STG < 4:
                continue
            k.copy(vb.t[:, :, :], Vf.t[:, :, :], [Vf], [vb], eng="pool")
            for c in range(8):
                k.tr(ptb.t[:, c * 128:(c + 1) * 128], vb.t[:, c, :], self.identb.t[:, :], [vb, self.identb], [ptb], inc=(c == 7))
            vt = vtok[bi % 2]
            k.act(vt.t[:, :], ptb.t[:, :], AF.Copy, [ptb], [vt])
            k.store(VTOK[t0:t0 + n, :], vt, vt.t[:, :])
            if STG < 5:
                continue
            k.tt(kk.t[:, :, :], Kf.t[:, :, :], kkv, ALU.mult, [Kf, self.vecs], [kk])
            k.tt(sqk.t[:, :, :], kk.t[:, :, :], kk.t[:, :, :], ALU.mult, [kk], [sqk], eng="pool")
            for c in range(8):
                p = nextp()
                k.mm(p.t[:, 0:n], self.blkb.t[:, :], sqk.t[:, c, :], [self.blkb, sqk], [p])
                k.act(inv.t[:, c, :], p.t[:, 0:n], AF.Ln, [p], [inv], bias=1e-24)
            k.act(inv.t[:, :, :], inv.t[:, :, :], AF.Exp, [inv], [inv], scale=-0.5)
            k.tt(kk.t[:, :, :], kk.t[:, :, :], inv.t[:, :, :], ALU.mult, [kk, inv], [kk])
            k.ts(tA.t[:, :, :], Af.t[:, :, :], -1.0, None, ALU.add, None, [Af], [tA], eng="pool")
            k.tt(tA.t[:, :, :], tA.t[:, :, :], kav, ALU.mult, [tA, self.vecs], [tA], eng="pool")
            k.stt(Kf.t[:, :, :], tA.t[:, :, :], 1.0, Kf.t[:, :, :], ALU.add, ALU.mult, [tA, Kf], [Kf])
            k.tt(bq.t[:, :, :], kk.t[:, :, :], Af.t[:, :, :], ALU.mult, [kk, Af], [bq], eng="pool")
            if STG < 6:
                continue
            k.tt(tA.t[:, :, :], R.t[:, :, :], Kf.t[:, :, :], ALU.mult, [R, Kf], [tA])
            k.tt(sqk.t[:, :, :], tA.t[:, :, :], rkv, ALU.mult, [tA, self.vecs], [sqk])
            bo = nexto()
            for c in range(8):
                p = nextp()
                k.mm(p.t[:, 0:n], self.blkb.t[:, :], sqk.t[:, c, :], [self.blkb, sqk], [p])
                k.tt(bo.t[:, c, :], p.t[:, 0:n], Vf.t[:, c, :], ALU.mult, [p, Vf], [bo])
            k.store(fm(BON, t0, n), bo, bo.t[:, :, :])
            if STG < 7:
                continue
            nq = n // CH
            for d in range(2):
                sg = SG[d]
                for c in range(8):
                    k.op("dve", lambda h, c=c: h.tensor_tensor_scan(out=cum.t[:, c, :], data0=rmask, data1=sg.t[:, c, :], initial=0.0,
                                                                    op0=ALU.mult, op1=ALU.add), [sg, self.con], [cum])
                if d == 0:
                    cd = cum
                else:
                    k.tt(tA.t[:, :, :], sg.t[:, :, :], cum.t[:, :, :], ALU.subtract, [sg, cum], [tA], eng="pool")
                    c4 = cum.t[:, :, :].rearrange("p c (q t) -> p (c q) t", t=CH)
                    k.tt(cum2.t[:, :, :].rearrange("p c (q t) -> p (c q) t", t=CH), tA.t[:, :, :].rearrange("p c (q t) -> p (c q) t", t=CH),
                         bc(c4[:, :, CH - 1:CH], [128, 8 * nq, CH]), ALU.add, [tA, cum], [cum2], eng="pool")
                    cd = cum2
                k.act(E1.t[:, :, :], cd.t[:, :, :], AF.Exp, [cd], [E1], scale=-DEC)
                k.act(E2.t[:, :, :], cd.t[:, :, :], AF.Exp, [cd], [E2], scale=DEC)
                k.tt(tA.t[:, :, :], cd.t[:, :, :], sg.t[:, :, :], ALU.subtract, [cd, sg], [tA], eng="pool")
                k.act(E3.t[:, :, :], tA.t[:, :, :], AF.Exp, [tA], [E3], scale=-DEC)
                g = gam[d]
                e4 = E1.t[:, :, :].rearrange("p c (q t) -> p c q t", t=CH)
                col = CH - 1 if d == 0 else 0
                k.copy(g.t[:, :, :], e4[:, :, :, col], [E1], [g], eng="pool")
                k.store(GAM[d].rearrange("(c p) q -> p c q", p=128)[:, :, t0 // CH:t0 // CH + nq], g, g.t[:, :, :])
                o_ = nexto()
                k.tt(o_.t[:, :, :], R.t[:, :, :], E1.t[:, :, :], ALU.mult, [R, E1], [o_])
                k.store(fm(RT[d], t0, n), o_, o_.t[:, :, :])
                o_ = nexto()
                k.tt(o_.t[:, :, :], Kf.t[:, :, :], E2.t[:, :, :], ALU.mult, [Kf, E2], [o_], eng="pool")
                k.store(fm(KT[d], t0, n), o_, o_.t[:, :, :])
                o_ = nexto()
                k.tt(o_.t[:, :, :], bq.t[:, :, :], E2.t[:, :, :], ALU.mult, [bq, E2], [o_])
                k.store(fm(BT[d], t0, n), o_, o_.t[:, :, :])
                o_ = nexto()
                k.stt(o_.t[:, :, :], kk.t[:, :, :], -1.0, E3.t[:, :, :], ALU.mult, ALU.mult, [kk, E3], [o_])
                k.store(fm(AT[d], t0, n), o_, o_.t[:, :, :])
        k.end()

    def phase_r2(self, i):
        k = self.k
        S = self.scratch
        RT = [S("RT%d" % d, [D, T], BF16) for d in range(2)]
        KT = [S("KT%d" % d, [D, T], BF16) for d in range(2)]
        BT = [S("BT%d" % d, [D, T], BF16) for d in range(2)]
        AT = [S("AT%d" % d, [D, T], BF16) for d in range(2)]
        GAM = [S("GAM%d" % d, [D, NCHUNK]) for d in range(2)]
        VTOK = S("VTOK", [T, D], BF16)
        OT = [S("OT%d" % d, [D, T]) for d in range(2)]
        hv = lambda ap: ap.rearrange("(h k) t -> k h t", k=64)
        k.begin()
        H8 = 8
        co = COFF
        mk = k.sb([128, 2, 128], F32, "mk")
        mt = k.sb([128, 2, 64], F32, "mt")
        idf = k.sb([128, 64], F32, "idf")
        for e in range(2):
            rows = slice(64 * e, 64 * e + 64)
            for d, (ms, mi, mtn) in enumerate((("mf_s", "mf_i", "mb_s"), ("mb_s", "mb_i", "mf_s"))):
                k.load(mk, mk.t[rows, d, 0:64], self.consts_d[0:64, co[ms][0]:co[ms][0] + 64])
                k.load(mk, mk.t[rows, d, 64:128], self.consts_d[0:64, co[mi][0]:co[mi][0] + 64])
                k.load(mt, mt.t[rows, d, :], self.consts_d[0:64, co[mtn][0]:co[mtn][0] + 64])
            k.load(idf, idf.t[rows, :], self.consts_d[0:64, co["ident"][0]:co["ident"][0] + 64])
        gam = [k.sb([128, H8, NCHUNK], F32, "gam") for _ in range(2)]
        for d in range(2):
            for e in range(2):
                k.load(gam[d], gam[d].t[64 * e:64 * e + 64, :, :], GAM[d].rearrange("(h k) q -> k h q", k=64)[:, e * H8:(e + 1) * H8, :])
        pg = [[k.ps([128, H8, 64], F32, "pg") for _ in range(3)] for _ in range(2)]
        ptk = [k.ps([128, H8, 2, 64], BF16, "ptk") for _ in range(2)]
        gi = [0, 0]

        def nextg(e):
            gi[e] += 1
            return pg[e][gi[e] % 3]

        ch = {}
        for d in range(2):
            for e in range(2):
                c = {"d": d, "e": e, "rows": slice(64 * e, 64 * e + 64)}
                A = lambda shape, dt, nm: k.sb([128] + shape, dt, nm)
                c["kb"] = [A([H8, 2, 64], BF16, "kb") for _ in range(2)]
                c["ar"] = [A([H8, 2, 64], BF16, "ar") for _ in range(2)]
                c["v"] = [A([H8, 64], BF16, "v") for _ in range(2)]
                c["MK"] = A([H8, 128], BF16, "MK")
                c["MB"] = A([H8, 64], BF16, "MB")
                c["Ub"] = A([H8, 64], BF16, "Ub")
                c["XT"] = [A([H8, 64], F32, "XT") for _ in range(2)]
                c["X"] = [A([H8, 64], F32, "X") for _ in range(2)]
                c["P"] = A([H8, 64], F32, "P")
                c["Z"] = A([H8, 64], F32, "Z")
                c["kbtok"] = A([H8, 2, 64], BF16, "kbtok")
                c["Hf"] = A([H8, 64], F32, "Hf")
                c["Hb"] = A([H8, 64], BF16, "Hb")
                c["tmp"] = A([H8, 64], F32, "tmp")
                c["ot"] = [A([H8, 64], F32, "ot") for _ in range(2)]
                r_ = c["rows"]
                k.memset(c["Hf"].t[r_, :, :], 0.0, [c["Hf"]])
                k.memset(c["Hb"].t[r_, :, :], 0.0, [c["Hb"]], eng="pool")
                ch[(d, e)] = c
        order = [list(range(NCHUNK)), [3, 2, 1, 0] + list(range(NCHUNK - 1, 3, -1))]
        import os
        NSTEP = int(os.environ.get("R2_NSTEP", str(NCHUNK)))

        def flat(ap):
            return ap.rearrange("p a b -> p (a b)")

        def group(d, fn_mm, n_acc=1):
            ps = [nextg(0), nextg(1)]
            for h in range(H8):
                for e in range(2):
                    fn_mm(ch[(d, e)], ps[e], h, last=(h == H8 - 1))
            return ps

        def issue_loads(step):
            for d in range(2):
                for e in range(2):
                    c = ch[(d, e)]
                    r_ = c["rows"]
                    q = order[d][step]
                    c0 = q * CH
                    h0 = e * H8
                    kb = c["kb"][step % 2]
                    ar = c["ar"][step % 2]
                    v = c["v"][step % 2]
                    k.load(kb, kb.t[r_, :, 0, :], hv(KT[d])[:, h0:h0 + H8, c0:c0 + CH])
                    k.load(kb, kb.t[r_, :, 1, :], hv(BT[d])[:, h0:h0 + H8, c0:c0 + CH])
                    k.load(ar, ar.t[r_, :, 0, :], hv(AT[d])[:, h0:h0 + H8, c0:c0 + CH])
                    k.load(ar, ar.t[r_, :, 1, :], hv(RT[d])[:, h0:h0 + H8, c0:c0 + CH])
                    k.load(v, v.t[r_, :, :], VTOK[c0:c0 + CH, h0 * 64:(h0 + H8) * 64].rearrange("t (h e) -> t h e", e=64))
                    c["curs"][step % 2] = (kb, ar, v, q, c0, h0)

        for c_ in ch.values():
            c_["curs"] = [None, None]
        issue_loads(0)
        for step in range(NSTEP):
            for c_ in ch.values():
                c_["cur"] = c_["curs"][step % 2]
            if step + 1 < NSTEP:
                issue_loads(step + 1)
            for d in range(2):
                for (li, ri, dst) in ((0, 0, "ak"), (0, 1, "rk"), (1, 0, "x"), (1, 1, "rb")):
                    def f(c, p, h, last, li=li, ri=ri):
                        kb, ar = c["cur"][0], c["cur"][1]
                        r_ = c["rows"]
                        k.mm(p.t[r_, h, :], kb.t[r_, h, li, :], ar.t[r_, h, ri, :], [kb, ar], [p], inc=last)
                    ps = group(d, f)
                    for e in range(2):
                        c = ch[(d, e)]
                        r_ = c["rows"]
                        p = ps[e]
                        if dst == "ak":
                            k.tt(c["MK"].t[r_, :, 0:64], p.t[r_, :, :], bc(mk.t[r_, d, 0:64].unsqueeze(1), [64, H8, 64]), ALU.mult, [p, mk], [c["MK"]])
                        elif dst == "rk":
                            k.tt(c["MK"].t[r_, :, 64:128], p.t[r_, :, :], bc(mk.t[r_, d, 64:128].unsqueeze(1), [64, H8, 64]), ALU.mult, [p, mk], [c["MK"]])
                        elif dst == "x":
                            k.tt(c["X"][0].t[r_, :, :], p.t[r_, :, :], bc(mk.t[r_, d, 0:64].unsqueeze(1), [64, H8, 64]), ALU.mult, [p, mk], [c["X"][0]])
                            k.tt(c["P"].t[r_, :, :], c["X"][0].t[r_, :, :], bc(idf.t[r_, :].unsqueeze(1), [64, H8, 64]), ALU.add,
                                 [c["X"][0], idf], [c["P"]], eng="pool")
                        else:
                            k.tt(c["MB"].t[r_, :, :], p.t[r_, :, :], bc(mk.t[r_, d, 64:128].unsqueeze(1), [64, H8, 64]), ALU.mult, [p, mk], [c["MB"]])

                def fxt(c, p, h, last):
                    kb, ar = c["cur"][0], c["cur"][1]
                    r_ = c["rows"]
                    k.mm(p.t[r_, h, :], ar.t[r_, h, 0, :], kb.t[r_, h, 1, :], [kb, ar], [p], inc=last)
                ps = group(d, fxt)
                for e in range(2):
                    c = ch[(d, e)]
                    r_ = c["rows"]
                    k.tt(c["XT"][0].t[r_, :, :], ps[e].t[r_, :, :], bc(mt.t[r_, d, :].unsqueeze(1), [64, H8, 64]), ALU.mult, [ps[e], mt], [c["XT"][0]])
                for h in range(H8):
                    for a_ in range(2):
                        for e in range(2):
                            c = ch[(d, e)]
                            r_ = c["rows"]
                            kb = c["cur"][0]
                            k.tr(ptk[e].t[r_, h, a_, :], kb.t[r_, h, a_, :], self.identb.t[r_, 64 * e:64 * e + 64], [kb, self.identb], [ptk[e]],
                                 inc=(h == H8 - 1 and a_ == 1))
                for e in range(2):
                    c = ch[(d, e)]
                    r_ = c["rows"]
                    k.act(c["kbtok"].t[r_, :, :, :], ptk[e].t[r_, :, :, :], AF.Copy, [ptk[e]], [c["kbtok"]])
            for lv in range(5):
                for d in range(2):
                    def fa(c, p, h, last, lv=lv):
                        r_ = c["rows"]
                        Xc, XTc = c["X"][lv % 2], c["XT"][lv % 2]
                        k.mm(p.t[r_, h, :], Xc.t[r_, h, :], XTc.t[r_, h, :], [XTc, Xc], [p], inc=last)
                    ps = group(d, fa)
                    for e in range(2):
                        c = ch[(d, e)]
                        r_ = c["rows"]
                        k.act(c["XT"][(lv + 1) % 2].t[r_, :, :], ps[e].t[r_, :, :], AF.Copy, [ps[e]], [c["XT"][(lv + 1) % 2]])
                    if lv < 4:
                        def fb(c, p, h, last, lv=lv):
                            r_ = c["rows"]
                            Xc, XTc = c["X"][lv % 2], c["XT"][lv % 2]
                            k.mm(p.t[r_, h, :], XTc.t[r_, h, :], Xc.t[r_, h, :], [XTc, Xc], [p], inc=last)
                        ps = group(d, fb)
                        for e in range(2):
                            c = ch[(d, e)]
                            r_ = c["rows"]
                            k.copy(c["X"][(lv + 1) % 2].t[r_, :, :], ps[e].t[r_, :, :], [ps[e]], [c["X"][(lv + 1) % 2]])

                    def fc(c, p, h, last, lv=lv):
                        r_ = c["rows"]
                        XTn = c["XT"][(lv + 1) % 2]
                        k.mm(p.t[r_, h, :], XTn.t[r_, h, :], c["P"].t[r_, h, :], [XTn, c["P"]], [p], inc=last)
                    ps = group(d, fc)
                    for e in range(2):
                        c = ch[(d, e)]
                        r_ = c["rows"]
                        k.tt(c["P"].t[r_, :, :], ps[e].t[r_, :, :], c["P"].t[r_, :, :], ALU.add, [ps[e], c["P"]], [c["P"]])
            for d in range(2):
                def fy(c, p, h, last):
                    kb, ar, v = c["cur"][0], c["cur"][1], c["cur"][2]
                    r_ = c["rows"]
                    k.mm(p.t[r_, h, :], ar.t[r_, h, 0, :], c["Hb"].t[r_, h, :], [ar, c["Hb"]], [p], start=True, stop=False, inc=False)
                    k.mm(p.t[r_, h, :], c["MK"].t[r_, h, 0:64], v.t[r_, h, :], [c["MK"], v], [p], start=False, stop=True, inc=last)
                ps = group(d, fy)
                for e in range(2):
                    c = ch[(d, e)]
                    r_ = c["rows"]
                    k.act(c["Z"].t[r_, :, :], ps[e].t[r_, :, :], AF.Copy, [ps[e]], [c["Z"]])
            for d in range(2):
                def fu(c, p, h, last):
                    r_ = c["rows"]
                    k.mm(p.t[r_, h, :], c["P"].t[r_, h, :], c["Z"].t[r_, h, :], [c["P"], c["Z"]], [p], inc=last)
                ps = group(d, fu)
                for e in range(2):
                    c = ch[(d, e)]
                    r_ = c["rows"]
                    k.act(c["Ub"].t[r_, :, :], ps[e].t[r_, :, :], AF.Copy, [ps[e]], [c["Ub"]])
            for d in range(2):
                def fo(c, p, h, last):
                    kb, ar, v = c["cur"][0], c["cur"][1], c["cur"][2]
                    r_ = c["rows"]
                    k.mm(p.t[r_, h, :], c["Hb"].t[r_, h, :], ar.t[r_, h, 1, :], [c["Hb"], ar], [p], start=True, stop=False, inc=False)
                    k.mm(p.t[r_, h, :], c["Ub"].t[r_, h, :], c["MB"].t[r_, h, :], [c["Ub"], c["MB"]], [p], start=False, stop=False, inc=False)
                    k.mm(p.t[r_, h, :], v.t[r_, h, :], c["MK"].t[r_, h, 64:128], [v, c["MK"]], [p], start=False, stop=True, inc=last)
                ps = group(d, fo)
                for e in range(2):
                    c = ch[(d, e)]
                    r_ = c["rows"]
                    kb, ar, v, q, c0, h0 = c["cur"]
                    ot = c["ot"][step % 2]
                    k.act(ot.t[r_, :, :], ps[e].t[r_, :, :], AF.Copy, [ps[e]], [ot])
                    k.store(hv(OT[d])[:, h0:h0 + H8, c0:c0 + CH], ot, ot.t[r_, :, :])

                def fh(c, p, h, last):
                    v = c["cur"][2]
                    r_ = c["rows"]
                    k.mm(p.t[r_, h, :], c["kbtok"].t[r_, h, 1, :], c["Ub"].t[r_, h, :], [c["kbtok"], c["Ub"]], [p], start=True, stop=False, inc=False)
                    k.mm(p.t[r_, h, :], c["kbtok"].t[r_, h, 0, :], v.t[r_, h, :], [c["kbtok"], v], [p], start=False, stop=True, inc=last)
                ps = group(d, fh)
                for e in range(2):
                    c = ch[(d, e)]
                    r_ = c["rows"]
                    q = c["cur"][3]
                    k.tt(c["tmp"].t[r_, :, :], ps[e].t[r_, :, :], c["Hf"].t[r_, :, :], ALU.add, [ps[e], c["Hf"]], [c["tmp"]])
                    k.tt(c["Hf"].t[r_, :, :], c["tmp"].t[r_, :, :], bc(gam[d].t[r_, :, q:q + 1], [64, H8, 64]), ALU.mult,
                         [c["tmp"], gam[d]], [c["Hf"]])
                    k.copy(c["Hb"].t[r_, :, :], c["Hf"].t[r_, :, :], [c["Hf"]], [c["Hb"]], eng="pool")
        k.end()

    def phase_r3(self, i):
        k = self.k
        j = i // 2
        n = 256
        S = self.scratch
        OT = [S("OT%d" % d, [D, T]) for d in range(2)]
        GT = [S("GT%d" % d, [D, T], BF16) for d in range(2)]
        BON = S("BON", [D, T], BF16)
        k.begin()
        F = lambda nm, dt=F32: k.sb([128, 8, n], dt, nm)
        o_in = [[F("o_in") for _ in range(2)] for _ in range(2)]
        g_in = [[F("g_in", BF16) for _ in range(2)] for _ in range(2)]
        b_in = [F("b_in", BF16) for _ in range(2)]
        sqs, Mts, E2ts, dds = [[F(nm) for _ in range(2)] for nm in ("sq", "Mt", "E2t", "dd")]
        yaccs = [F("yacc") for _ in range(2)]
        yo = [F("yo", BF16) for _ in range(2)]
        pss = [k.ps([128, 512], F32, "pp") for _ in range(6)]
        blk = self.cs("blk")
        lng = bc(self.vc("lng_%d" % j).unsqueeze(2), [128, 8, n])
        lnb = bc(self.vc("lnb_%d" % j).unsqueeze(2), [128, 8, n])
        pi = 0
        for bi, (t0, nb, s0, s1) in enumerate(self.blocks(n)):
            bon = b_in[bi % 2]
            k.load(bon, bon.t[:, :, :], fm(BON, t0, n))
            for d in range(2):
                o = o_in[d][bi % 2]
                g = g_in[d][bi % 2]
                sq, Mt, E2t, dd = sqs[d], Mts[d], E2ts[d], dds[d]
                yacc = yaccs[bi % 2]
                k.load(o, o.t[:, :, :], fm(OT[d], t0, n))
                k.load(g, g.t[:, :, :], fm(GT[d], t0, n))
                k.act(sq.t[:, :, :], o.t[:, :, :], AF.Square, [o], [sq])
                for c in range(8):
                    p = pss[pi % 6]
                    pi += 1
                    k.mm(p.t[:, 0:n], blk, o.t[:, c, :], [self.con, o], [p])
                    k.act(Mt.t[:, c, :], p.t[:, 0:n], AF.Copy, [p], [Mt], scale=1.0 / 64)
                    p = pss[pi % 6]
                    pi += 1
                    k.mm(p.t[:, 0:n], blk, sq.t[:, c, :], [self.con, sq], [p])
                    k.copy(E2t.t[:, c, :], p.t[:, 0:n], [p], [E2t])
                k.tt(sq.t[:, :, :], Mt.t[:, :, :], Mt.t[:, :, :], ALU.mult, [Mt], [sq], eng="pool")
                k.stt(E2t.t[:, :, :], E2t.t[:, :, :], 1.0 / 64, sq.t[:, :, :], ALU.mult, ALU.subtract, [E2t, sq], [E2t])
                k.act(E2t.t[:, :, :], E2t.t[:, :, :], AF.Ln, [E2t], [E2t], bias=64e-5)
                k.act(E2t.t[:, :, :], E2t.t[:, :, :], AF.Exp, [E2t], [E2t], scale=-0.5)
                k.tt(dd.t[:, :, :], o.t[:, :, :], Mt.t[:, :, :], ALU.subtract, [o, Mt], [dd])
                k.tt(dd.t[:, :, :], dd.t[:, :, :], E2t.t[:, :, :], ALU.mult, [dd, E2t], [dd])
                k.tt(dd.t[:, :, :], dd.t[:, :, :], lng, ALU.mult, [dd, self.vecs], [dd], eng="pool")
                k.tt(dd.t[:, :, :], dd.t[:, :, :], lnb, ALU.add, [dd, self.vecs], [dd], eng="pool")
                k.tt(dd.t[:, :, :], dd.t[:, :, :], bon.t[:, :, :], ALU.add, [dd, bon], [dd])
                if d == 0:
                    k.tt(yacc.t[:, :, :], dd.t[:, :, :], g.t[:, :, :], ALU.mult, [dd, g], [yacc])
                else:
                    k.tt(dd.t[:, :, :], dd.t[:, :, :], g.t[:, :, :], ALU.mult, [dd, g], [dd])
                    y = yo[bi % 2]
                    k.tt(y.t[:, :, :], dd.t[:, :, :], yacc.t[:, :, :], ALU.add, [dd, yacc], [y], eng="pool")
                    k.store(fm(self.YT, t0, n), y, y.t[:, :, :])
        k.end()

    def phase_a1(self, i):
        k = self.k
        j = i // 2
        last = i == DEPTH - 1
        n = 256
        S = self.scratch
        QA = S("QA", [D, T], BF16)
        KA = S("KA", [D, T], BF16)
        VA = S("VA", [T, 8 * 130], BF16)
        k.begin()
        stage = [k.sb([128, 1024], F32, "stg") for _ in range(2)]
        w = k.sb([128, 8, 3 * D], BF16, "wqkv")
        self.wload_mat(w, self.W["da_w_qkv"][j], 8, 3 * D, stage)
        permb = k.sb([128, 128], BF16, "permb")
        k.copy(permb.t[:, :], self.cs("perm"), [self.con], [permb])
        gs = k.sb([128, 2, 8], F32, "gs")
        sh = k.sb([128, 2, 8], F32, "sh")
        for wch in range(2):
            self.gs_cols(i, wch, 0, "ng%d_0" % i, gs, sh)
        xts = [k.sb([128, 8, n], F32, "xt") for _ in range(2)]
        sqs = [k.sb([128, 8, n], BF16, "sq") for _ in range(2)]
        hbs = [k.sb([128, 8, n], BF16, "hb") for _ in range(2)]
        rstds = [k.sb([128, n], F32, "rstd") for _ in range(2)]
        cs_ = [k.sb([128, 2, n], F32, "cs") for _ in range(2)]
        qb = [k.sb([128, n], BF16, "qb") for _ in range(2)]
        t1 = [k.sb([128, n], F32, "t1") for _ in range(2)]
        t2 = [k.sb([128, n], F32, "t2") for _ in range(2)]
        qo = [k.sb([128, 8, n], BF16, "qo") for _ in range(2)]
        ko = [k.sb([128, 8, n], BF16, "ko") for _ in range(2)]
        vo = [k.sb([128, 8, 130], BF16, "vo") for _ in range(2)]
        for v_ in vo:
            k.memset(v_.t[:, :, 128:130], 1.0, [v_])
        pss = [k.ps([128, 512], F32, "pp") for _ in range(4)]
        pr = [k.ps([128, 512], F32, "pr") for _ in range(2)]
        ps_ss = k.ps([128, 512], F32, "pss")
        pi = 0
        co, _ = COFF["cos"]
        so, _ = COFF["sin"]
        vi = 0
        import os
        STG = int(os.environ.get('A1_STAGE', '99'))
        NBLK = int(os.environ.get('A1_NBLK', '999'))
        for bi, (t0, nb, s0, s1) in enumerate(self.blocks(n)[:NBLK]):
            wch = 1 if t0 < TC else 0
            islat = t0 >= TC
            xt = xts[bi % 2]
            sq, hb, rstd = sqs[bi % 2], hbs[bi % 2], rstds[bi % 2]
            k.load(xt, xt.t[:, :, :], fm(self.XT, t0, n))
            if islat:
                cs = cs_[bi % 2]
                k.load(cs, cs.t[:, 0, :], self.consts_d[:, co + t0 - TC:co + t0 - TC + n])
                k.load(cs, cs.t[:, 1, :], self.consts_d[:, so + t0 - TC:so + t0 - TC + n])
            self.norm_mod(xt, n, None, None, None, i, wch, 0, sq, ps_ss, rstd)
            k.tt(xt.t[:, :, :], xt.t[:, :, :], bc(rstd.t[:, 0:n].unsqueeze(1), [128, 8, n]), ALU.mult, [xt, rstd], [xt], eng="pool")
            for c in range(8):
                k.act(hb.t[:, c, :], xt.t[:, c, :], AF.Identity, [xt, gs, sh], [hb], bias=sh.t[:, wch, c:c + 1], scale=gs.t[:, wch, c:c + 1])
            if STG < 1:
                continue
            for which, dst_s, dst in ((0, qo, QA), (1, ko, KA)):
                if which == 0 and last and not islat:
                    continue
                ob = dst_s[bi % 2]
                for oc in range(8):
                    p = pss[pi % 4]
                    pi += 1
                    for c in range(8):
                        k.mm(p.t[:, 0:n], w.t[:, c, which * D + oc * 128:which * D + (oc + 1) * 128], hb.t[:, c, :], [w, hb], [p],
                             start=(c == 0), stop=(c == 7), inc=(c == 7))
                    if not islat:
                        k.act(ob.t[:, oc, :], p.t[:, 0:n], AF.Copy, [p], [ob])
                        continue
                    q_ = qb[oc % 2]
                    k.act(q_.t[:, :], p.t[:, 0:n], AF.Copy, [p], [q_])
                    p2 = pr[oc % 2]
                    if os.environ.get("A1_NOPERM") == "1":
                        p2 = p
                    else:
                        k.mm(p2.t[:, 0:n], permb.t[:, :], q_.t[:, :], [permb, q_], [p2])
                    a1_, a2_ = t1[oc % 2], t2[oc % 2]
                    k.tt(a1_.t[:, :], q_.t[:, :], cs.t[:, 0, :], ALU.mult, [q_, cs], [a1_])
                    k.tt(a2_.t[:, :], p2.t[:, 0:n], cs.t[:, 1, :], ALU.mult, [p2, cs], [a2_])
                    k.tt(ob.t[:, oc, :], a1_.t[:, :], a2_.t[:, :], ALU.add, [a1_, a2_], [ob], eng=os.environ.get("A1_ADD", "pool"))
                k.store(fm(dst, t0, n), ob, ob.t[:, :, :])
            if STG < 3:
                continue
            for ts in range(n // 128):
                v_ = vo[vi % 2]
                vi += 1
                for half in range(2):
                    p = pss[pi % 4]
                    pi += 1
                    for c in range(8):
                        k.mm(p.t[:, 0:512], hb.t[:, c, ts * 128:(ts + 1) * 128], w.t[:, c, 2 * D + half * 512:2 * D + (half + 1) * 512], [w, hb], [p],
                             start=(c == 0), stop=(c == 7), inc=(c == 7))
                    if half == 0:
                        k.act(v_.t[:, 0:4, 0:128], p.t[:, 0:512].rearrange("p (h e) -> p h e", e=128), AF.Copy, [p], [v_])
                    else:
                        k.copy(v_.t[:, 4:8, 0:128], p.t[:, 0:512].rearrange("p (h e) -> p h e", e=128), [p], [v_])
                k.store(VA[t0 + ts * 128:t0 + (ts + 1) * 128, :].rearrange("t (h e) -> t h e", e=130), v_, v_.t[:, :, :])
        k.end()

    def phase_a2(self, i):
        k = self.k
        j = i // 2
        last = i == DEPTH - 1
        lambda_init = 0.8 - 0.6 * math.exp(-0.3 * i)
        S = self.scratch
        QA = S("QA", [D, T], BF16)
        KA = S("KA", [D, T], BF16)
        VA = S("VA", [T, 8 * 130], BF16)
        NKT = T // 128
        k.begin()
        lv = k.sb([128, 4, 64], F32, "lv")
        k.load(lv, lv.t[:, :, :].rearrange("p a e -> p (a e)"), self.lam_d[j].rearrange("a e -> (a e)").partition_broadcast(128))
        lt = k.sb([128, 2, 64], F32, "lt")
        ls = k.sb([128, 4], F32, "ls")
        k.tt(lt.t[:, 0, :], lv.t[:, 0, :], lv.t[:, 1, :], ALU.mult, [lv], [lt])
        k.tt(lt.t[:, 1, :], lv.t[:, 2, :], lv.t[:, 3, :], ALU.mult, [lv], [lt])
        k.op("dve", lambda h: h.reduce_sum(out=ls.t[:, 0:2], in_=lt.t[:, :, :], axis=mybir.AxisListType.X), [lt], [ls])
        k.act(ls.t[:, 0:2], ls.t[:, 0:2], AF.Exp, [ls], [ls])
        k.tt(ls.t[:, 2:3], ls.t[:, 0:1], ls.t[:, 1:2], ALU.subtract, [ls], [ls])
        k.ts(ls.t[:, 3:4], ls.t[:, 2:3], float(lambda_init), None, ALU.add, None, [ls], [ls])
        lam = ls.t[:, 3:4]
        gsub = k.sb([128, 128], F32, "gsub")
        k.load(gsub, gsub.t[:, :], self.subg_d[j].partition_broadcast(128))
        k.ts(gsub.t[:, :], gsub.t[:, :], float(1.0 - lambda_init), None, ALU.mult, None, [gsub], [gsub])
        nq = 512
        Kh = [k.sb([128, T], BF16, "Kh") for _ in range(2)]
        Qh = [k.sb([128, T], BF16, "Qh") for _ in range(2)]
        Vh = [k.sb([128, NKT, 130], BF16, "Vh") for _ in range(2)]
        PT = [[k.sb([128, NKT, nq], BF16, "PT") for _ in range(2)] for _ in range(2)]
        psS = [[k.ps([128, 512], F32, "psS") for _ in range(2)] for _ in range(2)]
        psO = [k.ps([128, 512], F32, "psO") for _ in range(2)]
        ptr = k.ps([128, 4, 128], BF16, "ptr")
        rr = k.sb([128, 4], F32, "rr")
        tt_ = k.sb([128, 128], F32, "tt")
        oo = k.sb([128, 128], F32, "oo")
        junk = k.sb([128, 128], F32, "junk")
        on = k.sb([128, 128], BF16, "on")
        yT = [k.sb([128, nq], BF16, "yT") for _ in range(2)]
        qblocks = [] if last else [(0, TC, [0, 1])]
        qblocks += [(TC + b_ * nq, nq, list(range(NKT))) for b_ in range(TL // nq)]
        import os
        NHD = int(os.environ.get("A2_NH", "8"))
        st = {"si": 0, "yi": 0}

        def emit_s(item, pset):
            h, (q0, nqb, kts) = item
            kh, qh = Kh[h % 2], Qh[h % 2]
            for kt in kts:
                st["si"] += 1
                pp_ = [psS[s_][st["si"] % 2] for s_ in range(2)]
                for s_ in range(2):
                    k.mm(pp_[s_].t[:, 0:nqb], kh.t[64 * s_:64 * s_ + 64, kt * 128:(kt + 1) * 128], qh.t[64 * s_:64 * s_ + 64, q0:q0 + nqb], [kh, qh], [pp_[s_]])
                for s_ in range(2):
                    k.act(PT[pset][s_].t[:, kt, 0:nqb], pp_[s_].t[:, 0:nqb], AF.Exp, [pp_[s_]], [PT[pset][s_]], scale=0.125)

        def emit_pv(item, pset):
            h, (q0, nqb, kts) = item
            vh = Vh[h % 2]
            y = yT[st["yi"] % 2]
            st["yi"] += 1
            for qs in range(nqb // 128):
                for s_ in range(2):
                    for ki, kt in enumerate(kts):
                        k.mm(psO[s_].t[:, 0:130], PT[pset][s_].t[:, kt, qs * 128:(qs + 1) * 128], vh.t[:, kt, :], [PT[pset][s_], vh], [psO[s_]],
                             start=(ki == 0), stop=(ki == len(kts) - 1), inc=(ki == len(kts) - 1))
                k.op("dve", lambda h_: h_.reciprocal(out=rr.t[:, 0:1], in_=psO[0].t[:, 128:129]), [psO[0]], [rr])
                k.op("dve", lambda h_: h_.reciprocal(out=rr.t[:, 1:2], in_=psO[1].t[:, 128:129]), [psO[1]], [rr])
                k.tt(rr.t[:, 1:2], rr.t[:, 1:2], lam, ALU.mult, [rr, ls], [rr])
                k.ts(tt_.t[:, :], psO[1].t[:, 0:128], rr.t[:, 1:2], None, ALU.mult, None, [psO[1], rr], [tt_])
                k.stt(oo.t[:, :], psO[0].t[:, 0:128], rr.t[:, 0:1], tt_.t[:, :], ALU.mult, ALU.subtract, [psO[0], rr, tt_], [oo])
                k.tt(junk.t[:, :], oo.t[:, :], oo.t[:, :], ALU.mult, [oo], [junk], eng="pool")
                k.op("dve", lambda h_: h_.reduce_sum(out=rr.t[:, 2:3], in_=junk.t[:, :], axis=mybir.AxisListType.X), [junk], [rr])
                k.act(rr.t[:, 3:4], rr.t[:, 2:3], AF.Ln, [rr], [rr], bias=1e-5, scale=1.0 / 128)
                k.act(rr.t[:, 3:4], rr.t[:, 3:4], AF.Exp, [rr], [rr], scale=-0.5)
                k.stt(on.t[:, :], oo.t[:, :], rr.t[:, 3:4], gsub.t[:, :], ALU.mult, ALU.mult, [oo, rr, gsub], [on])
                k.tr(ptr.t[:, qs, :], on.t[:, :], self.identb.t[:, :], [on, self.identb], [ptr])
                k.copy(y.t[:, qs * 128:(qs + 1) * 128], ptr.t[:, qs, :], [ptr], [y])
            k.store(self.YT[h * 128:(h + 1) * 128, q0:q0 + nqb], y, y.t[:, 0:nqb])

        items = [(h, qb_) for h in range(NHD) for qb_ in qblocks]
        prev = None
        loaded = -1
        for it, item in enumerate(items):
            h = item[0]
            if h != loaded:
                kh, qh, vh = Kh[h % 2], Qh[h % 2], Vh[h % 2]
                k.load(kh, kh.t[:, :], KA[h * 128:(h + 1) * 128, :])
                k.load(qh, qh.t[:, :], QA[h * 128:(h + 1) * 128, :])
                k.load(vh, vh.t[:, :, :], VA.rearrange("(kt p) (h e) -> p kt h e", p=128, e=130)[:, :, h, :])
                loaded = h
            emit_s(item, it % 2)
            if prev is not None:
                emit_pv(prev, (it - 1) % 2)
            prev = item
        emit_pv(prev, (len(items) - 1) % 2)
        k.end()


def make_in_maps(inp, cores):
    inp = {k_: np.asarray(v) for k_, v in inp.items()}
    vecs = build_vecs(inp)
    consts = build_consts()
    lamv = np.ascontiguousarray(np.stack([inp["da_lq1"], inp["da_lk1"], inp["da_lq2"], inp["da_lk2"]], axis=1).astype(np.float32))
    subg = np.ascontiguousarray(inp["da_subln_g"].astype(np.float32))
    shared = {n: np.ascontiguousarray(inp[n], dtype=np.float32) for n in WNAMES}
    maps = []
    for b in cores:
        cc = np.concatenate([inp["c"][b].reshape(8, 128).T, inp["c_ctx"].reshape(8, 128).T], axis=1).astype(np.float32)
        m = {"x": np.ascontiguousarray(inp["x"][b]), "ctx": np.ascontiguousarray(inp["ctx"][b]), "cc": np.ascontiguousarray(cc),
             "vecs": vecs, "consts": consts, "lamv": lamv, "subg": subg}
        m.update(shared)
        maps.append(m)
    return maps


def run_debug(inp, upto, dumps, cores=(0,)):
    p = Prog(upto=upto, dumps=dumps)
    nc = p.build()
    res = run_bass_kernel_spmd(nc, make_in_maps(inp, list(cores)), core_ids=list(range(len(cores))))
    return res.results


def kernel(**inputs):
    p = Prog()
    nc = p.build()
    res = run_bass_kernel_spmd(nc, make_in_maps(inputs, list(range(8))), core_ids=list(range(8)))
    return np.stack([np.asarray(r["out"], dtype=np.float32) for r in res.results], axis=0)
```

```python
import math
from contextlib import ExitStack
import numpy as np
import concourse.bass as bass
import concourse.mybir as mybir
from concourse.bass_utils import run_bass_kernel_spmd

F32 = mybir.dt.float32
BF16 = mybir.dt.bfloat16
AF = mybir.ActivationFunctionType
ALU = mybir.AluOpType

D = 1024
FC = 8
TC = 256
TL = 4096
T = TC + TL
DFF = 4096
DEPTH = 4
NH = 16
CH = 64
NCHUNK = T // CH
DEC = math.exp(-0.5)


class Tok:
    __slots__ = ("sem", "val")

    def __init__(self, sem, val):
        self.sem = sem
        self.val = val


class Buf:
    def __init__(self, t, name):
        self.t = t
        self.name = name
        self.w = None
        self.r = {}
        self.dsem = None
        self.psum = False

    def __getitem__(self, k):
        return self.t[k]


class Eng:
    def __init__(self, h, sem):
        self.h = h
        self.sem = sem
        self.cnt = 0
        self.waited = {}
        self.pending = []


class K:
    def __init__(self, nc):
        self.nc = nc
        self.es = ExitStack()
        self.E = {}
        for nm, h in (("pe", nc.tensor), ("act", nc.scalar), ("dve", nc.vector), ("pool", nc.gpsimd), ("sp", nc.sync)):
            self.E[nm] = Eng(h, self.es.enter_context(nc.semaphore("e_" + nm)))
        self.dsems = [[self.es.enter_context(nc.semaphore("d%d" % i)), 0] for i in range(90)]
        self.dfree = list(range(len(self.dsems)))
        self.phase_es = None
        self.phase_bufs = []
        self.rr = 0
        self.nbuf = 0
        self.plog = []

    def begin(self):
        self.phase_es = ExitStack()
        self.phase_bufs = []

    def sb(self, shape, dt=F32, name=None):
        self.nbuf += 1
        name = "%s_%d" % (name or "sb", self.nbuf)
        b = Buf(self.phase_es.enter_context(self.nc.sbuf_tensor(name, list(shape), dt)), name)
        self.phase_bufs.append(b)
        return b

    def ps(self, shape, dt=F32, name=None):
        self.nbuf += 1
        name = "%s_%d" % (name or "ps", self.nbuf)
        b = Buf(self.phase_es.enter_context(self.nc.psum_tensor(name, list(shape), dt)), name)
        b.psum = True
        self.phase_bufs.append(b)
        return b

    def gsb(self, shape, dt=F32, name=None):
        self.nbuf += 1
        name = "%s_%d" % (name or "g", self.nbuf)
        return Buf(self.es.enter_context(self.nc.sbuf_tensor(name, list(shape), dt)), name)

    def barrier(self):
        toks = [Tok(e.sem, e.cnt) for e in self.E.values() if e.cnt > 0]
        toks += [Tok(s, c) for s, c in self.dsems if c > 0]
        for e in self.E.values():
            assert not e.pending
            for t in toks:
                self._wait(e, t)

    def end(self, name=""):
        self.plog.append((name, {n: e.cnt for n, e in self.E.items()}))
        self.barrier()
        for b in self.phase_bufs:
            if b.dsem is not None:
                self.dfree.append(b.dsem)
        self.phase_es.close()
        self.phase_es = None
        self.phase_bufs = []

    def _wait(self, e, tok, raw=False):
        if tok is None:
            return
        if tok.sem is e.sem and (not raw or e is self.E["pe"]):
            return
        k = id(tok.sem)
        if e.waited.get(k, 0) >= tok.val:
            return
        e.h.wait_ge(tok.sem, tok.val)
        e.waited[k] = tok.val

    def _deps(self, e, reads, writes):
        for r in reads:
            self._wait(e, r.w, raw=True)
            if r.psum:
                for t in r.r.values():
                    self._wait(e, t)
        for w in writes:
            self._wait(e, w.w)
            for t in w.r.values():
                self._wait(e, t)

    @staticmethod
    def _commit(tok, reads, writes):
        for r in reads:
            r.r[id(tok.sem)] = tok
        for w in writes:
            w.w = tok
            w.r = {}

    def op(self, eng, fn, reads, writes, inc=True):
        e = self.E[eng]
        self._deps(e, reads, writes)
        ins = fn(e.h)
        if not inc:
            e.pending.append((reads, writes))
            return
        e.cnt += 1
        ins.then_inc(e.sem, 1)
        tok = Tok(e.sem, e.cnt)
        for (r, w) in e.pending:
            self._commit(tok, r, w)
        e.pending = []
        self._commit(tok, reads, writes)

    def dma(self, q, out, in_, reads, writes, owner):
        e = self.E[q]
        self._deps(e, reads, writes)
        if owner.dsem is None:
            owner.dsem = self.dfree.pop()
        ent = self.dsems[owner.dsem]
        ins = e.h.dma_start(out=out, in_=in_)
        ent[1] += 16
        ins.then_inc(ent[0], 16)
        tok = Tok(ent[0], ent[1])
        self._commit(tok, reads, writes)

    def load(self, buf, dst_ap, src_ap, q="sp"):
        self.dma(q, dst_ap, src_ap, [], [buf], buf)

    def store(self, dst_ap, buf, src_ap, q=None):
        import os
        self.dma(q or os.environ.get("STQ", "sp"), dst_ap, src_ap, [buf], [], buf)

    def mm(self, out, lhsT, rhs, reads, writes, start=True, stop=True, inc=True):
        self.op("pe", lambda h: h.matmul(out, lhsT, rhs, start=start, stop=stop), reads, writes, inc=inc)

    def tr(self, out, in_, ident, reads, writes, inc=True):
        self.op("pe", lambda h: h.transpose(out, in_, ident), reads, writes, inc=inc)

    def act(self, out, in_, func, reads, writes, bias=0.0, scale=1.0, accum=None):
        if accum is None:
            self.op("act", lambda h: h.activation(out=out, in_=in_, func=func, bias=bias, scale=scale), reads, writes)
        else:
            self.op("act", lambda h: h.activation(out=out, in_=in_, func=func, bias=bias, scale=scale, accum_out=accum), reads, writes)

    def veng(self):
        self.rr += 1
        return "pool" if self.rr % 3 == 0 else "dve"

    def tt(self, out, in0, in1, op, reads, writes, eng="dve"):
        self.op(eng, lambda h: h.tensor_tensor(out=out, in0=in0, in1=in1, op=op), reads, writes)

    def ts(self, out, in0, s1, s2, op0, op1, reads, writes, eng="dve"):
        if s2 is None:
            self.op(eng, lambda h: h.tensor_scalar(out=out, in0=in0, scalar1=s1, scalar2=None, op0=op0), reads, writes)
        else:
            self.op(eng, lambda h: h.tensor_scalar(out=out, in0=in0, scalar1=s1, scalar2=s2, op0=op0, op1=op1), reads, writes)

    def stt(self, out, in0, scalar, in1, op0, op1, reads, writes, eng="dve"):
        self.op(eng, lambda h: h.scalar_tensor_tensor(out=out, in0=in0, scalar=scalar, in1=in1, op0=op0, op1=op1), reads, writes)

    def copy(self, out, in_, reads, writes, eng="dve"):
        self.op(eng, lambda h: h.tensor_copy(out, in_), reads, writes)

    def memset(self, out, val, writes, eng="dve"):
        self.op(eng, lambda h: h.memset(out, val), [], writes)


def fm(ap, c0, n):
    return ap[:, c0:c0 + n].rearrange("(c p) t -> p c t", p=128)


def bc(ap, shape):
    return ap.to_broadcast(list(shape))


class VecLayout:
    def __init__(self):
        self.cols = {}
        self.arrs = []
        self.n = 0

    def add(self, name, v):
        v = np.asarray(v, np.float32).reshape(-1)
        assert v.size % 128 == 0
        c = v.size // 128
        self.arrs.append(v.reshape(c, 128).T)
        self.cols[name] = (self.n, c)
        self.n += c

    def build(self):
        return np.ascontiguousarray(np.concatenate(self.arrs, axis=1))


def vec_names():
    names = []
    for i in range(DEPTH):
        names += [("ng%d_0" % i, 8), ("ng%d_1" % i, 8), ("adab%d" % i, 48)]
    names += [("fing", 8)]
    for j in range(2):
        names += [("mix%d_%d" % (j, m), 8) for m in range(6)]
        names += [("w0_%d_%d" % (j, d), 8) for d in range(2)]
        names += [("a0_%d" % j, 8), ("kk_%d" % j, 8), ("ka_%d" % j, 8), ("rk_%d" % j, 8), ("lng_%d" % j, 8), ("lnb_%d" % j, 8)]
    names += [("v0", 8)]
    off = {}
    n = 0
    for nm, c in names:
        off[nm] = (n, c)
        n += c
    return off, n


VOFF, NV = vec_names()


def build_vecs(inp):
    vl = VecLayout()
    for i in range(DEPTH):
        vl.add("ng%d_0" % i, inp["norm_g"][i, 0])
        vl.add("ng%d_1" % i, inp["norm_g"][i, 1])
        vl.add("adab%d" % i, inp["ada_b"][i])
    vl.add("fing", inp["final_g"])
    for j in range(2):
        for m in range(6):
            vl.add("mix%d_%d" % (j, m), inp["rw_mix"][j, m])
        for d in range(2):
            vl.add("w0_%d_%d" % (j, d), inp["rw_w0"][j, d])
        vl.add("a0_%d" % j, inp["rw_a0"][j])
        vl.add("kk_%d" % j, inp["rw_kk"][j])
        vl.add("ka_%d" % j, inp["rw_ka"][j])
        vl.add("rk_%d" % j, inp["rw_rk"][j])
        vl.add("lng_%d" % j, inp["rw_ln_g"][j])
        vl.add("lnb_%d" % j, inp["rw_ln_b"][j])
    vl.add("v0", inp["rw_v0"][0])
    assert vl.cols == VOFF, (vl.cols, VOFF)
    return vl.build()


def const_layout():
    off = {}
    n = 0
    for nm, c in (("ident", 128), ("ones", 128), ("blk", 128), ("rmask", 512), ("mf_s", 64), ("mf_i", 64), ("mb_s", 64), ("mb_i", 64),
                  ("perm", 128), ("cos", TL), ("sin", TL)):
        off[nm] = (n, c)
        n += c
    return off, n


COFF, NCON = const_layout()


def build_consts():
    c = np.zeros((128, NCON), np.float32)

    def put(nm, a):
        o, w = COFF[nm]
        c[:a.shape[0], o:o + w] = a

    put("ident", np.eye(128, dtype=np.float32))
    put("ones", np.ones((128, 128), np.float32))
    blk = np.zeros((128, 128), np.float32)
    blk[:64, :64] = 1
    blk[64:, 64:] = 1
    put("blk", blk)
    rm = np.ones((128, 512), np.float32)
    rm[:, ::CH] = 0
    put("rmask", rm)
    s = np.arange(64)[:, None]
    t = np.arange(64)[None, :]
    put("mf_s", (s < t).astype(np.float32))
    put("mf_i", (s <= t).astype(np.float32))
    put("mb_s", (s > t).astype(np.float32))
    put("mb_i", (s >= t).astype(np.float32))
    perm = np.zeros((128, 128), np.float32)
    for m in range(128):
        j = m % 64
        q = j % 32
        partner = m + 16 if q < 16 else m - 16
        perm[partner, m] = 1.0
    put("perm", perm)
    inv = (10000.0 ** (-np.arange(16, dtype=np.float32) / 16)).astype(np.float32)
    pos = np.arange(TL)
    row = (pos // 64).astype(np.float32)
    col = (pos % 64).astype(np.float32)
    ang_r = (row[None, :] * inv[:, None]).astype(np.float32)
    ang_c = (col[None, :] * inv[:, None]).astype(np.float32)
    cos = np.zeros((128, TL), np.float32)
    sin = np.zeros((128, TL), np.float32)
    for m in range(128):
        j = m % 64
        a = ang_r if j < 32 else ang_c
        q = j % 32
        f = q % 16
        cos[m] = np.cos(a[f])
        sin[m] = -np.sin(a[f]) if q < 16 else np.sin(a[f])
    put("cos", cos)
    put("sin", sin)
    return c


WNAMES = ["ada_w", "rw_w_rkv", "rw_w1", "rw_w2", "rw_a1", "rw_a2", "rw_g1", "rw_g2", "rw_w_o", "rw_v1", "rw_v2",
          "da_w_qkv", "da_w_o", "mlp_w1", "mlp_w2"]
WSHAPES = {"ada_w": [4, 1024, 6144], "rw_w_rkv": [2, 3, 1024, 1024], "rw_w1": [2, 2, 1024, 64], "rw_w2": [2, 2, 64, 1024],
           "rw_a1": [2, 1024, 64], "rw_a2": [2, 64, 1024], "rw_g1": [2, 2, 1024, 160], "rw_g2": [2, 2, 160, 1024],
           "rw_w_o": [2, 1024, 1024], "rw_v1": [1, 1024, 32], "rw_v2": [1, 32, 1024], "da_w_qkv": [2, 1024, 3072],
           "da_w_o": [2, 1024, 1024], "mlp_w1": [4, 1024, 4096], "mlp_w2": [4, 4096, 1024]}


class Prog:
    def __init__(self, upto="all", dumps=()):
        self.upto = upto
        self.dumps = set(dumps)
        nc = bass.Bass("TRN2", target_bir_lowering=False)
        self.nc = nc
        self.k = K(nc)

        def din(name, shape):
            return nc.dram_tensor(name, list(shape), F32, kind="ExternalInput").ap()

        self.x = din("x", [TL, D])
        self.ctx = din("ctx", [TC, D])
        self.cc = din("cc", [128, 16])
        self.vecs_d = din("vecs", [128, NV])
        self.consts_d = din("consts", [128, NCON])
        self.lam_d = din("lamv", [2, 4, 64])
        self.subg_d = din("subg", [2, 128])
        self.W = {n: din(n, WSHAPES[n]) for n in WNAMES}
        self.out = nc.dram_tensor("out", [TL, D], F32, kind="ExternalOutput").ap()
        self.scr = {}

    def scratch(self, name, shape, dt=F32):
        if name not in self.scr:
            kind = "ExternalOutput" if name in self.dumps else "Internal"
            self.scr[name] = self.nc.dram_tensor(name, list(shape), dt, kind=kind).ap()
        return self.scr[name]

    def wload(self, dst, dst_ap, src_ap, shape, stage):
        k = self.k
        st = stage[self._stg % len(stage)]
        self._stg += 1
        sl = tuple(slice(0, s) for s in shape)
        k.load(st, st.t[sl], src_ap)
        eng = ("dve", "pool", "act")[self._stg % 3]
        if eng == "act":
            k.act(dst_ap, st.t[sl], AF.Copy, [st], [dst])
        else:
            k.copy(dst_ap, st.t[sl], [st], [dst], eng=eng)

    def wload_mat(self, dst, src, kc, ncols, stage, col0=0, dcol0=0, piece=1024):
        for c in range(kc):
            for n0 in range(0, ncols, piece):
                n = min(piece, ncols - n0)
                self.wload(dst, dst.t[:, c, dcol0 + n0:dcol0 + n0 + n], src[c * 128:(c + 1) * 128, col0 + n0:col0 + n0 + n], (128, n), stage)

    def build(self):
        k = self.k
        nc = self.nc
        self._stg = 0
        self.vecs = k.gsb([128, NV], F32, "vecs")
        self.con = k.gsb([128, NCON - 2 * TL], F32, "con")
        self.mod = k.gsb([128, DEPTH, 2, 48], F32, "mod")
        self.identb = k.gsb([128, 128], BF16, "identb")
        self.onesb = k.gsb([128, 128], BF16, "onesb")
        self.blkb = k.gsb([128, 128], BF16, "blkb")
        k.load(self.vecs, self.vecs.t[:, :], self.vecs_d[:, :])
        k.load(self.con, self.con.t[:, :], self.consts_d[:, 0:NCON - 2 * TL])
        k.copy(self.identb.t[:, :], self.cs("ident"), [self.con], [self.identb])
        k.copy(self.onesb.t[:, :], self.cs("ones"), [self.con], [self.onesb])
        k.copy(self.blkb.t[:, :], self.cs("blk"), [self.con], [self.blkb])
        self.XT = self.scratch("XT", [D, T])
        self.YT = self.scratch("YT", [D, T], BF16)

        self.phase_mod()
        if self.upto == "mod":
            return self.finish()
        self.phase_x0()
        if self.upto == "x0":
            return self.finish()
        for i in range(DEPTH):
            if i % 2 == 0:
                self.phase_r1(i)
                if self.upto == "r1_%d" % i:
                    return self.finish()
                self.phase_r2(i)
                if self.upto == "r2_%d" % i:
                    return self.finish()
                self.phase_r3(i)
                if self.upto == "r3_%d" % i:
                    return self.finish()
            else:
                self.phase_a1(i)
                if self.upto == "a1_%d" % i:
                    return self.finish()
                self.phase_a2(i)
                if self.upto == "a2_%d" % i:
                    return self.finish()
            self.phase_p3(i)
            if self.upto == "p3_%d" % i:
                return self.finish()
        self.phase_final()
        return self.finish()

    def finish(self):
        self.k.barrier()
        self.k.es.close()
        return self.nc

    def cs(self, name, rows=128):
        o, w = COFF[name]
        return self.con.t[0:rows, o:o + w]

    def vc(self, name):
        o, w = VOFF[name]
        return self.vecs.t[:, o:o + w]

    def blocks(self, n=256, ctx=True):
        bl = []
        if ctx:
            for t0 in range(0, TC, n):
                bl.append((t0, min(n, TC - t0), 0, TC))
        for t0 in range(TC, T, n):
            bl.append((t0, n, TC, T))
        return bl

    def phase_mod(self):
        k = self.k
        k.begin()
        cc = k.sb([128, 16], F32, "cc")
        sc = k.sb([128, 8, 2], F32, "sc")
        k.load(cc, cc.t[:, :], self.cc[:, :])
        sg = k.sb([128, 16], F32, "sg")
        k.act(sg.t[:, :], cc.t[:, :], AF.Sigmoid, [cc], [sg])
        k.tt(sg.t[:, :], sg.t[:, :], cc.t[:, :], ALU.mult, [cc, sg], [sg])
        k.copy(sc.t[:, :, 0], sg.t[:, 0:8], [sg], [sc])
        k.copy(sc.t[:, :, 1], sg.t[:, 8:16], [sg], [sc])
        wst = [k.sb([128, 8, 768], F32, "adaw") for _ in range(2)]
        pm = k.ps([128, 48, 2], F32, "pm")
        for i in range(DEPTH):
            for g in range(8):
                w = wst[(i * 8 + g) % 2]
                k.load(w, w.t[:, :, :], self.W["ada_w"][i, :, g * 768:(g + 1) * 768].rearrange("(c p) n -> p c n", p=128))
                for jj in range(6):
                    j = g * 6 + jj
                    for c in range(8):
                        k.mm(pm.t[:, j, :], w.t[:, c, jj * 128:(jj + 1) * 128], sc.t[:, c, :], [w, sc], [pm],
                             start=(c == 0), stop=(c == 7), inc=(c == 7))
            o, wd = VOFF["adab%d" % i]
            for wch in range(2):
                k.tt(self.mod.t[:, i, wch, :], pm.t[:, :, wch], self.vecs.t[:, o:o + wd], ALU.add, [pm, self.vecs], [self.mod])
        if "MODD" in self.dumps:
            k.store(self.scratch("MODD", [128, DEPTH * 2 * 48]), self.mod, self.mod.t[:, :, :, :].rearrange("p a b c -> p (a b c)"))
        k.end()

    def phase_x0(self):
        k = self.k
        k.begin()
        xin = [k.sb([128, D], F32, "xin") for _ in range(2)]
        xo = [k.sb([128, 8, 128], F32, "xo") for _ in range(2)]
        pt = [k.ps([128, 4, 128], F32, "pt") for _ in range(2)]
        ident = self.cs("ident")
        for it in range(T // 128):
            t0 = it * 128
            xi = xin[it % 2]
            src = self.ctx[t0:t0 + 128, :] if t0 < TC else self.x[t0 - TC:t0 - TC + 128, :]
            k.load(xi, xi.t[:, :], src)
            o = xo[it % 2]
            for half in range(2):
                p = pt[half]
                for c4 in range(4):
                    c = half * 4 + c4
                    k.tr(p.t[:, c4, :], xi.t[:, c * 128:(c + 1) * 128], ident, [xi, self.con], [p], inc=(c4 == 3))
                if half == 0:
                    k.copy(o.t[:, 0:4, :], p.t[:, :, :], [p], [o])
                else:
                    k.act(o.t[:, 4:8, :], p.t[:, :, :], AF.Copy, [p], [o])
            k.store(fm(self.XT, t0, 128), o, o.t[:, :, :])
        k.end()

    def norm_mod(self, xt, n, h_out, h_reads, gname, i, wch, which, tmp_sq, ps_ss, rstd, out_dt_buf=None):
        k = self.k
        k.act(tmp_sq.t[:, :, 0:n], xt.t[:, :, 0:n], AF.Square, [xt], [tmp_sq])
        for c in range(8):
            k.mm(ps_ss.t[:, 0:n], self.onesb.t[:, :], tmp_sq.t[:, c, 0:n], [self.onesb, tmp_sq], [ps_ss],
                 start=(c == 0), stop=(c == 7), inc=(c == 7))
        k.act(rstd.t[:, 0:n], ps_ss.t[:, 0:n], AF.Ln, [ps_ss], [rstd], bias=1e-6, scale=1.0 / D)
        k.act(rstd.t[:, 0:n], rstd.t[:, 0:n], AF.Exp, [rstd], [rstd], scale=-0.5)

    def gs_cols(self, i, wch, which, gname, gs, sh):
        k = self.k
        base = 0 if which == 0 else 24
        shv = self.mod.t[:, i, wch, base:base + 8]
        scv = self.mod.t[:, i, wch, base + 8:base + 16]
        k.stt(gs.t[:, wch, :], scv, 1.0, self.vc(gname), ALU.add, ALU.mult, [self.mod, self.vecs], [gs])
        k.copy(sh.t[:, wch, :], shv, [self.mod], [sh])

    def phase_p3(self, i):
        k = self.k
        last = i == DEPTH - 1
        j = i // 2
        n = 256
        k.begin()
        stage = [k.sb([128, 512], F32, "stg") for _ in range(2)]
        wo = k.sb([128, 8, D], BF16, "wo")
        w1g = [k.sb([128, 8, DFF // 4], BF16, "w1") for _ in range(4)]
        w2g = [k.sb([128, 8, D], BF16, "w2") for _ in range(4)]
        wo_src = self.W["rw_w_o"][j] if i % 2 == 0 else self.W["da_w_o"][j]
        self.wload_mat(wo, wo_src, 8, D, stage, piece=512)
        for g_ in range(4):
            self.wload_mat(w1g[g_], self.W["mlp_w1"][i], 8, DFF // 4, stage, col0=g_ * (DFF // 4), piece=512)
        for g_ in range(4):
            self.wload_mat(w2g[g_], self.W["mlp_w2"][i][g_ * 1024:(g_ + 1) * 1024, :], 8, D, stage, piece=512)
        gs = k.sb([128, 2, 8], F32, "gs")
        sh = k.sb([128, 2, 8], F32, "sh")
        for wch in range(2):
            self.gs_cols(i, wch, 1, "ng%d_1" % i, gs, sh)
        xts = [k.sb([128, 8, n], F32, "xt") for _ in range(1)]
        ys = [k.sb([128, 8, n], BF16, "y") for _ in range(1)]
        xn = k.sb([128, 8, n], F32, "xn")
        hb = k.sb([128, 8, n], BF16, "hb")
        hid = k.sb([128, 32, n], BF16, "hid")
        rl = [k.sb([128, n], F32, "rl") for _ in range(2)]
        rstd = k.sb([128, n], F32, "rstd")
        pss = [k.ps([128, 512], F32, "pp") for _ in range(6)]
        ps_ss = k.ps([128, 512], F32, "pss")
        pi = 0
        for bi, (t0, nb, s0, s1) in enumerate(self.blocks(n, ctx=not last)):
            wch = 1 if t0 < TC else 0
            xt = xts[0]
            y = ys[0]
            k.load(xt, xt.t[:, :, :], fm(self.XT, t0, n))
            k.load(y, y.t[:, :, :], fm(self.YT, t0, n))
            g1 = self.mod.t[:, i, wch, 16:24]
            g2 = self.mod.t[:, i, wch, 40:48]
            for oc in range(8):
                p = pss[pi % 6]
                pi += 1
                for c in range(8):
                    k.mm(p.t[:, 0:n], wo.t[:, c, oc * 128:(oc + 1) * 128], y.t[:, c, :], [wo, y], [p],
                         start=(c == 0), stop=(c == 7), inc=(c == 7))
                k.stt(xt.t[:, oc, :], p.t[:, 0:n], g1[:, oc:oc + 1], xt.t[:, oc, :], ALU.mult, ALU.add, [p, self.mod, xt], [xt])
            self.norm_mod(xt, n, None, None, None, i, wch, 1, hb, ps_ss, rstd)
            k.tt(xn.t[:, :, :], xt.t[:, :, :], bc(rstd.t[:, 0:n].unsqueeze(1), [128, 8, n]), ALU.mult, [xt, rstd], [xn], eng="pool")
            for c in range(8):
                k.act(hb.t[:, c, :], xn.t[:, c, :], AF.Identity, [xn, gs, sh], [hb], bias=sh.t[:, wch, c:c + 1], scale=gs.t[:, wch, c:c + 1])
            for hc in range(32):
                p = pss[pi % 6]
                pi += 1
                for c in range(8):
                    w1 = w1g[hc // 8]
                    k.mm(p.t[:, 0:n], w1.t[:, c, (hc % 8) * 128:(hc % 8 + 1) * 128], hb.t[:, c, :], [w1, hb], [p],
                         start=(c == 0), stop=(c == 7), inc=(c == 7))
                r = rl[hc % 2]
                k.act(r.t[:, :], p.t[:, 0:n], AF.Relu, [p], [r])
                k.tt(hid.t[:, hc, :], r.t[:, :], r.t[:, :], ALU.mult, [r], [hid], eng=("pool" if hc % 2 else "dve"))
            for oc in range(8):
                p = pss[pi % 6]
                pi += 1
                for hc in range(32):
                    w2 = w2g[hc // 8]
                    k.mm(p.t[:, 0:n], w2.t[:, hc % 8, oc * 128:(oc + 1) * 128], hid.t[:, hc, :], [w2, hid], [p],
                         start=(hc == 0), stop=(hc == 31), inc=(hc == 31))
                k.stt(xt.t[:, oc, :], p.t[:, 0:n], g2[:, oc:oc + 1], xt.t[:, oc, :], ALU.mult, ALU.add, [p, self.mod, xt], [xt])
            k.store(fm(self.XT, t0, n), xt, xt.t[:, :, :])
        k.end()

    def phase_final(self):
        k = self.k
        n = 128
        k.begin()
        xts = [k.sb([128, 8, n], F32, "xt") for _ in range(2)]
        sq = k.sb([128, 8, n], BF16, "sq")
        xn = k.sb([128, 8, n], F32, "xn")
        rstd = k.sb([128, n], F32, "rstd")
        ps_ss = k.ps([128, 512], F32, "pss")
        pt = [k.ps([128, 4, 128], F32, "pt") for _ in range(2)]
        outs = [k.sb([128, D], F32, "ot") for _ in range(2)]
        ident = self.cs("ident")
        fg = self.vc("fing")
        k.load(xts[0], xts[0].t[:, :, :], fm(self.XT, TC, n))
        for bi in range(TL // n):
            t0 = TC + bi * n
            xt = xts[bi % 2]
            if bi + 1 < TL // n:
                k.load(xts[(bi + 1) % 2], xts[(bi + 1) % 2].t[:, :, :], fm(self.XT, t0 + n, n))
            self.norm_mod(xt, n, None, None, None, 0, 0, 0, sq, ps_ss, rstd)
            k.tt(xn.t[:, :, :], xt.t[:, :, :], bc(rstd.t[:, 0:n].unsqueeze(1), [128, 8, n]), ALU.mult, [xt, rstd], [xn])
            for c in range(8):
                k.act(xn.t[:, c, :], xn.t[:, c, :], AF.Copy, [xn, self.vecs], [xn], scale=fg[:, c:c + 1])
            o = outs[bi % 2]
            for half in range(2):
                p = pt[half]
                for c4 in range(4):
                    c = half * 4 + c4
                    k.tr(p.t[:, c4, :], xn.t[:, c, :], ident, [xn, self.con], [p], inc=(c4 == 3))
                if half == 0:
                    k.copy(o.t[:, 0:512], p.t[:, :, :].rearrange("p a b -> p (a b)"), [p], [o])
                else:
                    k.act(o.t[:, 512:1024], p.t[:, :, :].rearrange("p a b -> p (a b)"), AF.Copy, [p], [o])
            k.store(self.out[t0 - TC:t0 - TC + n, :], o, o.t[:, :])
        k.end()

    def phase_r1(self, i):
        k = self.k
        j = i // 2
        n = 128
        nh = n + 2
        vres = j > 0
        S = self.scratch
        RT = [S("RT%d" % d, [D, T], BF16) for d in range(2)]
        KT = [S("KT%d" % d, [D, T], BF16) for d in range(2)]
        BT = [S("BT%d" % d, [D, T], BF16) for d in range(2)]
        AT = [S("AT%d" % d, [D, T], BF16) for d in range(2)]
        GT = [S("GT%d" % d, [D, T], BF16) for d in range(2)]
        GAM = [S("GAM%d" % d, [D, NCHUNK]) for d in range(2)]
        BON = S("BON", [D, T], BF16)
        VTOK = S("VTOK", [T, D], BF16)
        VFIRST = S("VFIRST", [D, T])
        k.begin()
        stage = [k.sb([128, 1024], F32, "stg") for _ in range(2)]
        wr = k.sb([128, 8, D], BF16, "wr")
        wk = k.sb([128, 8, D], BF16, "wk")
        wv = k.sb([128, 8, D], BF16, "wv")
        for wt, si in ((wr, 0), (wk, 1), (wv, 2)):
            self.wload_mat(wt, self.W["rw_w_rkv"][j, si], 8, D, stage)
        w1 = k.sb([128, 8, 128], BF16, "w1")
        w2 = k.sb([64, 2, D], BF16, "w2")
        a1 = k.sb([128, 8, 64], BF16, "a1")
        a2 = k.sb([64, D], BF16, "a2")
        g1 = k.sb([128, 8, 320], BF16, "g1")
        g2a = k.sb([128, 2, D], BF16, "g2a")
        g2b = k.sb([128, 2, D], BF16, "g2b")
        k.memset(g2b.t[:, :, :], 0.0, [g2b], eng="pool")
        for d in range(2):
            self.wload_mat(w1, self.W["rw_w1"][j, d], 8, 64, stage, dcol0=d * 64)
            self.wload(w2, w2.t[0:64, d, :], self.W["rw_w2"][j, d], (64, D), stage)
            self.wload_mat(g1, self.W["rw_g1"][j, d], 8, 160, stage, dcol0=d * 160)
            self.wload(g2a, g2a.t[:, d, :], self.W["rw_g2"][j, d, 0:128, :], (128, D), stage)
            self.wload(g2b, g2b.t[0:32, d, :], self.W["rw_g2"][j, d, 128:160, :], (32, D), stage)
        self.wload_mat(a1, self.W["rw_a1"][j], 8, 64, stage)
        self.wload(a2, a2.t[0:64, :], self.W["rw_a2"][j], (64, D), stage)
        if vres:
            v1 = k.sb([128, 8, 32], BF16, "v1")
            v2 = k.sb([64, D], BF16, "v2")
            k.memset(v2.t[:, :], 0.0, [v2], eng="pool")
            self.wload_mat(v1, self.W["rw_v1"][0], 8, 32, stage)
            self.wload(v2, v2.t[0:32, :], self.W["rw_v2"][0], (32, D), stage)
        gs = k.sb([128, 2, 8], F32, "gs")
        sh = k.sb([128, 2, 8], F32, "sh")
        for wch in range(2):
            self.gs_cols(i, wch, 0, "ng%d_0" % i, gs, sh)
        F = lambda nm, dt=F32: k.sb([128, 8, n], dt, nm)
        xhs = [k.sb([128, 8, nh], F32, "xh") for _ in range(2)]
        sqh = k.sb([128, 8, nh], BF16, "sqh")
        rstd = k.sb([128, nh], F32, "rstd")
        xx = F("xx")
        tA = F("tA")
        xm = [F("xm%d" % m, BF16) for m in range(6)]
        R, Kf, Vf, Af = F("R"), F("Kf"), F("Vf"), F("Af")
        SG = [F("SG0"), F("SG1")]
        kk, inv, bq = F("kk"), F("inv"), F("bq")
        sqk = F("sqk", BF16)
        cum, cum2 = F("cum"), F("cum2")
        E1, E2, E3 = F("E1"), F("E2"), F("E3")
        if vres:
            vfl, vg = inv, bq
        outs = [F("o%d" % q, BF16) for q in range(8)]
        gam = [k.sb([128, 8, 2], F32, "gam") for _ in range(2)]
        vb = F("vb", BF16)
        vtok = [k.sb([128, D], BF16, "vtok") for _ in range(2)]
        th = k.sb([64, 2, n], BF16, "th")
        am = k.sb([64, n], BF16, "am")
        gm = k.sb([128, 2, n], BF16, "gm")
        gm2 = k.sb([128, 2, n], BF16, "gm2")
        k.memset(gm2.t[:, :, :], 0.0, [gm2], eng="pool")
        vm = k.sb([64, n], BF16, "vm")
        k.memset(vm.t[:, :], 0.0, [vm], eng="pool")
        pss = [k.ps([128, 512], F32, "pp") for _ in range(6)]
        ps_ss = k.ps([128, 512], F32, "pss")
        ptb = k.ps([128, D], BF16, "ptb")
        st = {"pi": 0, "oi": 0}

        def nextp():
            st["pi"] += 1
            return pss[st["pi"] % 6]

        def nexto():
            st["oi"] += 1
            return outs[st["oi"] % 8]

        def proj(wt, col0, M, x):
            p = nextp()
            for c in range(8):
                k.mm(p.t[0:M, 0:n], wt.t[:, c, col0:col0 + M], x.t[:, c, :], [wt, x], [p], start=(c == 0), stop=(c == 7), inc=(c == 7))
            return p

        rmask = self.cs("rmask")[:, 0:n]
        kkv = bc(self.vc("kk_%d" % j).unsqueeze(2), [128, 8, n])
        kav = bc(self.vc("ka_%d" % j).unsqueeze(2), [128, 8, n])
        rkv = bc(self.vc("rk_%d" % j).unsqueeze(2), [128, 8, n])
        import os
        STG = int(os.environ.get('R1_STAGE', '99'))
        NBLK = int(os.environ.get('R1_NBLK', '999'))
        blist = self.blocks(n)[:NBLK]

        def issue_load(bj):
            t0_, nb_, s0_, s1_ = blist[bj]
            left_ = t0_ > s0_
            right_ = t0_ + n < s1_
            c0_ = 0 if left_ else 1
            c1_ = nh if right_ else n + 1
            xh_ = xhs[bj % 2]
            k.load(xh_, xh_.t[:, :, c0_:c1_], fm(self.XT, t0_ - 1 + c0_, c1_ - c0_))
            if not left_:
                k.memset(xh_.t[:, :, 0:1], 1.0, [xh_], eng="pool")
            if not right_:
                k.memset(xh_.t[:, :, n + 1:nh], 1.0, [xh_], eng="pool")

        issue_load(0)
        for bi, (t0, nb, s0, s1) in enumerate(blist):
            wch = 1 if t0 < TC else 0
            left = t0 > s0
            right = t0 + n < s1
            xh = xhs[bi % 2]
            if bi + 1 < len(blist):
                issue_load(bi + 1)
            self.norm_mod(xh, nh, None, None, None, i, wch, 0, sqh, ps_ss, rstd)
            k.tt(xh.t[:, :, :], xh.t[:, :, :], bc(rstd.t[:, 0:nh].unsqueeze(1), [128, 8, nh]), ALU.mult, [xh, rstd], [xh])
            for c in range(8):
                k.act(xh.t[:, c, :], xh.t[:, c, :], AF.Identity, [xh, gs, sh], [xh], bias=sh.t[:, wch, c:c + 1], scale=gs.t[:, wch, c:c + 1])
            if not left:
                k.memset(xh.t[:, :, 0:1], 0.0, [xh], eng="act" if False else "dve")
            if not right:
                k.memset(xh.t[:, :, n + 1:nh], 0.0, [xh], eng="dve")
            hc_ = xh.t[:, :, 1:n + 1]
            k.tt(tA.t[:, :, :], xh.t[:, :, 0:n], xh.t[:, :, 2:nh], ALU.add, [xh], [tA])
            k.stt(xx.t[:, :, :], tA.t[:, :, :], 0.5, hc_, ALU.mult, ALU.subtract, [tA, xh], [xx])
            for m in range(6):
                e = "pool" if m % 2 else "dve"
                mixv = bc(self.vc("mix%d_%d" % (j, m)).unsqueeze(2), [128, 8, n])
                k.tt(tA.t[:, :, :], xx.t[:, :, :], mixv, ALU.mult, [xx, self.vecs], [tA], eng=e)
                k.tt(xm[m].t[:, :, :], tA.t[:, :, :], hc_, ALU.add, [tA, xh], [xm[m]], eng=e)
            if STG < 1:
                continue
            for oc in range(8):
                p = proj(wr, oc * 128, 128, xm[0])
                k.act(R.t[:, oc, :], p.t[:, 0:n], AF.Copy, [p], [R])
            for oc in range(8):
                p = proj(wk, oc * 128, 128, xm[2])
                k.copy(Kf.t[:, oc, :], p.t[:, 0:n], [p], [Kf])
            for oc in range(8):
                p = proj(wv, oc * 128, 128, xm[3])
                k.act(Vf.t[:, oc, :], p.t[:, 0:n], AF.Copy, [p], [Vf])
            if STG < 2:
                continue
            for d in range(2):
                p = proj(w1, d * 64, 64, xm[1])
                k.act(th.t[0:64, d, :], p.t[0:64, 0:n], AF.Tanh, [p], [th])
                o, _ = VOFF["w0_%d_%d" % (j, d)]
                for oc in range(8):
                    p = nextp()
                    k.mm(p.t[:, 0:n], w2.t[0:64, d, oc * 128:(oc + 1) * 128], th.t[0:64, d, :], [w2, th], [p])
                    k.act(SG[d].t[:, oc, :], p.t[:, 0:n], AF.Sigmoid, [p, self.vecs], [SG[d]], bias=self.vecs.t[:, o + oc:o + oc + 1])
            p = proj(a1, 0, 64, xm[4])
            k.copy(am.t[0:64, :], p.t[0:64, 0:n], [p], [am])
            o, _ = VOFF["a0_%d" % j]
            for oc in range(8):
                p = nextp()
                k.mm(p.t[:, 0:n], a2.t[0:64, oc * 128:(oc + 1) * 128], am.t[0:64, :], [a2, am], [p])
                k.act(Af.t[:, oc, :], p.t[:, 0:n], AF.Sigmoid, [p, self.vecs], [Af], bias=self.vecs.t[:, o + oc:o + oc + 1])
            if STG < 3:
                continue
            for d in range(2):
                p = proj(g1, d * 160, 128, xm[5])
                k.act(gm.t[:, d, :], p.t[:, 0:n], AF.Sigmoid, [p], [gm])
                p = proj(g1, d * 160 + 128, 32, xm[5])
                k.act(gm2.t[0:32, d, :], p.t[0:32, 0:n], AF.Sigmoid, [p], [gm2])
                go = nexto()
                for oc in range(8):
                    p = nextp()
                    k.mm(p.t[:, 0:n], g2a.t[:, d, oc * 128:(oc + 1) * 128], gm.t[:, d, :], [g2a, gm], [p], start=True, stop=False, inc=False)
                    k.mm(p.t[:, 0:n], g2b.t[:, d, oc * 128:(oc + 1) * 128], gm2.t[:, d, :], [g2b, gm2], [p], start=False, stop=True)
                    if oc % 2:
                        k.copy(go.t[:, oc, :], p.t[:, 0:n], [p], [go])
                    else:
                        k.act(go.t[:, oc, :], p.t[:, 0:n], AF.Copy, [p], [go])
                k.store(fm(GT[d], t0, n), go, go.t[:, :, :])
            if vres:
                p = proj(v1, 0, 32, xm[3])
                k.copy(vm.t[0:32, :], p.t[0:32, 0:n], [p], [vm])
                o, _ = VOFF["v0"]
                for oc in range(8):
                    p = nextp()
                    k.mm(p.t[:, 0:n], v2.t[0:64, oc * 128:(oc + 1) * 128], vm.t[0:64, :], [v2, vm], [p])
                    k.act(vg.t[:, oc, :], p.t[:, 0:n], AF.Sigmoid, [p, self.vecs], [vg], bias=self.vecs.t[:, o + oc:o + oc + 1])
                k.load(vfl, vfl.t[:, :, :], fm(VFIRST, t0, n))
                k.tt(vfl.t[:, :, :], vfl.t[:, :, :], Vf.t[:, :, :], ALU.subtract, [vfl, Vf], [vfl], eng="pool")
                k.tt(vfl.t[:, :, :], vfl.t[:, :, :], vg.t[:, :, :], ALU.mult, [vfl, vg], [vfl], eng="pool")
                k.tt(Vf.t[:, :, :], Vf.t[:, :, :], vfl.t[:, :, :], ALU.add, [vfl, Vf], [Vf], eng="pool")
            else:
                k.store(fm(VFIRST, t0, n), Vf, Vf.t[:, :, :])
            if STG < 4:
                continue
            k.copy(vb.t[:, :, :], Vf.t[:, :, :], [Vf], [vb], eng="pool")
            for c in range(8):
                k.tr(ptb.t[:, c * 128:(c + 1) * 128], vb.t[:, c, :], self.identb.t[:, :], [vb, self.identb], [ptb], inc=(c == 7))
            vt = vtok[bi % 2]
            k.act(vt.t[:, :], ptb.t[:, :], AF.Copy, [ptb], [vt])
            k.store(VTOK[t0:t0 + n, :], vt, vt.t[:, :])
            if STG < 5:
                continue
            k.tt(kk.t[:, :, :], Kf.t[:, :, :], kkv, ALU.mult, [Kf, self.vecs], [kk])
            k.tt(sqk.t[:, :, :], kk.t[:, :, :], kk.t[:, :, :], ALU.mult, [kk], [sqk], eng="pool")
            for c in range(8):
                p = nextp()
                k.mm(p.t[:, 0:n], self.blkb.t[:, :], sqk.t[:, c, :], [self.blkb, sqk], [p])
                k.act(inv.t[:, c, :], p.t[:, 0:n], AF.Ln, [p], [inv], bias=1e-24)
            k.act(inv.t[:, :, :], inv.t[:, :, :], AF.Exp, [inv], [inv], scale=-0.5)
            k.tt(kk.t[:, :, :], kk.t[:, :, :], inv.t[:, :, :], ALU.mult, [kk, inv], [kk])
            k.ts(tA.t[:, :, :], Af.t[:, :, :], -1.0, None, ALU.add, None, [Af], [tA], eng="pool")
            k.tt(tA.t[:, :, :], tA.t[:, :, :], kav, ALU.mult, [tA, self.vecs], [tA], eng="pool")
            k.stt(Kf.t[:, :, :], tA.t[:, :, :], 1.0, Kf.t[:, :, :], ALU.add, ALU.mult, [tA, Kf], [Kf])
            k.tt(bq.t[:, :, :], kk.t[:, :, :], Af.t[:, :, :], ALU.mult, [kk, Af], [bq], eng="pool")
            if STG < 6:
                continue
            k.tt(tA.t[:, :, :], R.t[:, :, :], Kf.t[:, :, :], ALU.mult, [R, Kf], [tA])
            k.tt(sqk.t[:, :, :], tA.t[:, :, :], rkv, ALU.mult, [tA, self.vecs], [sqk])
            bo = nexto()
            for c in range(8):
                p = nextp()
                k.mm(p.t[:, 0:n], self.blkb.t[:, :], sqk.t[:, c, :], [self.blkb, sqk], [p])
                k.tt(bo.t[:, c, :], p.t[:, 0:n], Vf.t[:, c, :], ALU.mult, [p, Vf], [bo])
            k.store(fm(BON, t0, n), bo, bo.t[:, :, :])
            if STG < 7:
                continue
            nq = n // CH
            for d in range(2):
                sg = SG[d]
                for c in range(8):
                    k.op("dve", lambda h, c=c: h.tensor_tensor_scan(out=cum.t[:, c, :], data0=rmask, data1=sg.t[:, c, :], initial=0.0,
                                                                    op0=ALU.mult, op1=ALU.add), [sg, self.con], [cum])
                if d == 0:
                    cd = cum
                else:
                    k.tt(tA.t[:, :, :], sg.t[:, :, :], cum.t[:, :, :], ALU.subtract, [sg, cum], [tA], eng="pool")
                    c4 = cum.t[:, :, :].rearrange("p c (q t) -> p (c q) t", t=CH)
                    k.tt(cum2.t[:, :, :].rearrange("p c (q t) -> p (c q) t", t=CH), tA.t[:, :, :].rearrange("p c (q t) -> p (c q) t", t=CH),
                         bc(c4[:, :, CH - 1:CH], [128, 8 * nq, CH]), ALU.add, [tA, cum], [cum2], eng="pool")
                    cd = cum2
                k.act(E1.t[:, :, :], cd.t[:, :, :], AF.Exp, [cd], [E1], scale=-DEC)
                k.act(E2.t[:, :, :], cd.t[:, :, :], AF.Exp, [cd], [E2], scale=DEC)
                k.tt(tA.t[:, :, :], cd.t[:, :, :], sg.t[:, :, :], ALU.subtract, [cd, sg], [tA], eng="pool")
                k.act(E3.t[:, :, :], tA.t[:, :, :], AF.Exp, [tA], [E3], scale=-DEC)
                g = gam[d]
                e4 = E1.t[:, :, :].rearrange("p c (q t) -> p c q t", t=CH)
                col = CH - 1 if d == 0 else 0
                k.copy(g.t[:, :, :], e4[:, :, :, col], [E1], [g], eng="pool")
                k.store(GAM[d].rearrange("(c p) q -> p c q", p=128)[:, :, t0 // CH:t0 // CH + nq], g, g.t[:, :, :])
                o_ = nexto()
                k.tt(o_.t[:, :, :], R.t[:, :, :], E1.t[:, :, :], ALU.mult, [R, E1], [o_])
                k.store(fm(RT[d], t0, n), o_, o_.t[:, :, :])
                o_ = nexto()
                k.tt(o_.t[:, :, :], Kf.t[:, :, :], E2.t[:, :, :], ALU.mult, [Kf, E2], [o_], eng="pool")
                k.store(fm(KT[d], t0, n), o_, o_.t[:, :, :])
                o_ = nexto()
                k.tt(o_.t[:, :, :], bq.t[:, :, :], E2.t[:, :, :], ALU.mult, [bq, E2], [o_])
                k.store(fm(BT[d], t0, n), o_, o_.t[:, :, :])
                o_ = nexto()
                k.stt(o_.t[:, :, :], kk.t[:, :, :], -1.0, E3.t[:, :, :], ALU.mult, ALU.mult, [kk, E3], [o_])
                k.store(fm(AT[d], t0, n), o_, o_.t[:, :, :])
        k.end()

    def phase_r2(self, i):
        k = self.k
        S = self.scratch
        RT = [S("RT%d" % d, [D, T], BF16) for d in range(2)]
        KT = [S("KT%d" % d, [D, T], BF16) for d in range(2)]
        BT = [S("BT%d" % d, [D, T], BF16) for d in range(2)]
        AT = [S("AT%d" % d, [D, T], BF16) for d in range(2)]
        GAM = [S("GAM%d" % d, [D, NCHUNK]) for d in range(2)]
        VTOK = S("VTOK", [T, D], BF16)
        OT = [S("OT%d" % d, [D, T]) for d in range(2)]
        hv = lambda ap: ap.rearrange("(h k) t -> k h t", k=64)
        k.begin()
        H8 = 8
        co = COFF
        mk = k.sb([128, 2, 128], F32, "mk")
        mt = k.sb([128, 2, 64], F32, "mt")
        idf = k.sb([128, 64], F32, "idf")
        for e in range(2):
            rows = slice(64 * e, 64 * e + 64)
            for d, (ms, mi, mtn) in enumerate((("mf_s", "mf_i", "mb_s"), ("mb_s", "mb_i", "mf_s"))):
                k.load(mk, mk.t[rows, d, 0:64], self.consts_d[0:64, co[ms][0]:co[ms][0] + 64])
                k.load(mk, mk.t[rows, d, 64:128], self.consts_d[0:64, co[mi][0]:co[mi][0] + 64])
                k.load(mt, mt.t[rows, d, :], self.consts_d[0:64, co[mtn][0]:co[mtn][0] + 64])
            k.load(idf, idf.t[rows, :], self.consts_d[0:64, co["ident"][0]:co["ident"][0] + 64])
        gam = [k.sb([128, H8, NCHUNK], F32, "gam") for _ in range(2)]
        for d in range(2):
            for e in range(2):
                k.load(gam[d], gam[d].t[64 * e:64 * e + 64, :, :], GAM[d].rearrange("(h k) q -> k h q", k=64)[:, e * H8:(e + 1) * H8, :])
        pg = [[k.ps([128, H8, 64], F32, "pg") for _ in range(3)] for _ in range(2)]
        ptk = [k.ps([128, H8, 2, 64], BF16, "ptk") for _ in range(2)]
        gi = [0, 0]

        def nextg(e):
            gi[e] += 1
            return pg[e][gi[e] % 3]

        ch = {}
        for d in range(2):
            for e in range(2):
                c = {"d": d, "e": e, "rows": slice(64 * e, 64 * e + 64)}
                A = lambda shape, dt, nm: k.sb([128] + shape, dt, nm)
                c["kb"] = [A([H8, 2, 64], BF16, "kb") for _ in range(2)]
                c["ar"] = [A([H8, 2, 64], BF16, "ar") for _ in range(2)]
                c["v"] = [A([H8, 64], BF16, "v") for _ in range(2)]
                c["MK"] = A([H8, 128], BF16, "MK")
                c["MB"] = A([H8, 64], BF16, "MB")
                c["Ub"] = A([H8, 64], BF16, "Ub")
                c["XT"] = [A([H8, 64], F32, "XT") for _ in range(2)]
                c["X"] = [A([H8, 64], F32, "X") for _ in range(2)]
                c["P"] = A([H8, 64], F32, "P")
                c["Z"] = A([H8, 64], F32, "Z")
                c["kbtok"] = A([H8, 2, 64], BF16, "kbtok")
                c["Hf"] = A([H8, 64], F32, "Hf")
                c["Hb"] = A([H8, 64], BF16, "Hb")
                c["tmp"] = A([H8, 64], F32, "tmp")
                c["ot"] = [A([H8, 64], F32, "ot") for _ in range(2)]
                r_ = c["rows"]
                k.memset(c["Hf"].t[r_, :, :], 0.0, [c["Hf"]])
                k.memset(c["Hb"].t[r_, :, :], 0.0, [c["Hb"]], eng="pool")
                ch[(d, e)] = c
        order = [list(range(NCHUNK)), [3, 2, 1, 0] + list(range(NCHUNK - 1, 3, -1))]
        import os
        NSTEP = int(os.environ.get("R2_NSTEP", str(NCHUNK)))

        def flat(ap):
            return ap.rearrange("p a b -> p (a b)")

        def group(d, fn_mm, n_acc=1):
            ps = [nextg(0), nextg(1)]
            for h in range(H8):
                for e in range(2):
                    fn_mm(ch[(d, e)], ps[e], h, last=(h == H8 - 1))
            return ps

        def issue_loads(step):
            for d in range(2):
                for e in range(2):
                    c = ch[(d, e)]
                    r_ = c["rows"]
                    q = order[d][step]
                    c0 = q * CH
                    h0 = e * H8
                    kb = c["kb"][step % 2]
                    ar = c["ar"][step % 2]
                    v = c["v"][step % 2]
                    k.load(kb, kb.t[r_, :, 0, :], hv(KT[d])[:, h0:h0 + H8, c0:c0 + CH])
                    k.load(kb, kb.t[r_, :, 1, :], hv(BT[d])[:, h0:h0 + H8, c0:c0 + CH])
                    k.load(ar, ar.t[r_, :, 0, :], hv(AT[d])[:, h0:h0 + H8, c0:c0 + CH])
                    k.load(ar, ar.t[r_, :, 1, :], hv(RT[d])[:, h0:h0 + H8, c0:c0 + CH])
                    k.load(v, v.t[r_, :, :], VTOK[c0:c0 + CH, h0 * 64:(h0 + H8) * 64].rearrange("t (h e) -> t h e", e=64))
                    c["curs"][step % 2] = (kb, ar, v, q, c0, h0)

        for c_ in ch.values():
            c_["curs"] = [None, None]
        issue_loads(0)
        for step in range(NSTEP):
            for c_ in ch.values():
                c_["cur"] = c_["curs"][step % 2]
            if step + 1 < NSTEP:
                issue_loads(step + 1)
            for d in range(2):
                for (li, ri, dst) in ((0, 0, "ak"), (0, 1, "rk"), (1, 0, "x"), (1, 1, "rb")):
                    def f(c, p, h, last, li=li, ri=ri):
                        kb, ar = c["cur"][0], c["cur"][1]
                        r_ = c["rows"]
                        k.mm(p.t[r_, h, :], kb.t[r_, h, li, :], ar.t[r_, h, ri, :], [kb, ar], [p], inc=last)
                    ps = group(d, f)
                    for e in range(2):
                        c = ch[(d, e)]
                        r_ = c["rows"]
                        p = ps[e]
                        if dst == "ak":
                            k.tt(c["MK"].t[r_, :, 0:64], p.t[r_, :, :], bc(mk.t[r_, d, 0:64].unsqueeze(1), [64, H8, 64]), ALU.mult, [p, mk], [c["MK"]])
                        elif dst == "rk":
                            k.tt(c["MK"].t[r_, :, 64:128], p.t[r_, :, :], bc(mk.t[r_, d, 64:128].unsqueeze(1), [64, H8, 64]), ALU.mult, [p, mk], [c["MK"]])
                        elif dst == "x":
                            k.tt(c["X"][0].t[r_, :, :], p.t[r_, :, :], bc(mk.t[r_, d, 0:64].unsqueeze(1), [64, H8, 64]), ALU.mult, [p, mk], [c["X"][0]])
                            k.tt(c["P"].t[r_, :, :], c["X"][0].t[r_, :, :], bc(idf.t[r_, :].unsqueeze(1), [64, H8, 64]), ALU.add,
                                 [c["X"][0], idf], [c["P"]], eng="pool")
                        else:
                            k.tt(c["MB"].t[r_, :, :], p.t[r_, :, :], bc(mk.t[r_, d, 64:128].unsqueeze(1), [64, H8, 64]), ALU.mult, [p, mk], [c["MB"]])

                def fxt(c, p, h, last):
                    kb, ar = c["cur"][0], c["cur"][1]
                    r_ = c["rows"]
                    k.mm(p.t[r_, h, :], ar.t[r_, h, 0, :], kb.t[r_, h, 1, :], [kb, ar], [p], inc=last)
                ps = group(d, fxt)
                for e in range(2):
                    c = ch[(d, e)]
                    r_ = c["rows"]
                    k.tt(c["XT"][0].t[r_, :, :], ps[e].t[r_, :, :], bc(mt.t[r_, d, :].unsqueeze(1), [64, H8, 64]), ALU.mult, [ps[e], mt], [c["XT"][0]])
                for h in range(H8):
                    for a_ in range(2):
                        for e in range(2):
                            c = ch[(d, e)]
                            r_ = c["rows"]
                            kb = c["cur"][0]
                            k.tr(ptk[e].t[r_, h, a_, :], kb.t[r_, h, a_, :], self.identb.t[r_, 64 * e:64 * e + 64], [kb, self.identb], [ptk[e]],
                                 inc=(h == H8 - 1 and a_ == 1))
                for e in range(2):
                    c = ch[(d, e)]
                    r_ = c["rows"]
                    k.act(c["kbtok"].t[r_, :, :, :], ptk[e].t[r_, :, :, :], AF.Copy, [ptk[e]], [c["kbtok"]])
            for lv in range(5):
                for d in range(2):
                    def fa(c, p, h, last, lv=lv):
                        r_ = c["rows"]
                        Xc, XTc = c["X"][lv % 2], c["XT"][lv % 2]
                        k.mm(p.t[r_, h, :], Xc.t[r_, h, :], XTc.t[r_, h, :], [XTc, Xc], [p], inc=last)
                    ps = group(d, fa)
                    for e in range(2):
                        c = ch[(d, e)]
                        r_ = c["rows"]
                        k.act(c["XT"][(lv + 1) % 2].t[r_, :, :], ps[e].t[r_, :, :], AF.Copy, [ps[e]], [c["XT"][(lv + 1) % 2]])
                    if lv < 4:
                        def fb(c, p, h, last, lv=lv):
                            r_ = c["rows"]
                            Xc, XTc = c["X"][lv % 2], c["XT"][lv % 2]
                            k.mm(p.t[r_, h, :], XTc.t[r_, h, :], Xc.t[r_, h, :], [XTc, Xc], [p], inc=last)
                        ps = group(d, fb)
                        for e in range(2):
                            c = ch[(d, e)]
                            r_ = c["rows"]
                            k.copy(c["X"][(lv + 1) % 2].t[r_, :, :], ps[e].t[r_, :, :], [ps[e]], [c["X"][(lv + 1) % 2]])

                    def fc(c, p, h, last, lv=lv):
                        r_ = c["rows"]
                        XTn = c["XT"][(lv + 1) % 2]
                        k.mm(p.t[r_, h, :], XTn.t[r_, h, :], c["P"].t[r_, h, :], [XTn, c["P"]], [p], inc=last)
                    ps = group(d, fc)
                    for e in range(2):
                        c = ch[(d, e)]
                        r_ = c["rows"]
                        k.tt(c["P"].t[r_, :, :], ps[e].t[r_, :, :], c["P"].t[r_, :, :], ALU.add, [ps[e], c["P"]], [c["P"]])
            for d in range(2):
                def fy(c, p, h, last):
                    kb, ar, v = c["cur"][0], c["cur"][1], c["cur"][2]
                    r_ = c["rows"]
                    k.mm(p.t[r_, h, :], ar.t[r_, h, 0, :], c["Hb"].t[r_, h, :], [ar, c["Hb"]], [p], start=True, stop=False, inc=False)
                    k.mm(p.t[r_, h, :], c["MK"].t[r_, h, 0:64], v.t[r_, h, :], [c["MK"], v], [p], start=False, stop=True, inc=last)
                ps = group(d, fy)
                for e in range(2):
                    c = ch[(d, e)]
                    r_ = c["rows"]
                    k.act(c["Z"].t[r_, :, :], ps[e].t[r_, :, :], AF.Copy, [ps[e]], [c["Z"]])
            for d in range(2):
                def fu(c, p, h, last):
                    r_ = c["rows"]
                    k.mm(p.t[r_, h, :], c["P"].t[r_, h, :], c["Z"].t[r_, h, :], [c["P"], c["Z"]], [p], inc=last)
                ps = group(d, fu)
                for e in range(2):
                    c = ch[(d, e)]
                    r_ = c["rows"]
                    k.act(c["Ub"].t[r_, :, :], ps[e].t[r_, :, :], AF.Copy, [ps[e]], [c["Ub"]])
            for d in range(2):
                def fo(c, p, h, last):
                    kb, ar, v = c["cur"][0], c["cur"][1], c["cur"][2]
                    r_ = c["rows"]
                    k.mm(p.t[r_, h, :], c["Hb"].t[r_, h, :], ar.t[r_, h, 1, :], [c["Hb"], ar], [p], start=True, stop=False, inc=False)
                    k.mm(p.t[r_, h, :], c["Ub"].t[r_, h, :], c["MB"].t[r_, h, :], [c["Ub"], c["MB"]], [p], start=False, stop=False, inc=False)
                    k.mm(p.t[r_, h, :], v.t[r_, h, :], c["MK"].t[r_, h, 64:128], [v, c["MK"]], [p], start=False, stop=True, inc=last)
                ps = group(d, fo)
                for e in range(2):
                    c = ch[(d, e)]
                    r_ = c["rows"]
                    kb, ar, v, q, c0, h0 = c["cur"]
                    ot = c["ot"][step % 2]
                    k.act(ot.t[r_, :, :], ps[e].t[r_, :, :], AF.Copy, [ps[e]], [ot])
                    k.store(hv(OT[d])[:, h0:h0 + H8, c0:c0 + CH], ot, ot.t[r_, :, :])

                def fh(c, p, h, last):
                    v = c["cur"][2]
                    r_ = c["rows"]
                    k.mm(p.t[r_, h, :], c["kbtok"].t[r_, h, 1, :], c["Ub"].t[r_, h, :], [c["kbtok"], c["Ub"]], [p], start=True, stop=False, inc=False)
                    k.mm(p.t[r_, h, :], c["kbtok"].t[r_, h, 0, :], v.t[r_, h, :], [c["kbtok"], v], [p], start=False, stop=True, inc=last)
                ps = group(d, fh)
                for e in range(2):
                    c = ch[(d, e)]
                    r_ = c["rows"]
                    q = c["cur"][3]
                    k.tt(c["tmp"].t[r_, :, :], ps[e].t[r_, :, :], c["Hf"].t[r_, :, :], ALU.add, [ps[e], c["Hf"]], [c["tmp"]])
                    k.tt(c["Hf"].t[r_, :, :], c["tmp"].t[r_, :, :], bc(gam[d].t[r_, :, q:q + 1], [64, H8, 64]), ALU.mult,
                         [c["tmp"], gam[d]], [c["Hf"]])
                    k.copy(c["Hb"].t[r_, :, :], c["Hf"].t[r_, :, :], [c["Hf"]], [c["Hb"]], eng="pool")
        k.end()

    def phase_r3(self, i):
        k = self.k
        j = i // 2
        n = 256
        S = self.scratch
        OT = [S("OT%d" % d, [D, T]) for d in range(2)]
        GT = [S("GT%d" % d, [D, T], BF16) for d in range(2)]
        BON = S("BON", [D, T], BF16)
        k.begin()
        F = lambda nm, dt=F32: k.sb([128, 8, n], dt, nm)
        o_in = [[F("o_in") for _ in range(2)] for _ in range(2)]
        g_in = [[F("g_in", BF16) for _ in range(2)] for _ in range(2)]
        b_in = [F("b_in", BF16) for _ in range(2)]
        sqs, Mts, E2ts, dds = [[F(nm) for _ in range(2)] for nm in ("sq", "Mt", "E2t", "dd")]
        yaccs = [F("yacc") for _ in range(2)]
        yo = [F("yo", BF16) for _ in range(2)]
        pss = [k.ps([128, 512], F32, "pp") for _ in range(6)]
        blk = self.cs("blk")
        lng = bc(self.vc("lng_%d" % j).unsqueeze(2), [128, 8, n])
        lnb = bc(self.vc("lnb_%d" % j).unsqueeze(2), [128, 8, n])
        pi = 0
        blist = self.blocks(n)

        def issue_load(bj):
            t0_ = blist[bj][0]
            k.load(b_in[bj % 2], b_in[bj % 2].t[:, :, :], fm(BON, t0_, n))
            for d_ in range(2):
                k.load(o_in[d_][bj % 2], o_in[d_][bj % 2].t[:, :, :], fm(OT[d_], t0_, n))
                k.load(g_in[d_][bj % 2], g_in[d_][bj % 2].t[:, :, :], fm(GT[d_], t0_, n))

        issue_load(0)
        for bi, (t0, nb, s0, s1) in enumerate(blist):
            bon = b_in[bi % 2]
            if bi + 1 < len(blist):
                issue_load(bi + 1)
            for d in range(2):
                o = o_in[d][bi % 2]
                g = g_in[d][bi % 2]
                sq, Mt, E2t, dd = sqs[d], Mts[d], E2ts[d], dds[d]
                yacc = yaccs[bi % 2]
                k.act(sq.t[:, :, :], o.t[:, :, :], AF.Square, [o], [sq])
                for c in range(8):
                    p = pss[pi % 6]
                    pi += 1
                    k.mm(p.t[:, 0:n], blk, o.t[:, c, :], [self.con, o], [p])
                    k.act(Mt.t[:, c, :], p.t[:, 0:n], AF.Copy, [p], [Mt], scale=1.0 / 64)
                    p = pss[pi % 6]
                    pi += 1
                    k.mm(p.t[:, 0:n], blk, sq.t[:, c, :], [self.con, sq], [p])
                    k.copy(E2t.t[:, c, :], p.t[:, 0:n], [p], [E2t])
                k.tt(sq.t[:, :, :], Mt.t[:, :, :], Mt.t[:, :, :], ALU.mult, [Mt], [sq], eng="pool")
                k.stt(E2t.t[:, :, :], E2t.t[:, :, :], 1.0 / 64, sq.t[:, :, :], ALU.mult, ALU.subtract, [E2t, sq], [E2t])
                k.act(E2t.t[:, :, :], E2t.t[:, :, :], AF.Ln, [E2t], [E2t], bias=64e-5)
                k.act(E2t.t[:, :, :], E2t.t[:, :, :], AF.Exp, [E2t], [E2t], scale=-0.5)
                k.tt(dd.t[:, :, :], o.t[:, :, :], Mt.t[:, :, :], ALU.subtract, [o, Mt], [dd])
                k.tt(dd.t[:, :, :], dd.t[:, :, :], E2t.t[:, :, :], ALU.mult, [dd, E2t], [dd])
                k.tt(dd.t[:, :, :], dd.t[:, :, :], lng, ALU.mult, [dd, self.vecs], [dd], eng="pool")
                k.tt(dd.t[:, :, :], dd.t[:, :, :], lnb, ALU.add, [dd, self.vecs], [dd], eng="pool")
                k.tt(dd.t[:, :, :], dd.t[:, :, :], bon.t[:, :, :], ALU.add, [dd, bon], [dd])
                if d == 0:
                    k.tt(yacc.t[:, :, :], dd.t[:, :, :], g.t[:, :, :], ALU.mult, [dd, g], [yacc])
                else:
                    k.tt(dd.t[:, :, :], dd.t[:, :, :], g.t[:, :, :], ALU.mult, [dd, g], [dd])
                    y = yo[bi % 2]
                    k.tt(y.t[:, :, :], dd.t[:, :, :], yacc.t[:, :, :], ALU.add, [dd, yacc], [y], eng="pool")
                    k.store(fm(self.YT, t0, n), y, y.t[:, :, :])
        k.end()

    def phase_a1(self, i):
        k = self.k
        j = i // 2
        last = i == DEPTH - 1
        n = 256
        S = self.scratch
        QA = S("QA", [D, T], BF16)
        KA = S("KA", [D, T], BF16)
        VA = S("VA", [T, 8 * 130], BF16)
        k.begin()
        stage = [k.sb([128, 1024], F32, "stg") for _ in range(2)]
        w = k.sb([128, 8, 3 * D], BF16, "wqkv")
        self.wload_mat(w, self.W["da_w_qkv"][j], 8, 3 * D, stage)
        permb = k.sb([128, 128], BF16, "permb")
        k.copy(permb.t[:, :], self.cs("perm"), [self.con], [permb])
        gs = k.sb([128, 2, 8], F32, "gs")
        sh = k.sb([128, 2, 8], F32, "sh")
        for wch in range(2):
            self.gs_cols(i, wch, 0, "ng%d_0" % i, gs, sh)
        xts = [k.sb([128, 8, n], F32, "xt") for _ in range(2)]
        sqs = [k.sb([128, 8, n], BF16, "sq") for _ in range(2)]
        hbs = [k.sb([128, 8, n], BF16, "hb") for _ in range(2)]
        rstds = [k.sb([128, n], F32, "rstd") for _ in range(2)]
        cs_ = [k.sb([128, 2, n], F32, "cs") for _ in range(2)]
        qb = [k.sb([128, n], BF16, "qb") for _ in range(2)]
        t1 = [k.sb([128, n], F32, "t1") for _ in range(2)]
        t2 = [k.sb([128, n], F32, "t2") for _ in range(2)]
        qo = [k.sb([128, 8, n], BF16, "qo") for _ in range(2)]
        ko = [k.sb([128, 8, n], BF16, "ko") for _ in range(2)]
        vo = [k.sb([128, 8, 130], BF16, "vo") for _ in range(2)]
        for v_ in vo:
            k.memset(v_.t[:, :, 128:130], 1.0, [v_])
        pss = [k.ps([128, 512], F32, "pp") for _ in range(4)]
        pr = [k.ps([128, 512], F32, "pr") for _ in range(2)]
        ps_ss = k.ps([128, 512], F32, "pss")
        pi = 0
        co, _ = COFF["cos"]
        so, _ = COFF["sin"]
        vi = 0
        import os
        STG = int(os.environ.get('A1_STAGE', '99'))
        NBLK = int(os.environ.get('A1_NBLK', '999'))
        blist = self.blocks(n)[:NBLK]

        def issue_load(bj):
            t0_ = blist[bj][0]
            k.load(xts[bj % 2], xts[bj % 2].t[:, :, :], fm(self.XT, t0_, n))
            if t0_ >= TC:
                cs__ = cs_[bj % 2]
                k.load(cs__, cs__.t[:, 0, :], self.consts_d[:, co + t0_ - TC:co + t0_ - TC + n])
                k.load(cs__, cs__.t[:, 1, :], self.consts_d[:, so + t0_ - TC:so + t0_ - TC + n])

        issue_load(0)
        for bi, (t0, nb, s0, s1) in enumerate(blist):
            wch = 1 if t0 < TC else 0
            islat = t0 >= TC
            xt = xts[bi % 2]
            sq, hb, rstd = sqs[bi % 2], hbs[bi % 2], rstds[bi % 2]
            if bi + 1 < len(blist):
                issue_load(bi + 1)
            if islat:
                cs = cs_[bi % 2]
            self.norm_mod(xt, n, None, None, None, i, wch, 0, sq, ps_ss, rstd)
            k.tt(xt.t[:, :, :], xt.t[:, :, :], bc(rstd.t[:, 0:n].unsqueeze(1), [128, 8, n]), ALU.mult, [xt, rstd], [xt], eng="pool")
            for c in range(8):
                k.act(hb.t[:, c, :], xt.t[:, c, :], AF.Identity, [xt, gs, sh], [hb], bias=sh.t[:, wch, c:c + 1], scale=gs.t[:, wch, c:c + 1])
            if STG < 1:
                continue
            for which, dst_s, dst in ((0, qo, QA), (1, ko, KA)):
                if which == 0 and last and not islat:
                    continue
                ob = dst_s[bi % 2]
                for oc in range(8):
                    p = pss[pi % 4]
                    pi += 1
                    for c in range(8):
                        k.mm(p.t[:, 0:n], w.t[:, c, which * D + oc * 128:which * D + (oc + 1) * 128], hb.t[:, c, :], [w, hb], [p],
                             start=(c == 0), stop=(c == 7), inc=(c == 7))
                    if not islat:
                        k.act(ob.t[:, oc, :], p.t[:, 0:n], AF.Copy, [p], [ob])
                        continue
                    q_ = qb[oc % 2]
                    k.act(q_.t[:, :], p.t[:, 0:n], AF.Copy, [p], [q_])
                    p2 = pr[oc % 2]
                    if os.environ.get("A1_NOPERM") == "1":
                        p2 = p
                    else:
                        k.mm(p2.t[:, 0:n], permb.t[:, :], q_.t[:, :], [permb, q_], [p2])
                    a1_, a2_ = t1[oc % 2], t2[oc % 2]
                    k.tt(a1_.t[:, :], q_.t[:, :], cs.t[:, 0, :], ALU.mult, [q_, cs], [a1_])
                    k.tt(a2_.t[:, :], p2.t[:, 0:n], cs.t[:, 1, :], ALU.mult, [p2, cs], [a2_])
                    k.tt(ob.t[:, oc, :], a1_.t[:, :], a2_.t[:, :], ALU.add, [a1_, a2_], [ob], eng=os.environ.get("A1_ADD", "pool"))
                k.store(fm(dst, t0, n), ob, ob.t[:, :, :])
            if STG < 3:
                continue
            for ts in range(n // 128):
                v_ = vo[vi % 2]
                vi += 1
                for half in range(2):
                    p = pss[pi % 4]
                    pi += 1
                    for c in range(8):
                        k.mm(p.t[:, 0:512], hb.t[:, c, ts * 128:(ts + 1) * 128], w.t[:, c, 2 * D + half * 512:2 * D + (half + 1) * 512], [w, hb], [p],
                             start=(c == 0), stop=(c == 7), inc=(c == 7))
                    if half == 0:
                        k.act(v_.t[:, 0:4, 0:128], p.t[:, 0:512].rearrange("p (h e) -> p h e", e=128), AF.Copy, [p], [v_])
                    else:
                        k.copy(v_.t[:, 4:8, 0:128], p.t[:, 0:512].rearrange("p (h e) -> p h e", e=128), [p], [v_])
                k.store(VA[t0 + ts * 128:t0 + (ts + 1) * 128, :].rearrange("t (h e) -> t h e", e=130), v_, v_.t[:, :, :])
        k.end()

    def phase_a2(self, i):
        k = self.k
        j = i // 2
        last = i == DEPTH - 1
        lambda_init = 0.8 - 0.6 * math.exp(-0.3 * i)
        S = self.scratch
        QA = S("QA", [D, T], BF16)
        KA = S("KA", [D, T], BF16)
        VA = S("VA", [T, 8 * 130], BF16)
        NKT = T // 128
        k.begin()
        lv = k.sb([128, 4, 64], F32, "lv")
        k.load(lv, lv.t[:, :, :].rearrange("p a e -> p (a e)"), self.lam_d[j].rearrange("a e -> (a e)").partition_broadcast(128))
        lt = k.sb([128, 2, 64], F32, "lt")
        ls = k.sb([128, 4], F32, "ls")
        k.tt(lt.t[:, 0, :], lv.t[:, 0, :], lv.t[:, 1, :], ALU.mult, [lv], [lt])
        k.tt(lt.t[:, 1, :], lv.t[:, 2, :], lv.t[:, 3, :], ALU.mult, [lv], [lt])
        k.op("dve", lambda h: h.reduce_sum(out=ls.t[:, 0:2], in_=lt.t[:, :, :], axis=mybir.AxisListType.X), [lt], [ls])
        k.act(ls.t[:, 0:2], ls.t[:, 0:2], AF.Exp, [ls], [ls])
        k.tt(ls.t[:, 2:3], ls.t[:, 0:1], ls.t[:, 1:2], ALU.subtract, [ls], [ls])
        k.ts(ls.t[:, 3:4], ls.t[:, 2:3], float(lambda_init), None, ALU.add, None, [ls], [ls])
        lam = ls.t[:, 3:4]
        gsub = k.sb([128, 128], F32, "gsub")
        k.load(gsub, gsub.t[:, :], self.subg_d[j].partition_broadcast(128))
        k.ts(gsub.t[:, :], gsub.t[:, :], float(1.0 - lambda_init), None, ALU.mult, None, [gsub], [gsub])
        nq = 512
        Kh = [k.sb([128, T], BF16, "Kh") for _ in range(2)]
        Qh = [k.sb([128, T], BF16, "Qh") for _ in range(2)]
        Vh = [k.sb([128, NKT, 130], BF16, "Vh") for _ in range(2)]
        PT = [[k.sb([128, NKT, nq], BF16, "PT") for _ in range(2)] for _ in range(2)]
        psS = [[k.ps([128, 512], F32, "psS") for _ in range(2)] for _ in range(2)]
        psO = [k.ps([128, 512], F32, "psO") for _ in range(2)]
        ptr = k.ps([128, 4, 128], BF16, "ptr")
        rr = k.sb([128, 4], F32, "rr")
        tt_ = k.sb([128, 128], F32, "tt")
        oo = k.sb([128, 128], F32, "oo")
        junk = k.sb([128, 128], F32, "junk")
        on = k.sb([128, 128], BF16, "on")
        yT = [k.sb([128, nq], BF16, "yT") for _ in range(2)]
        qblocks = [] if last else [(0, TC, [0, 1])]
        qblocks += [(TC + b_ * nq, nq, list(range(NKT))) for b_ in range(TL // nq)]
        import os
        NHD = int(os.environ.get("A2_NH", "8"))
        st = {"si": 0, "yi": 0}

        def emit_s(item, pset):
            h, (q0, nqb, kts) = item
            kh, qh = Kh[h % 2], Qh[h % 2]
            for kt in kts:
                st["si"] += 1
                pp_ = [psS[s_][st["si"] % 2] for s_ in range(2)]
                for s_ in range(2):
                    k.mm(pp_[s_].t[:, 0:nqb], kh.t[64 * s_:64 * s_ + 64, kt * 128:(kt + 1) * 128], qh.t[64 * s_:64 * s_ + 64, q0:q0 + nqb], [kh, qh], [pp_[s_]])
                for s_ in range(2):
                    k.act(PT[pset][s_].t[:, kt, 0:nqb], pp_[s_].t[:, 0:nqb], AF.Exp, [pp_[s_]], [PT[pset][s_]], scale=0.125)

        def emit_pv(item, pset):
            h, (q0, nqb, kts) = item
            vh = Vh[h % 2]
            y = yT[st["yi"] % 2]
            st["yi"] += 1
            for qs in range(nqb // 128):
                for s_ in range(2):
                    for ki, kt in enumerate(kts):
                        k.mm(psO[s_].t[:, 0:130], PT[pset][s_].t[:, kt, qs * 128:(qs + 1) * 128], vh.t[:, kt, :], [PT[pset][s_], vh], [psO[s_]],
                             start=(ki == 0), stop=(ki == len(kts) - 1), inc=(ki == len(kts) - 1))
                k.op("dve", lambda h_: h_.reciprocal(out=rr.t[:, 0:1], in_=psO[0].t[:, 128:129]), [psO[0]], [rr])
                k.op("dve", lambda h_: h_.reciprocal(out=rr.t[:, 1:2], in_=psO[1].t[:, 128:129]), [psO[1]], [rr])
                k.tt(rr.t[:, 1:2], rr.t[:, 1:2], lam, ALU.mult, [rr, ls], [rr])
                k.ts(tt_.t[:, :], psO[1].t[:, 0:128], rr.t[:, 1:2], None, ALU.mult, None, [psO[1], rr], [tt_])
                k.stt(oo.t[:, :], psO[0].t[:, 0:128], rr.t[:, 0:1], tt_.t[:, :], ALU.mult, ALU.subtract, [psO[0], rr, tt_], [oo])
                k.tt(junk.t[:, :], oo.t[:, :], oo.t[:, :], ALU.mult, [oo], [junk], eng="pool")
                k.op("dve", lambda h_: h_.reduce_sum(out=rr.t[:, 2:3], in_=junk.t[:, :], axis=mybir.AxisListType.X), [junk], [rr])
                k.act(rr.t[:, 3:4], rr.t[:, 2:3], AF.Ln, [rr], [rr], bias=1e-5, scale=1.0 / 128)
                k.act(rr.t[:, 3:4], rr.t[:, 3:4], AF.Exp, [rr], [rr], scale=-0.5)
                k.stt(on.t[:, :], oo.t[:, :], rr.t[:, 3:4], gsub.t[:, :], ALU.mult, ALU.mult, [oo, rr, gsub], [on])
                k.tr(ptr.t[:, qs, :], on.t[:, :], self.identb.t[:, :], [on, self.identb], [ptr])
                k.copy(y.t[:, qs * 128:(qs + 1) * 128], ptr.t[:, qs, :], [ptr], [y])
            k.store(self.YT[h * 128:(h + 1) * 128, q0:q0 + nqb], y, y.t[:, 0:nqb])

        items = [(h, qb_) for h in range(NHD) for qb_ in qblocks]
        prev = None
        loaded = -1
        for it, item in enumerate(items):
            h = item[0]
            if h != loaded:
                kh, qh, vh = Kh[h % 2], Qh[h % 2], Vh[h % 2]
                k.load(kh, kh.t[:, :], KA[h * 128:(h + 1) * 128, :])
                k.load(qh, qh.t[:, :], QA[h * 128:(h + 1) * 128, :])
                k.load(vh, vh.t[:, :, :], VA.rearrange("(kt p) (h e) -> p kt h e", p=128, e=130)[:, :, h, :])
                loaded = h
            emit_s(item, it % 2)
            if prev is not None:
                emit_pv(prev, (it - 1) % 2)
            prev = item
        emit_pv(prev, (len(items) - 1) % 2)
        k.end()


def make_in_maps(inp, cores):
    inp = {k_: np.asarray(v) for k_, v in inp.items()}
    vecs = build_vecs(inp)
    consts = build_consts()
    lamv = np.ascontiguousarray(np.stack([inp["da_lq1"], inp["da_lk1"], inp["da_lq2"], inp["da_lk2"]], axis=1).astype(np.float32))
    subg = np.ascontiguousarray(inp["da_subln_g"].astype(np.float32))
    shared = {n: np.ascontiguousarray(inp[n], dtype=np.float32) for n in WNAMES}
    maps = []
    for b in cores:
        cc = np.concatenate([inp["c"][b].reshape(8, 128).T, inp["c_ctx"].reshape(8, 128).T], axis=1).astype(np.float32)
        m = {"x": np.ascontiguousarray(inp["x"][b]), "ctx": np.ascontiguousarray(inp["ctx"][b]), "cc": np.ascontiguousarray(cc),
             "vecs": vecs, "consts": consts, "lamv": lamv, "subg": subg}
        m.update(shared)
        maps.append(m)
    return maps


def run_debug(inp, upto, dumps, cores=(0,)):
    p = Prog(upto=upto, dumps=dumps)
    nc = p.build()
    res = run_bass_kernel_spmd(nc, make_in_maps(inp, list(cores)), core_ids=list(range(len(cores))))
    return res.results


def kernel(**inputs):
    p = Prog()
    nc = p.build()
    res = run_bass_kernel_spmd(nc, make_in_maps(inputs, list(range(8))), core_ids=list(range(8)))
    return np.stack([np.asarray(r["out"], dtype=np.float32) for r in res.results], axis=0)
```
